# Optimizing a Trainium2 kernel written in Bass

```python
import jax, jax.numpy as jnp
from jax import lax
import numpy as np

D_MODEL = 1024
BATCH = 32
SEQ = 256
DEPTH = 2
DEC_BATCH = 4
DEC_SEQ = 4096
PAST_LEN = 256

GRID_W = 64
NORM_EPS = 1e-6
NEG_BIG = -1e30
LB_EPS = 1e-6
A_HEADS = 4
A_DK = 128
A_DV = 128
A_CHUNK = 64
SHORT_CONV = 5
B_HEADS = 4
B_DK = 128
B_DV = 128
B_CHUNK = 32
C_QHEADS = 8
C_KVHEADS = 2
C_GROUP = C_QHEADS // C_KVHEADS
C_HD = 64
C_WINDOW = 128
C_BLOCK = 128
ROPE_THETA = 10000.0
N_BRANCH = 3
BRANCH_W = 512
D_FF = 2816
FFN_CONV = 3

A_QK_W = A_HEADS * A_DK
A_V_W = A_HEADS * A_DV
B_K_W = B_HEADS * B_DK
B_V_W = B_HEADS * B_DV
C_Q_W = C_QHEADS * C_HD
C_KV_W = C_KVHEADS * C_HD
SPLIT_SIZES = (A_QK_W, A_QK_W, A_V_W, A_V_W, 2 * A_HEADS, 2 * A_HEADS,
               B_K_W, B_V_W, 2 * B_K_W, B_V_W,
               C_Q_W, C_KV_W, C_KV_W,
               N_BRANCH * D_MODEL)
SPLIT_POINTS = tuple(np.cumsum(SPLIT_SIZES)[:-1].tolist())
D_IN = int(sum(SPLIT_SIZES))

kernel_name = 'hybrid_diffusion_trunk_step'


def _rms(x, w):
    xf = x.astype(jnp.float32)
    y = xf * lax.rsqrt(jnp.mean(jnp.square(xf), axis=-1, keepdims=True) + NORM_EPS)
    return (y * w.astype(jnp.float32)).astype(x.dtype)


def _l2(x):
    xf = x.astype(jnp.float32)
    return xf * lax.rsqrt(jnp.sum(jnp.square(xf), axis=-1, keepdims=True) + NORM_EPS)


def _dwconv(x, w):
    width = w.shape[0]
    return lax.conv_general_dilated(x, w[:, None, :].astype(x.dtype), window_strides=(1,),
                                    padding=[(width // 2, width // 2)],
                                    dimension_numbers=('NWC', 'WIO', 'NWC'),
                                    feature_group_count=x.shape[-1])


def _flip(a):
    return jnp.flip(a, axis=1)


def _to_chunks(a, chunk):
    b, t = a.shape[:2]
    a = a.reshape(b, t // chunk, chunk, *a.shape[2:])
    return jnp.swapaxes(jnp.moveaxis(a, 1, 0), 2, 3)


def _from_chunks(o):
    n, b, h, c, d = o.shape
    return jnp.moveaxis(jnp.swapaxes(o, 2, 3), 0, 1).reshape(b, n * c, h, d)


def _masked_exp(mask, diff):
    return jnp.where(mask, jnp.exp(jnp.where(mask, diff, 0.0)), 0.0)


def _gated_delta(q, k, v, beta, g, s0):
    c = A_CHUNK
    q, k, v = _to_chunks(q, c), _to_chunks(k, c), _to_chunks(v, c)
    beta, g = _to_chunks(beta, c), _to_chunks(g, c)
    gc = jnp.cumsum(g, axis=-1)
    idx = jnp.arange(c)
    incl = idx[:, None] >= idx[None, :]
    strict = idx[:, None] > idx[None, :]
    decay = _masked_exp(incl, gc[..., :, None] - gc[..., None, :])
    kb = k * beta[..., None]
    m = jnp.where(strict, jnp.einsum('nbhid,nbhjd->nbhij', kb, k) * decay, 0.0)
    eye = jnp.eye(c, dtype=m.dtype)
    t_inv = lax.linalg.triangular_solve(eye + m, jnp.broadcast_to(eye, m.shape),
                                        left_side=True, lower=True, unit_diagonal=True)
    u = t_inv @ (v * beta[..., None])
    w = t_inv @ (kb * jnp.exp(gc)[..., None])
    qk = jnp.einsum('nbhid,nbhjd->nbhij', q, k) * decay
    qg = q * jnp.exp(gc)[..., None]
    g_last = gc[..., -1]
    kd = k * jnp.exp(g_last[..., None] - gc)[..., None]

    def step(s, xs):
        qg_n, qk_n, u_n, w_n, kd_n, gl_n = xs
        v_new = u_n - jnp.einsum('bhck,bhkv->bhcv', w_n, s)
        o = jnp.einsum('bhck,bhkv->bhcv', qg_n, s) + jnp.einsum('bhcs,bhsv->bhcv', qk_n, v_new)
        s = s * jnp.exp(gl_n)[..., None, None] + jnp.einsum('bhck,bhcv->bhkv', kd_n, v_new)
        return s, o

    s, o = lax.scan(step, s0, (qg, qk, u, w, kd, g_last))
    return _from_chunks(o), s


def _hgrn2(q, k, v, lf, s0):
    c = B_CHUNK
    q, k, v, lf = _to_chunks(q, c), _to_chunks(k, c), _to_chunks(v, c), _to_chunks(lf, c)
    idx = jnp.arange(c)
    incl = (idx[:, None] >= idx[None, :])[:, :, None]

    def step(s, xs):
        q_n, k_n, v_n, lf_n = xs
        b = jnp.cumsum(lf_n, axis=2)
        rel = _masked_exp(incl, b[:, :, :, None, :] - b[:, :, None, :, :])
        att = jnp.einsum('bhtk,bhsk,bhtsk->bhts', q_n, k_n, rel)
        o = jnp.einsum('bhtk,bhkv->bhtv', q_n * jnp.exp(b), s) + jnp.einsum('bhts,bhsv->bhtv', att, v_n)
        b_last = b[:, :, -1]
        s = s * jnp.exp(b_last)[..., None] + jnp.einsum('bhsk,bhsv->bhkv', k_n * jnp.exp(b_last[:, :, None] - b), v_n)
        return s, o

    s, o = lax.scan(step, s0, (q, k, v, lf))
    return _from_chunks(o), s


def _axial_rope(x):
    t, dh = x.shape[1], x.shape[-1]
    rows = t // GRID_W
    row = jnp.repeat(jnp.arange(rows, dtype=jnp.float32), GRID_W)
    col = jnp.tile(jnp.arange(GRID_W, dtype=jnp.float32), rows)
    nf = dh // 4
    inv = ROPE_THETA ** (-jnp.arange(nf, dtype=jnp.float32) / nf)
    xf = x.astype(jnp.float32)

    def rot(xh, pos):
        ang = pos[:, None] * inv
        cos, sin = jnp.cos(ang)[None, :, None, :], jnp.sin(ang)[None, :, None, :]
        x1, x2 = xh[..., :nf], xh[..., nf:]
        return jnp.concatenate([x1 * cos - x2 * sin, x2 * cos + x1 * sin], axis=-1)

    out = jnp.concatenate([rot(xf[..., :dh // 2], row), rot(xf[..., dh // 2:], col)], axis=-1)
    return out.astype(x.dtype)


def _sink_probs(s, sink):
    sk = sink.astype(jnp.float32)[:, :, None, None]
    m = jnp.maximum(jnp.max(s, axis=-1, keepdims=True), sk)
    p = jnp.exp(s - m)
    return p / (jnp.sum(p, axis=-1, keepdims=True) + jnp.exp(sk - m))


def _ctx_attention(q, k, v, sink):
    b, t = q.shape[:2]
    qg = q.reshape(b, t, C_KVHEADS, C_GROUP, C_HD)
    s = jnp.einsum('bqhgd,bkhd->bhgqk', qg, k, preferred_element_type=jnp.float32) * (C_HD ** -0.5)
    p = _sink_probs(s, sink).astype(v.dtype)
    o = jnp.einsum('bhgqk,bkhd->bqhgd', p, v)
    return o.reshape(b, t, C_Q_W)


def _latent_attention(q, k, v, k_ctx, v_ctx, sink):
    b, t = q.shape[:2]
    nb = t // C_BLOCK
    qb = q.reshape(b, nb, C_BLOCK, C_KVHEADS, C_GROUP, C_HD)

    def neigh(a):
        a = a.reshape(b, nb, C_BLOCK, C_KVHEADS, C_HD)
        a = jnp.pad(a, ((0, 0), (1, 1), (0, 0), (0, 0), (0, 0)))
        return jnp.concatenate([a[:, :-2], a[:, 1:-1], a[:, 2:]], axis=2)

    kn, vn = neigh(k), neigh(v)
    scale = C_HD ** -0.5
    s_loc = jnp.einsum('bnqhgd,bnkhd->bnhgqk', qb, kn, preferred_element_type=jnp.float32) * scale
    blk = jnp.arange(nb)[:, None] * C_BLOCK
    qpos = blk + jnp.arange(C_BLOCK)[None, :]
    kpos = blk - C_BLOCK + jnp.arange(3 * C_BLOCK)[None, :]
    mask = ((jnp.abs(qpos[:, :, None] - kpos[:, None, :]) <= C_WINDOW)
            & (kpos >= 0)[:, None, :] & (kpos < t)[:, None, :])
    s_loc = jnp.where(mask[None, :, None, None], s_loc, NEG_BIG)
    s_ctx = jnp.einsum('bnqhgd,bphd->bnhgqp', qb, k_ctx.astype(q.dtype), preferred_element_type=jnp.float32) * scale
    p = _sink_probs(jnp.concatenate([s_loc, s_ctx], axis=-1), sink).astype(v.dtype)
    nk = 3 * C_BLOCK
    o = (jnp.einsum('bnhgqk,bnkhd->bnqhgd', p[..., :nk], vn)
         + jnp.einsum('bnhgqp,bphd->bnqhgd', p[..., nk:], v_ctx.astype(v.dtype)))
    return o.reshape(b, t, C_Q_W)


def _layer(x, cond, l, P, past):
    f32 = jnp.float32
    bx, t, _ = x.shape
    mod = jax.nn.silu(cond) @ P['ada_w'][l] + P['ada_b'][l]
    sh1, sc1, g1, sh2, sc2, g2 = jnp.split(mod[:, None, :], 6, axis=-1)
    h = _rms(x, P['norm1_w'][l]) * (1 + sc1) + sh1
    proj = h @ P['w_in'][l]
    (qa, ka, va, ga, beta_raw, alpha_raw, qb, ib, fb, gb,
     q_c, k_c, v_c, mg) = jnp.split(proj, SPLIT_POINTS, axis=-1)

    if past is None:
        sa0 = jnp.zeros((bx, 2, A_HEADS, A_DK, A_DV), f32)
        sb0 = jnp.zeros((bx, 2, B_HEADS, B_DK, B_DV), f32)
    else:
        sa0 = past[0].astype(f32)
        sb0 = past[1].astype(f32)

    qkv = jax.nn.silu(_dwconv(jnp.concatenate([qa, ka, va], axis=-1), P['conv_a'][l]))
    qa, ka, va = jnp.split(qkv, [A_QK_W, 2 * A_QK_W], axis=-1)
    q = _l2(qa.reshape(bx, t, A_HEADS, A_DK)) * (A_DK ** -0.5)
    k = _l2(ka.reshape(bx, t, A_HEADS, A_DK))
    v = va.reshape(bx, t, A_HEADS, A_DV).astype(f32)
    beta = jax.nn.sigmoid(beta_raw.astype(f32)).reshape(bx, t, 2, A_HEADS)
    g = -jnp.exp(P['a_log'][l].astype(f32)) * jax.nn.softplus(
        alpha_raw.astype(f32).reshape(bx, t, 2, A_HEADS) + P['dt_bias'][l].astype(f32))
    o_f, sa_f = _gated_delta(q, k, v, beta[:, :, 0], g[:, :, 0], sa0[:, 0])
    o_b, sa_b = _gated_delta(_flip(q), _flip(k), _flip(v), _flip(beta[:, :, 1]), _flip(g[:, :, 1]), sa0[:, 1])
    o_a = _rms(o_f + _flip(o_b), P['norm_a'][l]) * jax.nn.silu(ga.astype(f32).reshape(bx, t, A_HEADS, A_DV))
    o_a = o_a.reshape(bx, t, BRANCH_W).astype(x.dtype)
    state_a = jnp.stack([sa_f, sa_b], axis=1)

    z = fb.astype(f32).reshape(bx, t, 2, B_K_W)
    if l == 0:
        lf = jax.nn.log_sigmoid(z)
    else:
        lb_p = jax.nn.softmax(P['lb_logits'].astype(f32), axis=1)
        lb = jnp.clip(jnp.sum(lb_p[:, 1:l + 1], axis=1), LB_EPS, 1.0 - LB_EPS)
        lf = jnp.logaddexp(jnp.log(lb), jnp.log1p(-lb) + jax.nn.log_sigmoid(z))
    lf = lf.reshape(bx, t, 2, B_HEADS, B_DK)
    kf = -jnp.expm1(lf)
    qh = jax.nn.silu(qb.astype(f32)).reshape(bx, t, B_HEADS, B_DK)
    ih = ib.astype(f32).reshape(bx, t, B_HEADS, B_DV)
    o_f, sb_f = _hgrn2(qh, kf[:, :, 0], ih, lf[:, :, 0], sb0[:, 0])
    o_b, sb_b = _hgrn2(_flip(qh), _flip(kf[:, :, 1]), _flip(ih), _flip(lf[:, :, 1]), sb0[:, 1])
    o_b2 = _rms(o_f + _flip(o_b), P['norm_b'][l]) * jax.nn.silu(gb.astype(f32).reshape(bx, t, B_HEADS, B_DV))
    o_b2 = o_b2.reshape(bx, t, BRANCH_W).astype(x.dtype)
    state_b = jnp.stack([sb_f, sb_b], axis=1)

    qh_c = _rms(q_c.reshape(bx, t, C_QHEADS, C_HD), P['q_norm'][l])
    kh_c = _rms(k_c.reshape(bx, t, C_KVHEADS, C_HD), P['k_norm'][l])
    vh_c = v_c.reshape(bx, t, C_KVHEADS, C_HD)
    sink = P['sink'][l].reshape(C_KVHEADS, C_GROUP)
    if past is None:
        o_c = _ctx_attention(qh_c, kh_c, vh_c, sink)
    else:
        o_c = _latent_attention(_axial_rope(qh_c), _axial_rope(kh_c), vh_c, past[2], past[3], sink)

    gates = jax.nn.sigmoid(mg.astype(f32).reshape(bx, t, N_BRANCH, D_MODEL))
    br = jnp.einsum('btrw,rwd->btrd', jnp.stack([o_a, o_b2, o_c], axis=2), P['w_branch'][l])
    merged = jnp.sum(gates * br.astype(f32), axis=2).astype(x.dtype)
    x = x + g1 * (merged @ P['w_out'][l])

    h = _rms(x, P['norm2_w'][l]) * (1 + sc2) + sh2
    u = _dwconv(h @ P['w_up'][l], P['conv_ffn'][l])
    a, u = jnp.split(u, 2, axis=-1)
    x = x + g2 * ((jax.nn.silu(a) * u) @ P['w_down'][l])
    return x, (state_a, state_b, kh_c, vh_c)


def setup_inputs(seed: int = 0) -> dict:
    key = jax.random.key(seed)
    ks = jax.random.split(key, 32)
    f32 = jnp.float32

    def nrm(k, shape, s):
        return jax.random.normal(k, shape, f32) * s

    dt = jnp.exp(jax.random.uniform(ks[13], (DEPTH, 2, A_HEADS), f32, float(np.log(1e-3)), float(np.log(1e-1))))
    return {
        'x_prompt': nrm(ks[0], (BATCH, SEQ, D_MODEL), 1.0),
        'x_sample': nrm(ks[1], (DEC_BATCH, DEC_SEQ, D_MODEL), 1.0),
        'state_delta': nrm(ks[2], (DEC_BATCH, DEPTH, 2, A_HEADS, A_DK, A_DV), 0.1),
        'state_hgrn': nrm(ks[3], (DEC_BATCH, DEPTH, 2, B_HEADS, B_DK, B_DV), 0.5),
        'cache_k': nrm(ks[4], (DEC_BATCH, DEPTH, PAST_LEN, C_KVHEADS, C_HD), 1.0),
        'cache_v': nrm(ks[5], (DEC_BATCH, DEPTH, PAST_LEN, C_KVHEADS, C_HD), 1.0),
        'c': nrm(ks[6], (DEC_BATCH, D_MODEL), 1.0),
        'c_ctx': nrm(ks[7], (D_MODEL,), 1.0),
        'ada_w': nrm(ks[8], (DEPTH, D_MODEL, 6 * D_MODEL), 0.5 * D_MODEL ** -0.5),
        'ada_b': nrm(ks[9], (DEPTH, 6 * D_MODEL), 0.01),
        'norm1_w': 1.0 + nrm(ks[10], (DEPTH, D_MODEL), 0.02),
        'w_in': nrm(ks[11], (DEPTH, D_MODEL, D_IN), D_MODEL ** -0.5),
        'conv_a': nrm(ks[12], (DEPTH, SHORT_CONV, 2 * A_QK_W + A_V_W), SHORT_CONV ** -0.5),
        'a_log': jnp.log(jax.random.uniform(ks[14], (DEPTH, 2, A_HEADS), f32, 1.0, 16.0)),
        'dt_bias': jnp.log(jnp.expm1(dt)),
        'norm_a': 1.0 + nrm(ks[15], (DEPTH, A_DV), 0.02),
        'lb_logits': nrm(ks[16], (2, DEPTH, B_K_W), 1.0),
        'norm_b': 1.0 + nrm(ks[17], (DEPTH, B_DV), 0.02),
        'q_norm': 1.0 + nrm(ks[18], (DEPTH, C_HD), 0.02),
        'k_norm': 1.0 + nrm(ks[19], (DEPTH, C_HD), 0.02),
        'sink': nrm(ks[20], (DEPTH, C_QHEADS), 0.5),
        'w_branch': nrm(ks[21], (DEPTH, N_BRANCH, BRANCH_W, D_MODEL), BRANCH_W ** -0.5),
        'w_out': nrm(ks[22], (DEPTH, D_MODEL, D_MODEL), D_MODEL ** -0.5),
        'norm2_w': 1.0 + nrm(ks[23], (DEPTH, D_MODEL), 0.02),
        'w_up': nrm(ks[24], (DEPTH, D_MODEL, 2 * D_FF), D_MODEL ** -0.5),
        'conv_ffn': nrm(ks[25], (DEPTH, FFN_CONV, 2 * D_FF), FFN_CONV ** -0.5),
        'w_down': nrm(ks[26], (DEPTH, D_FF, D_MODEL), D_FF ** -0.5),
    }


def reference(x_prompt, x_sample, state_delta, state_hgrn, cache_k, cache_v, c, c_ctx,
              ada_w, ada_b, norm1_w, w_in, conv_a, a_log, dt_bias, norm_a, lb_logits, norm_b,
              q_norm, k_norm, sink, w_branch, w_out, norm2_w, w_up, conv_ffn, w_down):
    P = dict(ada_w=ada_w, ada_b=ada_b, norm1_w=norm1_w, w_in=w_in, conv_a=conv_a, a_log=a_log,
             dt_bias=dt_bias, norm_a=norm_a, lb_logits=lb_logits, norm_b=norm_b, q_norm=q_norm,
             k_norm=k_norm, sink=sink, w_branch=w_branch, w_out=w_out, norm2_w=norm2_w,
             w_up=w_up, conv_ffn=conv_ffn, w_down=w_down)

    y_prompt = x_prompt
    cond_ctx = c_ctx[None, :]
    st_a, st_b, ck, cv = [], [], [], []
    for l in range(DEPTH):
        y_prompt, (s_a, s_b, k_l, v_l) = _layer(y_prompt, cond_ctx, l, P, None)
        st_a.append(s_a)
        st_b.append(s_b)
        ck.append(k_l)
        cv.append(v_l)

    y_sample = x_sample
    for l in range(DEPTH):
        past = (state_delta[:, l], state_hgrn[:, l], cache_k[:, l], cache_v[:, l])
        y_sample, _ = _layer(y_sample, c, l, P, past)

    new_state_delta = jnp.stack(st_a, axis=1)
    new_state_hgrn = jnp.stack(st_b, axis=1)
    new_cache_k = jnp.stack(ck, axis=1)
    new_cache_v = jnp.stack(cv, axis=1)
    return (y_prompt, y_sample, new_state_delta, new_state_hgrn, new_cache_k, new_cache_v)
```

```python
from contextlib import ExitStack
import numpy as np
import concourse.bass as bass
import concourse.mybir as mybir
from concourse.bass_utils import run_bass_kernel_spmd

F32 = mybir.dt.float32
BF16 = mybir.dt.bfloat16
AF = mybir.ActivationFunctionType
ALU = mybir.AluOpType
AX = mybir.AxisListType


_UID = [0]


def _un(name):
    _UID[0] += 1
    return f"{name}_u{_UID[0]}"


class KB:
    RING = 12

    def __init__(self, nc):
        self.nc = nc
        self.es = ExitStack()
        self.eng = {"pe": nc.tensor, "act": nc.scalar, "dve": nc.vector, "pool": nc.gpsimd, "sp": nc.sync}
        self.sem = {}
        for e in ("pe", "act", "dve", "pool"):
            self.sem[e] = self.es.enter_context(nc.semaphore("s_" + e))
        self.cnt = {e: 0 for e in self.sem}
        self.ring = {}
        self.dcnt = {}
        for q in ("sp", "pool", "act"):
            self.ring[q] = [self.es.enter_context(nc.semaphore(f"d_{q}{i}")) for i in range(self.RING)]
            self.dcnt[q] = 0
        self.seen = {e: {} for e in self.eng}
        self.seenseq = {e: {} for e in self.eng}
        self.seq = {e: 0 for e in self.sem}
        self.lastins = {}
        self.sigmap = {}
        self.last_w = {}
        self.readers = {}
        self.n_inst = 0

    def sb(self, name, shape, dt):
        return self.es.enter_context(self.nc.sbuf_tensor(name, list(shape), dt))

    def ps(self, name, shape, dt=F32):
        return self.es.enter_context(self.nc.psum_tensor(name, list(shape), dt))

    def dram(self, name, shape, dt, kind="Internal"):
        return self.nc.dram_tensor(name, list(shape), dt, kind=kind)

    def _signal_upto(self, e2, seq):
        sm = self.sigmap.setdefault(e2, [])
        if not sm or sm[-1][0] < seq:
            ins, lseq = self.lastins[e2]
            assert lseq >= seq
            self.cnt[e2] += 1
            ins.then_inc(self.sem[e2], 1)
            sm.append((lseq, self.cnt[e2]))
        lo, hi = 0, len(sm) - 1
        while lo < hi:
            mid = (lo + hi) // 2
            if sm[mid][0] >= seq:
                hi = mid
            else:
                lo = mid + 1
        return sm[lo][1]

    def _wait(self, e, tok):
        kind = tok[0]
        if kind == "c":
            _, e2, seq = tok
            if e2 == e and e == "pe":
                return
            key = ("c", e2)
            if self.seenseq[e].get(key, 0) >= seq:
                return
            val = self._signal_upto(e2, seq)
            self.seenseq[e][key] = seq
            if self.seen[e].get(key, 0) >= val:
                return
            self.eng[e].wait_ge(self.sem[e2], val)
            self.seen[e][key] = val
        else:
            _, q, slot, val = tok
            key = ("d", q, slot)
            if self.seen[e].get(key, 0) >= val:
                return
            self.eng[e].wait_ge(self.ring[q][slot], val)
            self.seen[e][key] = val

    def _deps(self, e, reads, writes):
        toks = []
        for k in reads:
            if k in self.last_w:
                toks.append(self.last_w[k])
        for k in writes:
            if k in self.last_w:
                toks.append(self.last_w[k])
            toks.extend(self.readers.get(k, ()))
        for t in toks:
            self._wait(e, t)

    def _commit(self, tok, reads, writes):
        for k in reads:
            self.readers.setdefault(k, []).append(tok)
            if len(self.readers[k]) > 64:
                best = {}
                for t in self.readers[k]:
                    kk = t[:2] if t[0] == "c" else t[:3]
                    if kk not in best or t[-1] > best[kk][-1]:
                        best[kk] = t
                self.readers[k] = list(best.values())
        for k in writes:
            self.last_w[k] = tok
            self.readers[k] = []

    def op(self, e, fn, reads=(), writes=()):
        self._deps(e, reads, writes)
        ins = fn(self.eng[e])
        self.seq[e] += 1
        self.lastins[e] = (ins, self.seq[e])
        self._commit(("c", e, self.seq[e]), reads, writes)
        self.n_inst += 1
        return ins

    def dma(self, q, out, in_, reads=(), writes=(), **kw):
        i = self.dcnt[q]
        slot = i % self.RING
        rnd = i // self.RING
        if rnd > 0:
            self._wait(q, ("d", q, slot, 16 * rnd))
        self._deps(q, reads, writes)
        ins = self.eng[q].dma_start(out=out, in_=in_, **kw)
        ins.then_inc(self.ring[q][slot], 16)
        self.dcnt[q] += 1
        self._commit(("d", q, slot, 16 * (rnd + 1)), reads, writes)
        self.n_inst += 1
        return ins

    def finish(self, out_keys):
        for k in out_keys:
            if k in self.last_w:
                self._wait("sp", self.last_w[k])
        for e in ("pe", "act", "dve", "pool"):
            if self.seq[e]:
                self._wait("sp", ("c", e, self.seq[e]))
        for q in self.ring:
            n = self.dcnt[q]
            for slot in range(min(n, self.RING)):
                last_i = ((n - 1 - slot) // self.RING) * self.RING + slot
                self._wait("sp", ("d", q, slot, 16 * (last_i // self.RING + 1)))

    def barrier(self):
        toks = [("c", e, self.seq[e]) for e in self.seq if self.seq[e]]
        for q in self.ring:
            n = self.dcnt[q]
            for slot in range(min(n, self.RING)):
                last_i = ((n - 1 - slot) // self.RING) * self.RING + slot
                toks.append(("d", q, slot, 16 * (last_i // self.RING + 1)))
        for e in self.eng:
            for t in toks:
                self._wait(e, t)
        self.last_w = {}
        self.readers = {}


D = 1024
NLAT = 4096
NCTX = 1024
NT = NLAT + NCTX
DEPTH = 2
DIN = 8464
DFF = 2816
EPS = 1e-6
SEQS = [(0, 4096, 0)] + [(4096 + 256 * i, 256, 1) for i in range(4)]
TILES = [(512 * j, 512, 0, 0) for j in range(8)] + [(4096 + 256 * i, 256, 1, 1 + i) for i in range(4)]
PADA = 2
NTP = NT + 2 * PADA * len(SEQS)
BIG = 30000.0
C_QA, C_KA, C_VA, C_GA, C_BETA, C_ALPHA = 0, 512, 1024, 1536, 2048, 2056
C_QB, C_IB, C_FB, C_GB = 2064, 2576, 3088, 4112
C_QC, C_KC, C_VC, C_MG = 4624, 5136, 5264, 5392


def padcol(seq_idx, t):
    return t + PADA * (2 * seq_idx + 1)


def host_consts():
    c = {}
    i = np.arange(128)
    c["ident"] = np.eye(128, dtype=np.float32)
    c["ones"] = np.ones((128, 128), np.float32)
    triF = (i[:, None] <= i[None, :]).astype(np.float32)
    triB = (i[:, None] >= i[None, :]).astype(np.float32)
    c["tri0"], c["tri1"] = triF, triB
    c["neg0"] = np.where(i[None, :] < i[:, None], 0.0, BIG).astype(np.float32)
    c["neg1"] = np.where(i[None, :] > i[:, None], 0.0, BIG).astype(np.float32)
    blk = (i[:, None] // 32) == (i[None, :] // 32)
    c["hm0"] = (blk & (i[None, :] >= i[:, None])).astype(np.float32)
    c["hm1"] = (blk & (i[None, :] <= i[:, None])).astype(np.float32)
    c["wprev"] = (i[:, None] >= i[None, :]).astype(np.float32)
    c["wnext"] = (i[:, None] <= i[None, :]).astype(np.float32)
    def same(b_):
        return (i[:, None] // b_) == (i[None, :] // b_)
    c["b16"] = same(16).astype(np.float32)
    c["l1"] = (same(32) & ~same(16)).astype(np.float32)
    c["l2"] = (same(64) & ~same(32)).astype(np.float32)
    c["l3"] = (~same(64)).astype(np.float32)
    seg = np.ones((128, 512), np.float32)
    seg[:, ::32] = 0.0
    c["seg"] = seg
    names = list(c)
    arr = np.concatenate([c[n] for n in names], axis=1)
    offs = {}
    o = 0
    for n in names:
        offs[n] = (o, c[n].shape[1])
        o += c[n].shape[1]
    return arr, offs


def rope_tables():
    pos = np.arange(NLAT)
    row = (pos // 64).astype(np.float32)
    col = (pos % 64).astype(np.float32)
    nf = 16
    inv = (10000.0 ** (-np.arange(nf, dtype=np.float32) / nf)).astype(np.float32)
    ar = row[:, None] * inv
    ac = col[:, None] * inv
    cos = np.concatenate([np.cos(ar), np.cos(ac)], axis=1).astype(np.float32)
    sin = np.concatenate([np.sin(ar), np.sin(ac)], axis=1).astype(np.float32)
    return cos, sin


def build_program(stage=99, debug=()):
    nc = bass.Bass("TRN2", target_bir_lowering=False)
    k = KB(nc)
    carr, coff = host_consts()

    def din(name, shape, dt=F32):
        return nc.dram_tensor(name, list(shape), dt, kind="ExternalInput").ap()

    def dout(name, shape, dt=F32):
        return nc.dram_tensor(name, list(shape), dt, kind="ExternalOutput").ap()

    def dscr(name, shape, dt=F32):
        return nc.dram_tensor(name, list(shape), dt, kind="ExternalOutput" if debug else "Internal").ap()

    I = {}
    I["x_lat"] = din("x_lat", [NLAT, D]); I["x_ctx"] = din("x_ctx", [NCTX, D])
    I["cond2"] = din("cond2", [2, D])
    I["sd0"] = din("sd0", [DEPTH, 2, 4, 128, 128]); I["sh0"] = din("sh0", [DEPTH, 2, 4, 128, 128])
    I["ck"] = din("ck", [DEPTH, 256, 128]); I["cv"] = din("cv", [DEPTH, 256, 128])
    I["ada_w"] = din("ada_w", [DEPTH, D, 6 * D]); I["ada_b"] = din("ada_b", [DEPTH, 6 * D])
    I["norm1_w"] = din("norm1_w", [DEPTH, D]); I["w_in"] = din("w_in", [DEPTH, D, DIN])
    I["conv_a"] = din("conv_a", [DEPTH, 5, 1536]); I["a_log"] = din("a_log", [DEPTH, 8]); I["dt_bias"] = din("dt_bias", [DEPTH, 8])
    I["norm_a"] = din("norm_a", [DEPTH, 128]); I["lb_logits"] = din("lb_logits", [2, DEPTH, 512]); I["norm_b"] = din("norm_b", [DEPTH, 128])
    I["q_norm"] = din("q_norm", [DEPTH, 64]); I["k_norm"] = din("k_norm", [DEPTH, 64]); I["sink"] = din("sink", [DEPTH, 8])
    I["w_branch"] = din("w_branch", [DEPTH, 3, 512, D]); I["w_out"] = din("w_out", [DEPTH, D, D]); I["norm2_w"] = din("norm2_w", [DEPTH, D])
    I["w_up"] = din("w_up", [DEPTH, D, 2 * DFF]); I["conv_ffn"] = din("conv_ffn", [DEPTH, 3, 2 * DFF]); I["w_down"] = din("w_down", [DEPTH, DFF, D])
    I["consts"] = din("consts", list(carr.shape)); I["rcos"] = din("rcos", [NLAT, 32]); I["rsin"] = din("rsin", [NLAT, 32])
    O = {}
    O["y_lat"] = dout("y_lat", [NLAT, D]); O["y_ctx"] = dout("y_ctx", [NCTX, D])
    O["nsd"] = dout("nsd", [4, DEPTH, 2, 4, 128, 128]); O["nsh"] = dout("nsh", [4, DEPTH, 2, 4, 128, 128])
    O["nck"] = dout("nck", [4, DEPTH, 256, 128]); O["ncv"] = dout("ncv", [4, DEPTH, 256, 128])
    XT = dscr("XT", [D, NT + 2])
    QKVA = dscr("QKVA", [1536, NTP])
    QKVN = dscr("QKVN", [1536, NT], BF16)
    GA = dscr("GA", [512, NT], BF16); GB = dscr("GB", [512, NT], BF16)
    QB = dscr("QB", [512, NT], BF16); IB = dscr("IB", [512, NT], BF16)
    FFfull = dscr("FF", [1024, NT + 2])
    FF = FFfull[:, 0:NT]
    BG = dscr("BG", [16, NT])
    QCN = dscr("QCN", [64, 10, NT], BF16)
    VCN = dscr("VCN", [NT, 128], BF16)
    OA = QKVA[0:1024, 0:NT].rearrange("(d r) t -> d r t", d=2)
    OB = dscr("OB", [2, 512, NT])
    OC = dscr("OC", [64, 8, NT], BF16)
    DBG = {n: dout("dbg_" + n, s) for n, s in debug}

    XTv = XT.rearrange("(c p) t -> p c t", p=128)

    cst = k.sb("cst", [128, carr.shape[1]], F32)
    k.dma("sp", cst[:], I["consts"], writes=["cst"])

    def C(n):
        o, w = coff[n]
        return cst[:, o:o + w]
    identb = k.sb("identb", [128, 128], BF16); onesb = k.sb("onesb", [128, 128], BF16)
    k.op("dve", lambda e: e.tensor_copy(out=identb[:], in_=C("ident")), reads=["cst"], writes=["identb"])
    k.op("dve", lambda e: e.tensor_copy(out=onesb[:], in_=C("ones")), reads=["cst"], writes=["onesb"])
    PS = [k.ps(f"ps{i}", [128, 512]) for i in range(8)]
    psi = [0]

    def nps():
        i = psi[0] % 8
        psi[0] += 1
        return PS[i], f"ps{i}"

    evi = [0]

    def ev():
        evi[0] += 1
        return "dve" if evi[0] % 2 else "act"

    def copy(e, out, in_, r, w):
        if e == "act":
            k.op("act", lambda g: g.copy(out=out, in_=in_), reads=r, writes=w)
        else:
            k.op(e, lambda g: g.tensor_copy(out=out, in_=in_), reads=r, writes=w)

    def mm(ps_ap, lhsT, rhs, start, stop, r, w):
        k.op("pe", lambda e: e.matmul(ps_ap, lhsT=lhsT, rhs=rhs, start=start, stop=stop), reads=r, writes=w)

    def act(out, in_, func, r, w, bias=0.0, scale=1.0):
        k.op("act", lambda e: e.activation(out=out, in_=in_, func=func, bias=bias, scale=scale), reads=r, writes=w)

    def tt(e, out, in0, in1, op, r, w):
        k.op(e, lambda g: g.tensor_tensor(out=out, in0=in0, in1=in1, op=op), reads=r, writes=w)

    def ts(e, out, in0, s1, s2, op0, op1, r, w):
        if s2 is None:
            k.op(e, lambda g: g.tensor_scalar(out=out, in0=in0, scalar1=s1, scalar2=None, op0=op0), reads=r, writes=w)
        else:
            k.op(e, lambda g: g.tensor_scalar(out=out, in0=in0, scalar1=s1, scalar2=s2, op0=op0, op1=op1), reads=r, writes=w)

    def stt(e, out, in0, scalar, in1, op0, op1, r, w):
        k.op(e, lambda g: g.scalar_tensor_tensor(out=out, in0=in0, scalar=scalar, in1=in1, op0=op0, op1=op1), reads=r, writes=w)

    modT = k.sb("modT", [128, DEPTH, 2, 48], F32)
    gm1 = k.sb("gm1", [128, DEPTH, 2, 8], F32); gm2 = k.sb("gm2", [128, DEPTH, 2, 8], F32)
    with ExitStack() as ph:
        def sbp(name, shape, dt):
            return ph.enter_context(nc.sbuf_tensor(_un(name), list(shape), dt))
        cT = sbp("cT", [128, 8, 2], F32)
        for c in range(2):
            k.dma("sp", cT[:, :, c], I["cond2"][c].rearrange("(kc p) -> p kc", p=128), writes=["cT"], allow_slow_non_contiguous=True)
        act(cT[:], cT[:], AF.Silu, ["cT"], ["cT"])
        adab = sbp("adab", [128, DEPTH, 48], F32)
        nw = sbp("nw", [128, 2, DEPTH, 8], F32)
        for l in range(DEPTH):
            k.dma("sp", adab[:, l, :], I["ada_b"][l].rearrange("(c p) -> p c", p=128), writes=["adab"], allow_slow_non_contiguous=True)
            k.dma("sp", nw[:, 0, l, :], I["norm1_w"][l].rearrange("(c p) -> p c", p=128), writes=["nw"], allow_slow_non_contiguous=True)
            k.dma("sp", nw[:, 1, l, :], I["norm2_w"][l].rearrange("(c p) -> p c", p=128), writes=["nw"], allow_slow_non_contiguous=True)
        awb = [sbp(f"aw{i}", [128, 8, 768], F32) for i in range(2)]
        for l in range(DEPTH):
            for g in range(8):
                aw = awb[g % 2]; ak = f"aw{g % 2}"
                k.dma("sp" if g % 2 else "act", aw[:], I["ada_w"][l][:, g * 768:(g + 1) * 768].rearrange("(kc p) n -> p kc n", p=128), writes=[ak])
                pst, pk = nps()
                for j in range(6):
                    for kc in range(8):
                        mm(pst[:, 2 * j:2 * j + 2], aw[:, kc, j * 128:(j + 1) * 128], cT[:, kc, :], kc == 0, kc == 7, [ak, "cT"], [pk])
                for c in range(2):
                    tt("dve", modT[:, l, c, g * 6:(g + 1) * 6], pst[:, c:12:2], adab[:, l, g * 6:(g + 1) * 6], ALU.add, [pk, "adab"], ["modT"])
            for c in range(2):
                stt("dve", gm1[:, l, c, :], modT[:, l, c, 8:16], 1.0, nw[:, 0, l, :], ALU.add, ALU.mult, ["modT", "nw"], ["gm1"])
                stt("dve", gm2[:, l, c, :], modT[:, l, c, 32:40], 1.0, nw[:, 1, l, :], ALU.add, ALU.mult, ["modT", "nw"], ["gm2"])
        k.barrier()

    with ExitStack() as ph:
        def sbp(name, shape, dt):
            return ph.enter_context(nc.sbuf_tensor(_un(name), list(shape), dt))
        xin = [sbp(f"xin{i}", [128, D], F32) for i in range(2)]
        xo = [sbp(f"xo{i}", [128, 8, 128], F32) for i in range(2)]
        for j in range(NT // 128):
            t0 = j * 128
            src = I["x_lat"][t0:t0 + 128, :] if t0 < NLAT else I["x_ctx"][t0 - NLAT:t0 - NLAT + 128, :]
            b = j % 2
            k.dma("sp", xin[b][:], src, writes=[f"xin{b}"])
            for hf in range(2):
                pst, pk = nps()
                for q in range(4):
                    kc = hf * 4 + q
                    k.op("pe", lambda e: e.transpose(pst[:, q * 128:(q + 1) * 128], xin[b][:, kc * 128:(kc + 1) * 128], C("ident")),
                         reads=[f"xin{b}", "cst"], writes=[pk])
                copy(ev(), xo[b][:, hf * 4:hf * 4 + 4, :], pst[:].rearrange("p (c t) -> p c t", c=4), [pk], [f"xo{b}"])
            k.dma("act", XTv[:, :, 1 + t0:1 + t0 + 128], xo[b][:], reads=[f"xo{b}"], writes=["XT"])
        k.barrier()
    ctx = dict(nc=nc, k=k, I=I, O=O, C=C, nps=nps, ev=ev, copy=copy, mm=mm, act=act, tt=tt, ts=ts, stt=stt,
               identb=identb, onesb=onesb, modT=modT, gm1=gm1, gm2=gm2, XT=XT, XTv=XTv, QKVA=QKVA, QKVN=QKVN, GA=GA, GB=GB,
               QB=QB, IB=IB, FF=FF, FFfull=FFfull, BG=BG, QCN=QCN, VCN=VCN, OA=OA, OB=OB, OC=OC, DBG=DBG)
    for l in range(DEPTH):
        if stage >= 1:
            phase_proj(ctx, l)
        if stage >= 2:
            phase_aprep(ctx, l)
            for d in range(2):
                phase_delta(ctx, l, d)
        if stage >= 3:
            for d in range(2):
                phase_hgrn(ctx, l, d)
        if stage >= 4:
            phase_attn(ctx, l)
        if stage >= 5:
            phase_merge(ctx, l)
        if stage >= 6:
            phase_ffn(ctx, l)
        if stage < 99:
            break
    phase_out(ctx, stage)
    k.finish([])
    build_program.last_k = k
    return nc


def _norm_h(X, l, which, xT, hT, n, cond, tag):
    k = X["k"]; mm = X["mm"]; act = X["act"]; tt = X["tt"]; ts = X["ts"]
    gm = X["gm1"] if which == 1 else X["gm2"]
    shc = 0 if which == 1 else 24
    act(hT[:, :, :n], xT[:, :, :n], AF.Square, [tag + "xT"], [tag + "hT"])
    pst, pk = X["nps"]()
    for kc in range(8):
        mm(pst[:, :n], X["onesb"][:], hT[:, kc, :n], kc == 0, kc == 7, ["onesb", tag + "hT"], [pk])
    rstd = X[tag + "rstd"]
    act(rstd[:, :n], pst[:, :n], AF.Sqrt, [pk], [tag + "rstd"], bias=EPS, scale=1.0 / D)
    k.op("dve", lambda e: e.reciprocal(out=rstd[:, :n], in_=rstd[:, :n]), reads=[tag + "rstd"], writes=[tag + "rstd"])
    tt("dve", xT[:, :, :n], xT[:, :, :n], rstd[:, :n].unsqueeze(1).to_broadcast([128, 8, n]), ALU.mult, [tag + "xT", tag + "rstd"], [tag + "xT"])
    for kc in range(8):
        ts("pool" if kc % 2 else "dve", hT[:, kc, :n], xT[:, kc, :n], gm[:, l, cond, kc:kc + 1], X["modT"][:, l, cond, shc + kc:shc + kc + 1],
           ALU.mult, ALU.add, [tag + "xT", "gm1", "gm2", "modT"], [tag + "hT"])


def phase_proj(X, l):
    nc = X["nc"]; k = X["k"]; I = X["I"]; O = X["O"]; C = X["C"]
    mm = X["mm"]; act = X["act"]; tt = X["tt"]; ts = X["ts"]; stt = X["stt"]; copy = X["copy"]; nps = X["nps"]; ev = X["ev"]
    with ExitStack() as ph:
        def sbp(name, shape, dt):
            return ph.enter_context(nc.sbuf_tensor(_un(name), list(shape), dt))
        W = sbp("p1w", [128, 8, C_MG], BF16)
        for kc in range(8):
            k.dma("pool", W[:, kc, :], I["w_in"][l][kc * 128:(kc + 1) * 128, 0:C_MG], writes=["p1w"])
        xT = sbp("p1xT", [128, 8, 512], F32); hT = sbp("p1hT", [128, 8, 512], BF16); rstd = sbp("p1rstd", [128, 512], F32)
        X["p1rstd"] = rstd
        sqkv = sbp("sqkv", [128, 12, 512], F32); sff = sbp("sff", [128, 8, 512], F32)
        sga = sbp("sga", [128, 4, 512], BF16); sgb = sbp("sgb", [128, 4, 512], BF16)
        sqb = sbp("sqb", [128, 4, 512], BF16); sib = sbp("sib", [128, 4, 512], BF16)
        sbg = sbp("sbg", [16, 512], F32); tb1 = sbp("tb1", [16, 512], F32); tb2 = sbp("tb2", [16, 512], F32)
        bcs = sbp("bcs", [16, 4], F32)
        k.op("dve", lambda e: e.memset(bcs[:], 0.0), writes=["bcs"])
        k.op("dve", lambda e: e.memset(bcs[0:8, 1:2], 1.0), writes=["bcs"])
        k.op("dve", lambda e: e.memset(bcs[0:8, 2:3], -1.0), writes=["bcs"])
        alg = sbp("alg", [16, 1], F32)
        k.op("dve", lambda e: e.memset(alg[:], 0.0), writes=["alg"])
        k.dma("sp", bcs[8:16, 0:1], I["dt_bias"][l].rearrange("(p o) -> p o", o=1), reads=[], writes=["bcs"], allow_slow_non_contiguous=True)
        k.dma("sp", alg[8:16, 0:1], I["a_log"][l].rearrange("(p o) -> p o", o=1), writes=["alg"], allow_slow_non_contiguous=True)
        act(alg[:], alg[:], AF.Exp, ["alg"], ["alg"])
        stt("dve", bcs[:, 3:4], bcs[:, 1:2], -1.0, alg[:], ALU.add, ALU.mult, ["bcs", "alg"], ["bcs"])
        lbt = sbp("lbt", [128, 8], F32); oml = sbp("oml", [128, 8], F32)
        if l == 0:
            k.op("dve", lambda e: e.memset(lbt[:], 0.0), writes=["lbt"])
        else:
            l0 = sbp("l0", [128, 8], F32); l1 = sbp("l1", [128, 8], F32)
            for d in range(2):
                k.dma("sp", l0[:, d * 4:d * 4 + 4], I["lb_logits"][d, 0].rearrange("(h p) -> p h", p=128), writes=["l0"], allow_slow_non_contiguous=True)
                k.dma("sp", l1[:, d * 4:d * 4 + 4], I["lb_logits"][d, 1].rearrange("(h p) -> p h", p=128), writes=["l1"], allow_slow_non_contiguous=True)
            tt("dve", l1[:], l1[:], l0[:], ALU.subtract, ["l0", "l1"], ["l1"])
            act(lbt[:], l1[:], AF.Sigmoid, ["l1"], ["lbt"])
            ts("dve", lbt[:], lbt[:], 1e-6, 1.0 - 1e-6, ALU.max, ALU.min, ["lbt"], ["lbt"])
        ts("dve", oml[:], lbt[:], -1.0, 1.0, ALU.mult, ALU.add, ["lbt"], ["oml"])
        nqk = sbp("nqk", [128, 10, 64], F32)
        for hh in range(10):
            src = I["q_norm"][l:l + 1, :] if hh < 8 else I["k_norm"][l:l + 1, :]
            k.dma("sp", nqk[:, hh, :], src.to_broadcast([128, 64]), writes=["nqk"])
        qk = sbp("qk", [128, 10, 64], F32); qk2 = sbp("qk2", [128, 10, 64], F32); qss = sbp("qss", [128, 10], F32)
        qr = sbp("qr", [128, 10, 64], F32); rtmp = sbp("rtmp", [128, 10, 2, 16], F32)
        v32 = sbp("v32", [128, 128], F32); rc = sbp("rc", [128, 32], F32); rs_ = sbp("rs_", [128, 32], F32)
        qkT = sbp("qkT", [64, 10, 128], BF16)

        for (t0, n, cond, si) in TILES:
            k.dma("sp", xT[:, :, :n], X["XTv"][:, :, 1 + t0:1 + t0 + n], reads=["XT"], writes=["p1xT"])
            _norm_h(X, l, 1, xT, hT, n, cond, "p1")
            pc0 = padcol(si, t0)

            def proj(c0, m):
                pst, pk = nps()
                for kc in range(8):
                    mm(pst[:m, :n], W[:, kc, c0:c0 + m], hT[:, kc, :n], kc == 0, kc == 7, ["p1w", "p1hT"], [pk])
                return pst, pk
            for c in range(12):
                pst, pk = proj(C_QA + c * 128, 128)
                copy(ev(), sqkv[:, c, :n], pst[:, :n], [pk], ["sqkv"])
            k.dma("sp", X["QKVA"].rearrange("(c p) t -> p c t", p=128)[:, :, pc0:pc0 + n], sqkv[:, :, :n], reads=["sqkv"], writes=["QKVA"])
            for (c0, st, dst, key) in ((C_GA, sga, X["GA"], "GA"), (C_QB, sqb, X["QB"], "QB"), (C_GB, sgb, X["GB"], "GB")):
                for c in range(4):
                    pst, pk = proj(c0 + c * 128, 128)
                    act(st[:, c, :n], pst[:, :n], AF.Silu, [pk], ["s" + key])
                k.dma("act", dst.rearrange("(c p) t -> p c t", p=128)[:, :, t0:t0 + n], st[:, :, :n], reads=["s" + key], writes=[key])
            for c in range(4):
                pst, pk = proj(C_IB + c * 128, 128)
                copy(ev(), sib[:, c, :n], pst[:, :n], [pk], ["sIB"])
            k.dma("act", X["IB"].rearrange("(c p) t -> p c t", p=128)[:, :, t0:t0 + n], sib[:, :, :n], reads=["sIB"], writes=["IB"])
            for c in range(8):
                pst, pk = proj(C_FB + c * 128, 128)
                act(sff[:, c, :n], pst[:, :n], AF.Sigmoid, [pk], ["sff"])
                ts("pool", sff[:, c, :n], sff[:, c, :n], oml[:, c:c + 1], lbt[:, c:c + 1], ALU.mult, ALU.add, ["sff", "oml", "lbt"], ["sff"])
            k.dma("sp", X["FF"].rearrange("(c p) t -> p c t", p=128)[:, :, t0:t0 + n], sff[:, :, :n], reads=["sff"], writes=["FF"])
            pst, pk = proj(C_BETA, 16)
            act(tb1[:, :n], pst[:16, :n], AF.Exp, [pk, "bcs"], ["tb1"], bias=bcs[:, 0:1], scale=1.0)
            ts("dve", tb1[:, :n], tb1[:, :n], 1.0, None, ALU.add, None, ["tb1"], ["tb1"])
            act(tb2[:, :n], tb1[:, :n], AF.Ln, ["tb1"], ["tb2"])
            k.op("dve", lambda e: e.reciprocal(out=tb1[:, :n], in_=tb1[:, :n]), reads=["tb1"], writes=["tb1"])
            ts("dve", tb1[:, :n], tb1[:, :n], bcs[:, 2:3], bcs[:, 1:2], ALU.mult, ALU.add, ["tb1", "bcs"], ["tb1"])
            stt("dve", sbg[:, :n], tb2[:, :n], bcs[:, 3:4], tb1[:, :n], ALU.mult, ALU.add, ["tb1", "tb2", "bcs"], ["sbg"])
            k.dma("sp", X["BG"][:, t0:t0 + n], sbg[:, :n], reads=["sbg"], writes=["BG"])
            for j in range(n // 128):
                ta = t0 + j * 128
                p1, k1 = nps(); p2, k2 = nps()
                for kc in range(8):
                    mm(p1[:, :], hT[:, kc, j * 128:(j + 1) * 128], W[:, kc, C_QC:C_QC + 512], kc == 0, kc == 7, ["p1w", "p1hT"], [k1])
                for kc in range(8):
                    mm(p2[:, :256], hT[:, kc, j * 128:(j + 1) * 128], W[:, kc, C_KC:C_KC + 256], kc == 0, kc == 7, ["p1w", "p1hT"], [k2])
                copy("dve", qk[:, 0:8, :], p1[:].rearrange("p (h d) -> p h d", h=8), [k1], ["qk"])
                copy("act", qk[:, 8:10, :], p2[:, 0:128].rearrange("p (h d) -> p h d", h=2), [k2], ["qk"])
                copy("act", v32[:], p2[:, 128:256], [k2], ["v32"])
                k.dma("pool", X["VCN"][ta:ta + 128, :], v32[:], reads=["v32"], writes=["VCN"])
                if cond == 1:
                    k.dma("act", O["ncv"][si - 1, l, ta - t0:ta - t0 + 128, :], v32[:], reads=["v32"], writes=["ncv"])
                tt("dve", qk2[:], qk[:], qk[:], ALU.mult, ["qk"], ["qk2"])
                k.op("dve", lambda e: e.tensor_reduce(out=qss[:], in_=qk2[:], axis=AX.X, op=ALU.add), reads=["qk2"], writes=["qss"])
                act(qss[:], qss[:], AF.Sqrt, ["qss"], ["qss"], bias=EPS, scale=1.0 / 64)
                k.op("dve", lambda e: e.reciprocal(out=qss[:], in_=qss[:]), reads=["qss"], writes=["qss"])
                tt("dve", qk[:], qk[:], qss[:].unsqueeze(2).to_broadcast([128, 10, 64]), ALU.mult, ["qk", "qss"], ["qk"])
                tt("pool", qk[:], qk[:], nqk[:], ALU.mult, ["qk", "nqk"], ["qk"])
                src = qk
                if cond == 1:
                    k.dma("act", O["nck"][si - 1, l, ta - t0:ta - t0 + 128, :], qk[:, 8:10, :].rearrange("p h d -> p (h d)"), reads=["qk"], writes=["nck"])
                else:
                    k.dma("sp", rc[:], I["rcos"][ta:ta + 128, :], writes=["rc"])
                    k.dma("sp", rs_[:], I["rsin"][ta:ta + 128, :], writes=["rs_"])
                    xv = qk[:].rearrange("p h (a b f) -> p h a b f", a=2, b=2)
                    ov = qr[:].rearrange("p h (a b f) -> p h a b f", a=2, b=2)
                    cb = rc[:].rearrange("p (a f) -> p a f", a=2).unsqueeze(1).to_broadcast([128, 10, 2, 16])
                    sb_ = rs_[:].rearrange("p (a f) -> p a f", a=2).unsqueeze(1).to_broadcast([128, 10, 2, 16])
                    tt("dve", ov[:, :, :, 0, :], xv[:, :, :, 0, :], cb, ALU.mult, ["qk", "rc"], ["qr"])
                    tt("pool", rtmp[:], xv[:, :, :, 1, :], sb_, ALU.mult, ["qk", "rs_"], ["rtmp"])
                    tt("dve", ov[:, :, :, 0, :], ov[:, :, :, 0, :], rtmp[:], ALU.subtract, ["qr", "rtmp"], ["qr"])
                    tt("dve", ov[:, :, :, 1, :], xv[:, :, :, 1, :], cb, ALU.mult, ["qk", "rc"], ["qr"])
                    tt("pool", rtmp[:], xv[:, :, :, 0, :], sb_, ALU.mult, ["qk", "rs_", "qr"], ["rtmp"])
                    tt("dve", ov[:, :, :, 1, :], ov[:, :, :, 1, :], rtmp[:], ALU.add, ["qr", "rtmp"], ["qr"])
                    src = qr
                skey = "qk" if src is qk else "qr"
                for (h0, h1) in ((0, 4), (4, 8), (8, 10)):
                    pst, pk = nps()
                    for hh in range(h0, h1):
                        k.op("pe", lambda e: e.transpose(pst[:64, (hh - h0) * 128:(hh - h0 + 1) * 128], src[:, hh, :], C("ident")),
                             reads=[skey, "cst"], writes=[pk])
                    copy(ev(), qkT[:, h0:h1, :], pst[:64, :(h1 - h0) * 128].rearrange("p (h t) -> p h t", h=h1 - h0), [pk], ["qkT"])
                k.dma("sp", X["QCN"][:, :, ta:ta + 128], qkT[:], reads=["qkT"], writes=["QCN"])
        k.barrier()


def phase_out(X, stage):
    nc = X["nc"]; k = X["k"]; O = X["O"]; C = X["C"]; nps = X["nps"]; ev = X["ev"]; copy = X["copy"]
    with ExitStack() as ph:
        def sbp(name, shape, dt):
            return ph.enter_context(nc.sbuf_tensor(_un(name), list(shape), dt))
        xi = [sbp(f"oxi{i}", [128, 8, 128], F32) for i in range(2)]
        xo = [sbp(f"oxo{i}", [128, D], F32) for i in range(2)]
        for j in range(NT // 128):
            t0 = j * 128; b = j % 2
            k.dma("sp", xi[b][:], X["XTv"][:, :, 1 + t0:1 + t0 + 128], reads=["XT"], writes=[f"oxi{b}"])
            for hf in range(2):
                pst, pk = nps()
                for q in range(4):
                    kc = hf * 4 + q
                    k.op("pe", lambda e: e.transpose(pst[:, q * 128:(q + 1) * 128], xi[b][:, kc, :], C("ident")), reads=[f"oxi{b}", "cst"], writes=[pk])
                copy(ev(), xo[b][:, hf * 512:(hf + 1) * 512], pst[:], [pk], [f"oxo{b}"])
            dst = O["y_lat"][t0:t0 + 128, :] if t0 < NLAT else O["y_ctx"][t0 - NLAT:t0 - NLAT + 128, :]
            k.dma("act", dst, xo[b][:], reads=[f"oxo{b}"], writes=["y"])
        k.barrier()


def phase_aprep(X, l):
    nc = X["nc"]; k = X["k"]; I = X["I"]
    mm = X["mm"]; act = X["act"]; tt = X["tt"]; ts = X["ts"]; stt = X["stt"]; nps = X["nps"]
    with ExitStack() as ph:
        def sbp(name, shape, dt):
            return ph.enter_context(nc.sbuf_tensor(_un(name), list(shape), dt))
        cw = sbp("cw", [128, 12, 5], F32)
        for j in range(5):
            k.dma("sp", cw[:, :, j], I["conv_a"][l, j].rearrange("(c p) -> p c", p=128), writes=["cw"], allow_slow_non_contiguous=True)
        z = sbp("apz", [128, 12, 2], F32)
        k.op("dve", lambda e: e.memset(z[:], 0.0), writes=["apz"])
        Qv = X["QKVA"].rearrange("(c p) t -> p c t", p=128)
        if True:
            for si, (s0, ln, _) in enumerate(SEQS):
                a = padcol(si, s0)
                k.dma("sp", Qv[:, :, a - 2:a], z[:], reads=["apz"], writes=["QKVApad"])
                k.dma("sp", Qv[:, :, a + ln:a + ln + 2], z[:], reads=["apz"], writes=["QKVApad"])
        xp = [sbp(f"apx{i}", [128, 12, 516], F32) for i in range(2)]
        acc = sbp("apacc", [128, 12, 512], F32); sq = sbp("apsq", [128, 8, 512], BF16); rn = sbp("aprn", [128, 512], F32)
        ob = sbp("apob", [128, 12, 512], BF16)
        for ti, (t0, n, cond, si) in enumerate(TILES):
            b = ti % 2
            pc0 = padcol(si, t0)
            k.dma("sp", xp[b][:, :, :n + 4], Qv[:, :, pc0 - 2:pc0 + n + 2], reads=["QKVA", "QKVApad"], writes=[f"apx{b}"])
            for c in range(12):
                e = "dve"
                ts("dve" if c % 3 else "pool", acc[:, c, :n], xp[b][:, c, 0:n], cw[:, c, 0:1], None, ALU.mult, None, [f"apx{b}", "cw"], [f"acc{c}"])
                for j in range(1, 5):
                    stt(e, acc[:, c, :n], xp[b][:, c, j:j + n], cw[:, c, j:j + 1], acc[:, c, :n], ALU.mult, ALU.add, [f"apx{b}", "cw", f"acc{c}"], [f"acc{c}"])
            acck = [f"acc{c}" for c in range(12)]
            act(acc[:, :, :n], acc[:, :, :n], AF.Silu, acck, acck)
            act(sq[:, :, :n], acc[:, 0:8, :n], AF.Square, acck, ["apsq"])
            for c in range(8):
                pst, pk = nps()
                mm(pst[:, :n], X["onesb"][:], sq[:, c, :n], True, True, ["onesb", "apsq"], [pk])
                act(rn[:, :n], pst[:, :n], AF.Sqrt, [pk], ["aprn"], bias=EPS, scale=1.0)
                k.op("dve", lambda e: e.reciprocal(out=rn[:, :n], in_=rn[:, :n]), reads=["aprn"], writes=["aprn"])
                stt("dve", ob[:, c, :n], acc[:, c, :n], (128.0 ** -0.5) if c < 4 else 1.0, rn[:, :n], ALU.mult, ALU.mult, acck + ["aprn"], ["apob"])
            X["copy"]("pool", ob[:, 8:12, :n], acc[:, 8:12, :n], acck, ["apob"])
            k.dma("act", X["QKVN"].rearrange("(c p) t -> p c t", p=128)[:, :, t0:t0 + n], ob[:, :, :n], reads=["apob"], writes=["QKVN"])
        k.barrier()


def phase_delta(X, l, d):
    nc = X["nc"]; k = X["k"]; I = X["I"]; O = X["O"]; C = X["C"]
    mm = X["mm"]; act = X["act"]; tt = X["tt"]; ts = X["ts"]; stt = X["stt"]; nps = X["nps"]; ev = X["ev"]; copy = X["copy"]
    identb = X["identb"]
    with ExitStack() as ph:
        def sbp(name, shape, dt):
            return ph.enter_context(nc.sbuf_tensor(_un(name), list(shape), dt))
        S = sbp("dS", [128, 4, 128], F32); Sb = sbp("dSb", [128, 4, 128], BF16)
        negm = sbp("negm", [128, 4, 128], F32)
        copy("dve", negm[:], C(f"neg{d}").unsqueeze(1).to_broadcast([128, 4, 128]), ["cst"], ["negm"])
        qkv = [sbp(f"dqkv{i}", [128, 12, 128], BF16) for i in range(2)]
        bgT = [sbp(f"dbgT{i}", [16, 128], F32) for i in range(2)]
        ktok = sbp("dktok", [128, 4, 128], BF16); vtok = sbp("dvtok", [128, 4, 128], BF16)
        bgt = sbp("dbgt", [128, 16], F32); gct = sbp("dgct", [128, 8], F32); t12 = sbp("dt12", [128, 12], F32); e12 = sbp("de12", [128, 12], F32)
        bw = sbp("dbw", [128, 4], F32); nbeta = sbp("dnbeta", [128, 4], F32); ngc = sbp("dngc", [128, 4], F32)
        vb = sbp("dvb", [128, 4, 128], BF16); kbg = sbp("dkbg", [128, 4, 128], BF16); kd = sbp("dkd", [128, 4, 128], BF16)
        dg = sbp("ddg", [128, 2, 4, 128], F32); qg = sbp("dqg", [128, 4, 128], BF16)
        decs = sbp("ddecs", [128, 4, 128], BF16); P = sbp("dP", [128, 4, 128], BF16); Q = sbp("dQ", [128, 4, 128], BF16)
        qkm = sbp("dqkm", [128, 4, 128], BF16); qkmT = sbp("dqkmT", [128, 4, 128], BF16); R = sbp("dR", [128, 4, 128], BF16)
        nwT = sbp("dnwT", [128, 4, 128], BF16); vnew = sbp("dvnew", [128, 4, 128], BF16)
        oT = [sbp(f"doT{i}", [128, 4, 128], F32) for i in range(2)]
        Pk = sbp("dPk", [128, 4, 128], BF16); Qk = sbp("dQk", [128, 4, 128], BF16); Dm = sbp("dDm", [128, 4, 128], BF16)
        Xu = sbp("dXu", [128, 4, 128], BF16); Xd = sbp("dXd", [128, 4, 128], BF16)
        mk = {}
        for nm in ("b16", "l1", "l2", "l3"):
            mk[nm] = sbp("dmk" + nm, [128, 4, 128], BF16)
            copy("dve", mk[nm][:], C(nm).unsqueeze(1).to_broadcast([128, 4, 128]), ["cst"], ["dmk"])
        tri = C(f"tri{d}")
        identB4 = identb[:].unsqueeze(1).to_broadcast([128, 4, 128])
        identF4 = C("ident").unsqueeze(1).to_broadcast([128, 4, 128])

        def v4(ps):
            return ps[:].rearrange("p (h t) -> p h t", h=4)

        def vb4(ps):
            return ps[:].bitcast(BF16)[:, 0:512].rearrange("p (h t) -> p h t", h=4)

        ci = 0
        for si, (s0, ln, cond) in enumerate(SEQS):
            if cond == 0:
                k.dma("sp", S[:], I["sd0"][l, d].rearrange("h k v -> k h v"), writes=["dS"])
            else:
                k.op("dve", lambda e: e.memset(S[:], 0.0), writes=["dS"])
            copy("dve", Sb[:], S[:], ["dS"], ["dSb"])
            chunks = list(range(ln // 128))
            if d == 1:
                chunks = chunks[::-1]
            for cj in chunks:
                ta = s0 + cj * 128
                b = ci % 2; ci += 1
                qk_ = f"dqkv{b}"; bk_ = f"dbgT{b}"
                k.dma("sp", qkv[b][:], X["QKVN"].rearrange("(c p) t -> p c t", p=128)[:, :, ta:ta + 128], reads=["QKVN"], writes=[qk_])
                k.dma("sp", bgT[b][:], X["BG"][:, ta:ta + 128], reads=["BG"], writes=[bk_])
                qT = qkv[b][:, 0:4, :]; kT = qkv[b][:, 4:8, :]; vT = qkv[b][:, 8:12, :]
                for (srcT, dst, dk_) in ((kT, ktok, "dktok"), (vT, vtok, "dvtok")):
                    pst, pk = nps()
                    pb = vb4(pst)
                    for h in range(4):
                        k.op("pe", lambda e: e.transpose(pb[:, h, :], srcT[:, h, :], identb[:]), reads=[qk_, "identb"], writes=[pk])
                    copy(ev(), dst[:], pb, [pk], [dk_])
                pst, pk = nps()
                mm(pst[:, 0:16], bgT[b][:], C("ident")[0:16, 0:16], True, True, [bk_, "cst"], [pk])
                copy("dve", bgt[:], pst[:, 0:16], [pk], ["dbgt"])
                pst, pk = nps()
                gsl = bgt[:, 8 + 4 * d:12 + 4 * d]
                mm(pst[:, 0:4], tri, gsl, True, True, ["cst", "dbgt"], [pk])
                mm(pst[:, 4:8], C("ones"), gsl, True, True, ["cst", "dbgt"], [pk])
                copy("dve", gct[:], pst[:, 0:8], [pk], ["dgct"])
                copy("dve", t12[:, 0:4], gct[:, 0:4], ["dgct"], ["dt12"])
                tt("dve", t12[:, 4:8], gct[:, 4:8], gct[:, 0:4], ALU.subtract, ["dgct"], ["dt12"])
                copy("dve", t12[:, 8:12], gct[:, 4:8], ["dgct"], ["dt12"])
                act(e12[:], t12[:], AF.Exp, ["dt12"], ["de12"])
                beta = bgt[:, 4 * d:4 * d + 4]
                tt("dve", bw[:], beta, e12[:, 0:4], ALU.mult, ["dbgt", "de12"], ["dbw"])
                ts("dve", nbeta[:], beta, -1.0, None, ALU.mult, None, ["dbgt"], ["dnbeta"])

                def bc(ap):
                    return ap.unsqueeze(2).to_broadcast([128, 4, 128])
                tt("pool", vb[:], vtok[:], bc(beta), ALU.mult, ["dvtok", "dbgt"], ["dvb"])
                tt("pool", kbg[:], ktok[:], bc(bw[:]), ALU.mult, ["dktok", "dbw"], ["dkbg"])
                tt("pool", kd[:], ktok[:], bc(e12[:, 4:8]), ALU.mult, ["dktok", "de12"], ["dkd"])
                tt("dve", dg[:, 0], identF4, bc(e12[:, 0:4]), ALU.mult, ["cst", "de12"], ["ddg0"])
                tt("dve", dg[:, 1], identF4, bc(gct[:, 0:4]), ALU.mult, ["cst", "dgct"], ["ddg1"])
                pst, pk = nps()
                mm(pst[:], C("ones"), dg[:, 0].rearrange("p h t -> p (h t)"), True, True, ["cst", "ddg0"], [pk])
                tt("dve", qg[:], qT, v4(pst), ALU.mult, [qk_, pk], ["dqg"])
                pst, pk = nps()
                mm(pst[:], C("ones"), dg[:, 1].rearrange("p h t -> p (h t)"), True, False, ["cst", "ddg1"], [pk])
                mm(pst[:], C("ident"), negm[:].rearrange("p h t -> p (h t)"), False, True, ["cst", "negm"], [pk])
                for h in range(4):
                    act(decs[:, h, :], pst[:, h * 128:(h + 1) * 128], AF.Exp, [pk, "dgct"], ["ddecs"], bias=gct[:, h:h + 1], scale=-1.0)
                pkk, kkk = nps()
                for h in range(4):
                    mm(pkk[:, h * 128:(h + 1) * 128], kT[:, h, :], kT[:, h, :], True, True, [qk_], [kkk])
                for h in range(4):
                    stt("dve", P[:, h, :], pkk[:, h * 128:(h + 1) * 128], nbeta[:, h:h + 1], decs[:, h, :], ALU.mult, ALU.mult, [kkk, "dnbeta", "ddecs"], ["dP"])
                pqk, kqk = nps()
                for h in range(4):
                    mm(pqk[:, h * 128:(h + 1) * 128], qT[:, h, :], kT[:, h, :], True, True, [qk_], [kqk])
                tt("pool", decs[:], decs[:], identB4, ALU.add, ["ddecs", "identb"], ["ddecs"])
                tt("dve", qkm[:], v4(pqk), decs[:], ALU.mult, [kqk, "ddecs"], ["dqkm"])
                for (srcm, dst, sk_, dk_) in ((P, Q, "dP", "dQ"), (qkm, qkmT, "dqkm", "dqkmT")):
                    pst, pk = nps()
                    pb = vb4(pst)
                    for h in range(4):
                        k.op("pe", lambda e: e.transpose(pb[:, h, :], srcm[:, h, :], identb[:]), reads=[sk_, "identb"], writes=[pk])
                    copy(ev(), dst[:], pb, [pk], [dk_])
                def mm4(A, B, ra, rb):
                    ps_, pk_ = nps()
                    for h in range(4):
                        mm(ps_[:, h * 128:(h + 1) * 128], A[:, h, :], B[:, h, :], True, True, [ra, rb], [pk_])
                    return ps_, pk_
                tt("pool", Pk[:], P[:], mk["b16"][:], ALU.mult, ["dP", "dmk"], ["dPk"])
                tt("pool", Qk[:], Q[:], mk["b16"][:], ALU.mult, ["dQ", "dmk"], ["dQk"])
                tt("dve", Dm[:], Pk[:], identB4, ALU.add, ["dPk", "identb"], ["dDm"])
                tt("dve", R[:], Qk[:], identB4, ALU.add, ["dQk", "identb"], ["dR"])
                for lev in range(3):
                    pp, kp = mm4(Qk, Pk, "dQk", "dPk")
                    pq, kq = mm4(Pk, Qk, "dPk", "dQk")
                    copy("act", Pk[:], v4(pp), [kp], ["dPk"])
                    copy("dve", Qk[:], v4(pq), [kq], ["dQk"])
                    pd, kd_ = mm4(Qk, Dm, "dQk", "dDm")
                    pu, ku = mm4(Pk, R, "dPk", "dR")
                    tt("dve", Dm[:], Dm[:], v4(pd), ALU.add, ["dDm", kd_], ["dDm"])
                    tt("dve", R[:], R[:], v4(pu), ALU.add, ["dR", ku], ["dR"])
                for li, ln_ in enumerate(("l1", "l2", "l3")):
                    last = li == 2
                    tt("pool", Pk[:], P[:], mk[ln_][:], ALU.mult, ["dP", "dmk"], ["dPk"])
                    px, kx = mm4(Pk, R, "dPk", "dR")
                    copy("act", Xu[:], v4(px), [kx], ["dXu"])
                    if not last:
                        tt("pool", Qk[:], Q[:], mk[ln_][:], ALU.mult, ["dQ", "dmk"], ["dQk"])
                        pxd, kxd = mm4(Qk, Dm, "dQk", "dDm")
                        copy("dve", Xd[:], v4(pxd), [kxd], ["dXd"])
                    py, ky = mm4(Dm, Xu, "dDm", "dXu")
                    if not last:
                        pyd, kyd = mm4(R, Xd, "dR", "dXd")
                    tt("dve", R[:], R[:], v4(py), ALU.add, ["dR", ky], ["dR"])
                    if not last:
                        tt("dve", Dm[:], Dm[:], v4(pyd), ALU.add, ["dDm", kyd], ["dDm"])
                pst, pk = nps()
                for h in range(4):
                    mm(pst[:, h * 128:(h + 1) * 128], kbg[:, h, :], R[:, h, :], True, True, ["dkbg", "dR"], [pk])
                ts("dve", nwT[:], v4(pst), -1.0, None, ALU.mult, None, [pk], ["dnwT"])
                pst, pk = nps()
                for h in range(4):
                    mm(pst[:, h * 128:(h + 1) * 128], R[:, h, :], vb[:, h, :], True, False, ["dR", "dvb"], [pk])
                    mm(pst[:, h * 128:(h + 1) * 128], nwT[:, h, :], Sb[:, h, :], False, True, ["dnwT", "dSb"], [pk])
                copy("act", vnew[:], v4(pst), [pk], ["dvnew"])
                pst, pk = nps()
                for h in range(4):
                    mm(pst[:, h * 128:(h + 1) * 128], Sb[:, h, :], qg[:, h, :], True, False, ["dSb", "dqg"], [pk])
                    mm(pst[:, h * 128:(h + 1) * 128], vnew[:, h, :], qkmT[:, h, :], False, True, ["dvnew", "dqkmT"], [pk])
                ob_ = ci % 2
                copy("act", oT[ob_][:], v4(pst), [pk], [f"doT{ob_}"])
                k.dma("act", X["OA"][d].rearrange("(h p) t -> p h t", p=128)[:, :, ta:ta + 128], oT[ob_][:], reads=[f"doT{ob_}"], writes=["OA"])
                pst, pk = nps()
                for h in range(4):
                    mm(pst[:, h * 128:(h + 1) * 128], kd[:, h, :], vnew[:, h, :], True, True, ["dkd", "dvnew"], [pk])
                tt("dve", S[:], S[:], bc(e12[:, 8:12]), ALU.mult, ["dS", "de12"], ["dS"])
                tt("dve", S[:], S[:], v4(pst), ALU.add, ["dS", pk], ["dS"])
                copy("dve", Sb[:], S[:], ["dS"], ["dSb"])
            if cond == 1:
                k.dma("sp", O["nsd"][si - 1, l, d].rearrange("h k v -> k h v"), S[:], reads=["dS"], writes=["nsd"])
        k.barrier()


def phase_hgrn(X, l, d):
    nc = X["nc"]; k = X["k"]; I = X["I"]; O = X["O"]; C = X["C"]
    mm = X["mm"]; act = X["act"]; tt = X["tt"]; ts = X["ts"]; stt = X["stt"]; nps = X["nps"]; ev = X["ev"]; copy = X["copy"]
    identb = X["identb"]
    with ExitStack() as ph:
        def sbp(name, shape, dt):
            return ph.enter_context(nc.sbuf_tensor(_un(name), list(shape), dt))
        S = sbp("hS", [128, 4, 128], F32); Sb = sbp("hSb", [128, 4, 128], BF16)
        hm = sbp("hhm", [128, 4, 128], BF16)
        copy("dve", hm[:], C(f"hm{d}").unsqueeze(1).to_broadcast([128, 4, 128]), ["cst"], ["hhm"])
        qb = [sbp(f"hqb{i}", [128, 4, 128], BF16) for i in range(2)]
        ib = [sbp(f"hib{i}", [128, 4, 128], BF16) for i in range(2)]
        ff = [sbp(f"hff{i}", [128, 4, 128], F32) for i in range(2)]
        lf = sbp("hlf", [128, 512], F32); kf = sbp("hkf", [128, 512], F32); bb = sbp("hbb", [128, 512], F32); tmp = sbp("htmp", [128, 512], F32)
        bl = sbp("hbl", [128, 16], F32); ebl = sbp("hebl", [128, 16], F32)
        eb = sbp("heb", [128, 512], F32); enb = sbp("henb", [128, 512], F32); ekd = sbp("hekd", [128, 512], F32)
        qe = sbp("hqe", [128, 4, 128], BF16); ke = sbp("hke", [128, 4, 128], BF16); kdT = sbp("hkdT", [128, 4, 128], BF16)
        kdt = sbp("hkdt", [128, 4, 128], BF16); vtok = sbp("hvtok", [128, 4, 128], BF16); attm = sbp("hattm", [128, 4, 128], BF16); kd3 = sbp("hkd3", [128, 4, 128], BF16)
        oT = [sbp(f"hoT{i}", [128, 4, 128], F32) for i in range(2)]

        def v4(ps):
            return ps[:].rearrange("p (h t) -> p h t", h=4)

        def vb4(ps):
            return ps[:].bitcast(BF16)[:, 0:512].rearrange("p (h t) -> p h t", h=4)

        def f3(t):
            return t[:].rearrange("p (g s) -> p g s", s=32)
        ci = 0
        for si, (s0, ln, cond) in enumerate(SEQS):
            if cond == 0:
                k.dma("sp", S[:], I["sh0"][l, d].rearrange("h k v -> k h v"), writes=["hS"])
            else:
                k.op("dve", lambda e: e.memset(S[:], 0.0), writes=["hS"])
            copy("dve", Sb[:], S[:], ["hS"], ["hSb"])
            chunks = list(range(ln // 128))
            if d == 1:
                chunks = chunks[::-1]
            for cj in chunks:
                ta = s0 + cj * 128
                b = ci % 2; ci += 1
                k.dma("sp", qb[b][:], X["QB"].rearrange("(h p) t -> p h t", p=128)[:, :, ta:ta + 128], reads=["QB"], writes=[f"hqb{b}"])
                k.dma("sp", ib[b][:], X["IB"].rearrange("(h p) t -> p h t", p=128)[:, :, ta:ta + 128], reads=["IB"], writes=[f"hib{b}"])
                k.dma("act", ff[b][:], X["FF"][d * 512:(d + 1) * 512, :].rearrange("(h p) t -> p h t", p=128)[:, :, ta:ta + 128], reads=["FF"], writes=[f"hff{b}"])
                fv = ff[b][:].rearrange("p h t -> p (h t)")
                act(lf[:], fv, AF.Ln, [f"hff{b}"], ["hlf"])
                ts("pool", kf[:], fv, -1.0, 1.0, ALU.mult, ALU.add, [f"hff{b}"], ["hkf"])
                k.op("dve", lambda e: e.tensor_tensor_scan(out=bb[:], data0=C("seg"), data1=lf[:], initial=0.0, op0=ALU.mult, op1=ALU.add),
                     reads=["cst", "hlf"], writes=["hbb"])
                copy("dve", bl[:], f3(bb)[:, :, 31], ["hbb"], ["hbl"])
                if d == 1:
                    tt("dve", tmp[:], lf[:], bb[:], ALU.subtract, ["hlf", "hbb"], ["htmp"])
                    tt("dve", f3(bb), f3(tmp), bl[:].unsqueeze(2).to_broadcast([128, 16, 32]), ALU.add, ["htmp", "hbl"], ["hbb"])
                act(ebl[:], bl[:], AF.Exp, ["hbl"], ["hebl"])
                act(eb[:], bb[:], AF.Exp, ["hbb"], ["heb"])
                act(enb[:], bb[:], AF.Exp, ["hbb"], ["henb"], scale=-1.0)
                tt("dve", f3(tmp), bl[:].unsqueeze(2).to_broadcast([128, 16, 32]), f3(bb), ALU.subtract, ["hbb", "hbl", "htmp"], ["htmp"])
                act(ekd[:], tmp[:], AF.Exp, ["htmp"], ["hekd"])
                tt("dve", qe[:].rearrange("p h t -> p (h t)"), qb[b][:].rearrange("p h t -> p (h t)"), eb[:], ALU.mult, [f"hqb{b}", "heb"], ["hqe"])
                tt("pool", ke[:].rearrange("p h t -> p (h t)"), kf[:], enb[:], ALU.mult, ["hkf", "henb"], ["hke"])
                tt("pool", kdT[:].rearrange("p h t -> p (h t)"), kf[:], ekd[:], ALU.mult, ["hkf", "hekd"], ["hkdT"])
                for (srcT, dst, sk_, dk_) in ((kdT, kdt, "hkdT", "hkdt"), (ib[b], vtok, f"hib{b}", "hvtok")):
                    pst, pk = nps()
                    pb = vb4(pst)
                    for h in range(4):
                        k.op("pe", lambda e: e.transpose(pb[:, h, :], srcT[:, h, :], identb[:]), reads=[sk_, "identb"], writes=[pk])
                    copy(ev(), dst[:], pb, [pk], [dk_])
                ts("pool", kd3[64:128], kdt[64:128], C("tri1")[64:128, 96:97], None, ALU.mult, None, ["hkdt", "cst"], ["hkd3"])
                pst, pk = nps()
                for h in range(4):
                    mm(pst[:, h * 128:(h + 1) * 128], ke[:, h, :], qe[:, h, :], True, True, ["hke", "hqe"], [pk])
                tt("dve", attm[:], v4(pst), hm[:], ALU.mult, [pk, "hhm"], ["hattm"])
                po, ko = nps()
                blks = [0, 1, 2, 3] if d == 0 else [3, 2, 1, 0]
                for bi in blks:
                    cs = slice(bi * 32, (bi + 1) * 32)
                    for h in range(4):
                        osl = po[:, h * 128 + bi * 32:h * 128 + (bi + 1) * 32]
                        mm(osl, Sb[:, h, :], qe[:, h, cs], True, False, ["hSb", "hqe"], [ko])
                        mm(osl, vtok[:, h, :], attm[:, h, cs], False, True, ["hvtok", "hattm"], [ko])
                    pst, pk = nps()
                    for h in range(4):
                        if bi < 3:
                            mm(pst[:, h * 128:(h + 1) * 128], kdt[cs, h, :], vtok[cs, h, :], True, True, ["hkdt", "hvtok"], [pk])
                        else:
                            mm(pst[:, h * 128:(h + 1) * 128], kd3[64:128, h, :], vtok[64:128, h, :], True, True, ["hkd3", "hvtok"], [pk])
                    dec = ebl[:].rearrange("p (h g) -> p h g", g=4)[:, :, bi:bi + 1].to_broadcast([128, 4, 128])
                    tt("dve", S[:], S[:], dec, ALU.mult, ["hS", "hebl"], ["hS"])
                    tt("dve", S[:], S[:], v4(pst), ALU.add, ["hS", pk], ["hS"])
                    copy("act", Sb[:], S[:], ["hS"], ["hSb"])
                ob_ = ci % 2
                copy("act", oT[ob_][:], v4(po), [ko], [f"hoT{ob_}"])
                k.dma("act", X["OB"][d].rearrange("(h p) t -> p h t", p=128)[:, :, ta:ta + 128], oT[ob_][:], reads=[f"hoT{ob_}"], writes=["OB"])
            if cond == 1:
                k.dma("sp", O["nsh"][si - 1, l, d].rearrange("h k v -> k h v"), S[:], reads=["hS"], writes=["nsh"])
        k.barrier()


def phase_attn(X, l):
    nc = X["nc"]; k = X["k"]; I = X["I"]; O = X["O"]; C = X["C"]
    mm = X["mm"]; act = X["act"]; tt = X["tt"]; ts = X["ts"]; stt = X["stt"]; nps = X["nps"]; ev = X["ev"]; copy = X["copy"]
    with ExitStack() as ph:
        def sbp(name, shape, dt):
            return ph.enter_context(nc.sbuf_tensor(_un(name), list(shape), dt))
        ckt = sbp("ackt", [128, 2, 128], F32); ckT = sbp("ackT", [64, 2, 256], BF16); cvb = sbp("acvb", [128, 2, 128], BF16)
        k.dma("sp", ckt[:], I["ck"][l].rearrange("(b p) f -> p b f", p=128), writes=["ackt"])
        k.dma("pool", cvb[:], I["cv"][l].rearrange("(b p) f -> p b f", p=128), writes=["acvb"])
        for g in range(2):
            pst, pk = nps()
            for bk in range(2):
                k.op("pe", lambda e: e.transpose(pst[:64, bk * 128:(bk + 1) * 128], ckt[:, bk, g * 64:(g + 1) * 64], C("ident")), reads=["ackt", "cst"], writes=[pk])
            copy("dve", ckT[:, g, :], pst[:64, 0:256], [pk], ["ackT"])
        esk = sbp("aesk", [64, 8], F32)
        k.dma("sp", esk[:], I["sink"][l:l + 1, :].to_broadcast([64, 8]), writes=["aesk"])
        act(esk[:], esk[:], AF.Exp, ["aesk"], ["aesk"])
        wm = sbp("awm", [128, 2, 128], BF16)
        copy("dve", wm[:, 0, :], C("wprev"), ["cst"], ["awm"])
        copy("dve", wm[:, 1, :], C("wnext"), ["cst"], ["awm"])
        qT = [sbp(f"aq{i}", [64, 8, 128], BF16) for i in range(2)]
        kT3 = [sbp(f"ak{i}", [64, 2, 384], BF16) for i in range(2)]
        v3 = [sbp(f"av{i}", [128, 3, 128], BF16) for i in range(2)]
        pT = [sbp(f"ap{i}", [128, 512], BF16) for i in range(3)]
        den = sbp("aden", [64, 4, 128], F32); oc = [sbp(f"aoc{i}", [64, 8, 128], BF16) for i in range(2)]
        ci = 0; pi = 0
        for si, (s0, ln, cond) in enumerate(SEQS):
            nb = ln // 128
            for qbk in range(nb):
                ta = s0 + qbk * 128
                b = ci % 2; ci += 1
                k.dma("sp", qT[b][:], X["QCN"][:, 0:8, ta:ta + 128], reads=["QCN"], writes=[f"aq{b}"])
                if cond == 0:
                    lo = max(qbk - 1, 0); hi = min(qbk + 1, nb - 1)
                else:
                    lo, hi = 0, nb - 1
                nk = hi - lo + 1
                k.dma("sp", kT3[b][:, :, 0:nk * 128], X["QCN"][:, 8:10, s0 + lo * 128:s0 + (hi + 1) * 128], reads=["QCN"], writes=[f"ak{b}"])
                k.dma("act", v3[b][:, 0:nk, :], X["VCN"][s0 + lo * 128:s0 + (hi + 1) * 128, :].rearrange("(b p) f -> p b f", p=128), reads=["VCN"], writes=[f"av{b}"])
                for g in range(2):
                    kbl = []
                    for j in range(nk):
                        kb_ = lo + j
                        m = None
                        if cond == 0 and kb_ == qbk - 1:
                            m = 0
                        if cond == 0 and kb_ == qbk + 1:
                            m = 1
                        kbl.append((kT3[b][:, g, j * 128:(j + 1) * 128], v3[b][:, j, g * 64:(g + 1) * 64], m, [f"ak{b}"], [f"av{b}"]))
                    if cond == 0:
                        for bk in range(2):
                            kbl.append((ckT[:, g, bk * 128:(bk + 1) * 128], cvb[:, bk, g * 64:(g + 1) * 64], None, ["ackT"], ["acvb"]))
                    po, ko = nps(); pr, kr = nps()
                    for j, (kap, vap, m, kk_, vk_) in enumerate(kbl):
                        pst, pk = nps()
                        mm(pst[:], kap, qT[b][:, g * 4:(g + 1) * 4, :].rearrange("p h t -> p (h t)"), True, True, kk_ + [f"aq{b}"], [pk])
                        pb = pi % 3; pi += 1
                        act(pT[pb][:], pst[:], AF.Exp, [pk], [f"ap{pb}"], scale=0.125)
                        if m is not None:
                            tt("pool", pT[pb][:].rearrange("p (h t) -> p h t", h=4), pT[pb][:].rearrange("p (h t) -> p h t", h=4),
                               wm[:, m, :].unsqueeze(1).to_broadcast([128, 4, 128]), ALU.mult, [f"ap{pb}", "awm"], [f"ap{pb}"])
                        mm(po[:64, :], vap, pT[pb][:], j == 0, j == len(kbl) - 1, vk_ + [f"ap{pb}"], [ko])
                        mm(pr[:64, :], X["onesb"][:, 0:64], pT[pb][:], j == 0, j == len(kbl) - 1, ["onesb", f"ap{pb}"], [kr])
                    tt("dve", den[:], pr[:64, :].rearrange("p (h t) -> p h t", h=4), esk[:, g * 4:(g + 1) * 4].unsqueeze(2).to_broadcast([64, 4, 128]), ALU.add, [kr, "aesk"], ["aden"])
                    k.op("dve", lambda e: e.reciprocal(out=den[:], in_=den[:]), reads=["aden"], writes=["aden"])
                    tt("dve", oc[b][:, g * 4:(g + 1) * 4, :], po[:64, :].rearrange("p (h t) -> p h t", h=4), den[:], ALU.mult, [ko, "aden"], [f"aoc{b}"])
                k.dma("act", X["OC"][:, :, ta:ta + 128], oc[b][:], reads=[f"aoc{b}"], writes=["OC"])
        k.barrier()


MTILES = [(t0 + h * 256, 256, c, s) for (t0, n, c, s) in TILES for h in range(n // 256)]


def phase_merge(X, l):
    nc = X["nc"]; k = X["k"]; I = X["I"]; O = X["O"]; C = X["C"]
    mm = X["mm"]; act = X["act"]; tt = X["tt"]; ts = X["ts"]; stt = X["stt"]; nps = X["nps"]; ev = X["ev"]; copy = X["copy"]
    n = 256
    with ExitStack() as ph:
        def sbp(name, shape, dt):
            return ph.enter_context(nc.sbuf_tensor(_un(name), list(shape), dt))
        Wg = sbp("mWg", [128, 8, 3072], BF16); Wb = sbp("mWb", [128, 8, 1024], BF16); WbC = sbp("mWbC", [64, 8, 1024], BF16); Wo = sbp("mWo", [128, 8, 1024], BF16)
        for kc in range(8):
            k.dma("pool", Wg[:, kc, :], I["w_in"][l][kc * 128:(kc + 1) * 128, C_MG:DIN], writes=["mWg"])
            k.dma("pool", Wo[:, kc, :], I["w_out"][l][kc * 128:(kc + 1) * 128, :], writes=["mWo"])
            k.dma("pool", Wb[:, kc, :], I["w_branch"][l, kc // 4][(kc % 4) * 128:(kc % 4 + 1) * 128, :], writes=["mWb"])
            k.dma("pool", WbC[:, kc, :], I["w_branch"][l, 2][kc * 64:(kc + 1) * 64, :], writes=["mWbC"])
        nab = sbp("mnab", [128, 2], F32)
        k.dma("sp", nab[:, 0:1], I["norm_a"][l].rearrange("(p o) -> p o", o=1), writes=["mnab"], allow_slow_non_contiguous=True)
        k.dma("sp", nab[:, 1:2], I["norm_b"][l].rearrange("(p o) -> p o", o=1), writes=["mnab"], allow_slow_non_contiguous=True)
        xT = sbp("mxT", [128, 8, n], F32); xr = sbp("mxr", [128, 8, n], F32); hT = sbp("mhT", [128, 8, n], BF16); rstd = sbp("mrstd", [128, n], F32)
        X["mrstd"] = rstd
        of = sbp("mof", [128, 4, n], F32); ob = sbp("mob", [128, 4, n], F32); gt = sbp("mgt", [128, 4, n], BF16); sq = sbp("msq", [128, 4, n], BF16)
        rn = sbp("mrn", [128, 4, n], F32)
        oab = sbp("moab", [128, 8, n], BF16); oc = sbp("moc", [64, 8, n], BF16); mg = sbp("mmg", [128, 8, n], BF16)
        sg = [sbp(f"msg{i}", [128, n], F32) for i in range(3)]
        for (t0, _, cond, si) in MTILES:
            k.dma("sp", xT[:], X["XTv"][:, :, 1 + t0:1 + t0 + n], reads=["XT"], writes=["mxT"])
            k.dma("act", xr[:], X["XTv"][:, :, 1 + t0:1 + t0 + n], reads=["XT"], writes=["mxr"])
            _norm_h(X, l, 1, xT, hT, n, cond, "m")
            for r, (src, gsrc, gk) in enumerate(((X["OA"], X["GA"], "GA"), (X["OB"], X["GB"], "GB"))):
                k.dma("sp", of[:], src[0].rearrange("(h p) t -> p h t", p=128)[:, :, t0:t0 + n], reads=["OA", "OB"], writes=["mof"])
                k.dma("act", ob[:], src[1].rearrange("(h p) t -> p h t", p=128)[:, :, t0:t0 + n], reads=["OA", "OB"], writes=["mob"])
                k.dma("sp", gt[:], gsrc.rearrange("(h p) t -> p h t", p=128)[:, :, t0:t0 + n], reads=[gk], writes=["mgt"])
                tt("dve", of[:], of[:], ob[:], ALU.add, ["mof", "mob"], ["mof"])
                act(sq[:], of[:], AF.Square, ["mof"], ["msq"])
                for hp in range(2):
                    pst, pk = nps()
                    for hh in range(2):
                        mm(pst[:, hh * n:(hh + 1) * n], X["onesb"][:], sq[:, hp * 2 + hh, :], True, True, ["onesb", "msq"], [pk])
                    act(rn[:, hp * 2:hp * 2 + 2, :], pst[:].rearrange("p (h t) -> p h t", h=2), AF.Sqrt, [pk], ["mrn"], bias=EPS, scale=1.0 / 128)
                k.op("dve", lambda e: e.reciprocal(out=rn[:], in_=rn[:]), reads=["mrn"], writes=["mrn"])
                tt("dve", of[:], of[:], rn[:], ALU.mult, ["mof", "mrn"], ["mof"])
                stt("dve", oab[:, r * 4:(r + 1) * 4, :], of[:], nab[:, r:r + 1], gt[:], ALU.mult, ALU.mult, ["mof", "mnab", "mgt"], ["moab"])
            k.dma("sp", oc[:], X["OC"][:, :, t0:t0 + n], reads=["OC"], writes=["moc"])
            for m in range(8):
                ms = slice(m * 128, (m + 1) * 128)
                brs = []
                for r in range(3):
                    pb_, kb_ = nps()
                    if r < 2:
                        for kc in range(4):
                            mm(pb_[:, :n], Wb[:, r * 4 + kc, ms], oab[:, r * 4 + kc, :], kc == 0, kc == 3, ["mWb", "moab"], [kb_])
                    else:
                        for hh in range(8):
                            mm(pb_[:, :n], WbC[:, hh, ms], oc[:, hh, :], hh == 0, hh == 7, ["mWbC", "moc"], [kb_])
                    pg_, kg_ = nps()
                    for kc in range(8):
                        mm(pg_[:, :n], Wg[:, kc, r * 1024 + m * 128:r * 1024 + (m + 1) * 128], hT[:, kc, :], kc == 0, kc == 7, ["mWg", "mhT"], [kg_])
                    act(sg[r][:], pg_[:, :n], AF.Sigmoid, [kg_], [f"msg{r}"])
                    tt("dve", sg[r][:], sg[r][:], pb_[:, :n], ALU.mult, [f"msg{r}", kb_], [f"msg{r}"])
                tt("pool", sg[0][:], sg[0][:], sg[1][:], ALU.add, ["msg0", "msg1"], ["msg0"])
                tt("pool", mg[:, m, :], sg[0][:], sg[2][:], ALU.add, ["msg0", "msg2"], ["mmg"])
            for m in range(8):
                pst, pk = nps()
                for kc in range(8):
                    mm(pst[:, :n], Wo[:, kc, m * 128:(m + 1) * 128], mg[:, kc, :], kc == 0, kc == 7, ["mWo", "mmg"], [pk])
                stt("dve", xr[:, m, :], pst[:, :n], X["modT"][:, l, cond, 16 + m:17 + m], xr[:, m, :], ALU.mult, ALU.add, [pk, "modT", "mxr"], ["mxr"])
            k.dma("act", X["XTv"][:, :, 1 + t0:1 + t0 + n], xr[:], reads=["mxr"], writes=["XT"])
        k.barrier()


def phase_ffn(X, l):
    nc = X["nc"]; k = X["k"]; I = X["I"]; O = X["O"]; C = X["C"]
    mm = X["mm"]; act = X["act"]; tt = X["tt"]; ts = X["ts"]; stt = X["stt"]; nps = X["nps"]; ev = X["ev"]; copy = X["copy"]
    n = 256
    XT2 = X["FFfull"]
    XT2v = XT2.rearrange("(c p) t -> p c t", p=128)
    with ExitStack() as ph:
        def sbp(name, shape, dt):
            return ph.enter_context(nc.sbuf_tensor(_un(name), list(shape), dt))
        Wu = sbp("fWu", [128, 8, 2 * DFF], BF16); Wd = sbp("fWd", [128, 22, 1024], BF16)
        for kc in range(8):
            k.dma("pool", Wu[:, kc, :], I["w_up"][l][kc * 128:(kc + 1) * 128, :], writes=["fWu"])
        for j in range(22):
            k.dma("pool", Wd[:, j, :], I["w_down"][l][j * 128:(j + 1) * 128, :], writes=["fWd"])
        cwf = sbp("fcw", [128, 44, 3], F32)
        for j in range(3):
            k.dma("sp", cwf[:, :, j], I["conv_ffn"][l, j].rearrange("(c p) -> p c", p=128), writes=["fcw"], allow_slow_non_contiguous=True)
        xT = sbp("fxT", [128, 8, n], F32); xr = sbp("fxr", [128, 8, n], F32); hT = sbp("fhT", [128, 8, n], BF16); rstd = sbp("frstd", [128, n], F32)
        xh = sbp("gxT", [128, 8, 2], F32); hh_ = sbp("ghT", [128, 8, 2], BF16); rstdh = sbp("grstd", [128, 2], F32)
        X["frstd"] = rstd; X["grstd"] = rstdh
        acc = [sbp(f"facc{i}", [128, n], F32) for i in range(2)]
        pr = sbp("fpr", [128, 22, n], BF16)
        for (t0, _, cond, si) in MTILES:
            s0, ln, _c = SEQS[si]
            has_l = t0 > s0; has_r = (t0 + n) < (s0 + ln)
            k.dma("sp", xT[:], X["XTv"][:, :, 1 + t0:1 + t0 + n], reads=["XT"], writes=["fxT"])
            k.dma("act", xr[:], X["XTv"][:, :, 1 + t0:1 + t0 + n], reads=["XT"], writes=["fxr"])
            k.dma("sp", xh[:, :, 0:1], X["XTv"][:, :, t0:t0 + 1], reads=["XT"], writes=["gxT"], allow_slow_non_contiguous=True)
            k.dma("sp", xh[:, :, 1:2], X["XTv"][:, :, 1 + t0 + n:2 + t0 + n], reads=["XT"], writes=["gxT"], allow_slow_non_contiguous=True)
            _norm_h(X, l, 2, xT, hT, n, cond, "f")
            _norm_h(X, l, 2, xh, hh_, 2, cond, "g")
            for j in range(22):
                for w_, c0 in enumerate((j * 128, DFF + j * 128)):
                    cc = c0 // 128
                    pst, pk = nps()
                    for kc in range(8):
                        mm(pst[:, :n], Wu[:, kc, c0:c0 + 128], hT[:, kc, :], kc == 0, kc == 7, ["fWu", "fhT"], [pk])
                    ph_, kh_ = nps()
                    for kc in range(8):
                        mm(ph_[:, 0:2], Wu[:, kc, c0:c0 + 128], hh_[:, kc, :], kc == 0, kc == 7, ["fWu", "ghT"], [kh_])
                    a = acc[w_]; ak = f"facc{w_}"
                    act(a[:], pst[:, :n], AF.Copy, [pk, "fcw"], [ak], scale=cwf[:, cc, 1:2])
                    stt("dve", a[:, 1:n], pst[:, 0:n - 1], cwf[:, cc, 0:1], a[:, 1:n], ALU.mult, ALU.add, [pk, "fcw", ak], [ak])
                    stt("dve", a[:, 0:n - 1], pst[:, 1:n], cwf[:, cc, 2:3], a[:, 0:n - 1], ALU.mult, ALU.add, [pk, "fcw", ak], [ak])
                    if has_l:
                        stt("dve", a[:, 0:1], ph_[:, 0:1], cwf[:, cc, 0:1], a[:, 0:1], ALU.mult, ALU.add, [kh_, "fcw", ak], [ak])
                    if has_r:
                        stt("dve", a[:, n - 1:n], ph_[:, 1:2], cwf[:, cc, 2:3], a[:, n - 1:n], ALU.mult, ALU.add, [kh_, "fcw", ak], [ak])
                act(acc[0][:], acc[0][:], AF.Silu, ["facc0"], ["facc0"])
                tt("pool", pr[:, j, :], acc[0][:], acc[1][:], ALU.mult, ["facc0", "facc1"], ["fpr"])
            for m in range(8):
                pst, pk = nps()
                for j in range(22):
                    mm(pst[:, :n], Wd[:, j, m * 128:(m + 1) * 128], pr[:, j, :], j == 0, j == 21, ["fWd", "fpr"], [pk])
                stt("dve", xr[:, m, :], pst[:, :n], X["modT"][:, l, cond, 40 + m:41 + m], xr[:, m, :], ALU.mult, ALU.add, [pk, "modT", "fxr"], ["fxr"])
            k.dma("act", XT2v[:, :, 1 + t0:1 + t0 + n], xr[:], reads=["fxr"], writes=["XT2"])
        k.barrier()
        k.dma("sp", X["XT"][:, 1:1 + NT], XT2[:, 1:1 + NT], reads=["XT2"], writes=["XT"])
        k.barrier()


_CACHE = {}


def kernel(x_prompt, x_sample, state_delta, state_hgrn, cache_k, cache_v, c, c_ctx,
           ada_w, ada_b, norm1_w, w_in, conv_a, a_log, dt_bias, norm_a, lb_logits, norm_b,
           q_norm, k_norm, sink, w_branch, w_out, norm2_w, w_up, conv_ffn, w_down, _stage=99, _debug=()):
    f = lambda a: np.ascontiguousarray(np.asarray(a, dtype=np.float32))
    if "nc" not in _CACHE or _CACHE.get("stage") != _stage:
        _CACHE["nc"] = build_program(_stage, _debug)
        _CACHE["stage"] = _stage
    nc = _CACHE["nc"]
    carr, _ = host_consts()
    rcos, rsin = rope_tables()
    shared = dict(ada_w=f(ada_w), ada_b=f(ada_b), norm1_w=f(norm1_w), w_in=f(w_in), conv_a=f(conv_a),
                  a_log=f(a_log).reshape(DEPTH, 8), dt_bias=f(dt_bias).reshape(DEPTH, 8), norm_a=f(norm_a), lb_logits=f(lb_logits),
                  norm_b=f(norm_b), q_norm=f(q_norm), k_norm=f(k_norm), sink=f(sink), w_branch=f(w_branch), w_out=f(w_out),
                  norm2_w=f(norm2_w), w_up=f(w_up), conv_ffn=f(conv_ffn), w_down=f(w_down), consts=carr, rcos=rcos, rsin=rsin)
    xp = f(x_prompt); xs = f(x_sample); sd = f(state_delta); sh = f(state_hgrn); ck = f(cache_k); cv = f(cache_v)
    cc = f(c); cx = f(c_ctx)
    in_maps = []
    for core in range(8):
        b = core % 4
        m = dict(shared)
        m["x_lat"] = xs[b]
        m["x_ctx"] = xp[4 * core:4 * core + 4].reshape(NCTX, D)
        m["cond2"] = np.stack([cc[b], cx], axis=0)
        m["sd0"] = sd[b]; m["sh0"] = sh[b]
        m["ck"] = ck[b].reshape(DEPTH, 256, 128); m["cv"] = cv[b].reshape(DEPTH, 256, 128)
        in_maps.append(m)
    res = run_bass_kernel_spmd(nc, in_maps, core_ids=list(range(8)))
    R = res.results
    _CACHE["last"] = R
    y_prompt = np.concatenate([R[i]["y_ctx"].reshape(4, 256, D) for i in range(8)], axis=0)
    y_sample = np.stack([R[i]["y_lat"] for i in range(4)], axis=0)
    nsd = np.concatenate([R[i]["nsd"] for i in range(8)], axis=0)
    nsh = np.concatenate([R[i]["nsh"] for i in range(8)], axis=0)
    nck = np.concatenate([R[i]["nck"].reshape(4, DEPTH, 256, 2, 64) for i in range(8)], axis=0)
    ncv = np.concatenate([R[i]["ncv"].reshape(4, DEPTH, 256, 2, 64) for i in range(8)], axis=0)
    return (y_prompt.astype(np.float32), y_sample.astype(np.float32), nsd.astype(np.float32), nsh.astype(np.float32),
            nck.astype(np.float32), ncv.astype(np.float32))
```

```python
from contextlib import ExitStack
import numpy as np
import concourse.bass as bass
import concourse.mybir as mybir
from concourse.bass_utils import run_bass_kernel_spmd

F32 = mybir.dt.float32
BF16 = mybir.dt.bfloat16
AF = mybir.ActivationFunctionType
ALU = mybir.AluOpType
AX = mybir.AxisListType


_UID = [0]


def _un(name):
    _UID[0] += 1
    return f"{name}_u{_UID[0]}"


class KB:
    RING = 12

    def __init__(self, nc):
        self.nc = nc
        self.es = ExitStack()
        self.eng = {"pe": nc.tensor, "act": nc.scalar, "dve": nc.vector, "pool": nc.gpsimd, "sp": nc.sync}
        self.sem = {}
        for e in ("pe", "act", "dve", "pool"):
            self.sem[e] = self.es.enter_context(nc.semaphore("s_" + e))
        self.cnt = {e: 0 for e in self.sem}
        self.ring = {}
        self.dcnt = {}
        for q in ("sp", "pool", "act"):
            self.ring[q] = [self.es.enter_context(nc.semaphore(f"d_{q}{i}")) for i in range(self.RING)]
            self.dcnt[q] = 0
        self.seen = {e: {} for e in self.eng}
        self.seenseq = {e: {} for e in self.eng}
        self.seq = {e: 0 for e in self.sem}
        self.lastins = {}
        self.sigmap = {}
        self.last_w = {}
        self.readers = {}
        self.n_inst = 0
        self.nosame = False
        self._in_dma = False
        self.prefix = ""

    def sb(self, name, shape, dt):
        return self.es.enter_context(self.nc.sbuf_tensor(name, list(shape), dt))

    def ps(self, name, shape, dt=F32):
        return self.es.enter_context(self.nc.psum_tensor(name, list(shape), dt))

    def dram(self, name, shape, dt, kind="Internal"):
        return self.nc.dram_tensor(name, list(shape), dt, kind=kind)

    def _signal_upto(self, e2, seq):
        sm = self.sigmap.setdefault(e2, [])
        if not sm or sm[-1][0] < seq:
            ins, lseq = self.lastins[e2]
            assert lseq >= seq
            self.cnt[e2] += 1
            ins.then_inc(self.sem[e2], 1)
            sm.append((lseq, self.cnt[e2]))
        lo, hi = 0, len(sm) - 1
        while lo < hi:
            mid = (lo + hi) // 2
            if sm[mid][0] >= seq:
                hi = mid
            else:
                lo = mid + 1
        return sm[lo][1]

    def _wait(self, e, tok):
        kind = tok[0]
        if kind == "c":
            _, e2, seq = tok
            if e2 == e and (e == "pe" or (self.nosame and not self._in_dma)):
                return
            key = ("c", e2)
            if self.seenseq[e].get(key, 0) >= seq:
                return
            val = self._signal_upto(e2, seq)
            self.seenseq[e][key] = seq
            if self.seen[e].get(key, 0) >= val:
                return
            self.eng[e].wait_ge(self.sem[e2], val)
            self.seen[e][key] = val
        else:
            _, q, slot, val = tok
            key = ("d", q, slot)
            if self.seen[e].get(key, 0) >= val:
                return
            self.eng[e].wait_ge(self.ring[q][slot], val)
            self.seen[e][key] = val

    def _deps(self, e, reads, writes):
        toks = []
        for k in reads:
            if k in self.last_w:
                toks.append(self.last_w[k])
        for k in writes:
            if k in self.last_w:
                toks.append(self.last_w[k])
            toks.extend(self.readers.get(k, ()))
        for t in toks:
            self._wait(e, t)

    def _commit(self, tok, reads, writes):
        for k in reads:
            self.readers.setdefault(k, []).append(tok)
            if len(self.readers[k]) > 64:
                best = {}
                for t in self.readers[k]:
                    kk = t[:2] if t[0] == "c" else t[:3]
                    if kk not in best or t[-1] > best[kk][-1]:
                        best[kk] = t
                self.readers[k] = list(best.values())
        for k in writes:
            self.last_w[k] = tok
            self.readers[k] = []

    def _pk(self, keys):
        if not self.prefix:
            return keys
        return [x if x.startswith("ps") else self.prefix + x for x in keys]

    def op(self, e, fn, reads=(), writes=()):
        reads = self._pk(reads); writes = self._pk(writes)
        self._deps(e, reads, writes)
        ins = fn(self.eng[e])
        self.seq[e] += 1
        self.lastins[e] = (ins, self.seq[e])
        self._commit(("c", e, self.seq[e]), reads, writes)
        self.n_inst += 1
        return ins

    def dma(self, q, out, in_, reads=(), writes=(), **kw):
        reads = self._pk(reads); writes = self._pk(writes)
        i = self.dcnt[q]
        slot = i % self.RING
        rnd = i // self.RING
        if rnd > 0:
            self._wait(q, ("d", q, slot, 16 * rnd))
        self._in_dma = True
        self._deps(q, reads, writes)
        self._in_dma = False
        ins = self.eng[q].dma_start(out=out, in_=in_, **kw)
        ins.then_inc(self.ring[q][slot], 16)
        self.dcnt[q] += 1
        self._commit(("d", q, slot, 16 * (rnd + 1)), reads, writes)
        self.n_inst += 1
        return ins

    def finish(self, out_keys):
        for k in out_keys:
            if k in self.last_w:
                self._wait("sp", self.last_w[k])
        for e in ("pe", "act", "dve", "pool"):
            if self.seq[e]:
                self._wait("sp", ("c", e, self.seq[e]))
        for q in self.ring:
            n = self.dcnt[q]
            for slot in range(min(n, self.RING)):
                last_i = ((n - 1 - slot) // self.RING) * self.RING + slot
                self._wait("sp", ("d", q, slot, 16 * (last_i // self.RING + 1)))

    def barrier(self):
        toks = [("c", e, self.seq[e]) for e in self.seq if self.seq[e]]
        for q in self.ring:
            n = self.dcnt[q]
            for slot in range(min(n, self.RING)):
                last_i = ((n - 1 - slot) // self.RING) * self.RING + slot
                toks.append(("d", q, slot, 16 * (last_i // self.RING + 1)))
        for e in self.eng:
            for t in toks:
                self._wait(e, t)
        self.last_w = {}
        self.readers = {}


D = 1024
NLAT = 4096
NCTX = 1024
NT = NLAT + NCTX
DEPTH = 2
DIN = 8464
DFF = 2816
EPS = 1e-6
SEQS = [(0, 4096, 0)] + [(4096 + 256 * i, 256, 1) for i in range(4)]
TILES = [(512 * j, 512, 0, 0) for j in range(8)] + [(4096 + 256 * i, 256, 1, 1 + i) for i in range(4)]
PADA = 2
NTP = NT + 2 * PADA * len(SEQS)
BIG = 30000.0
C_QA, C_KA, C_VA, C_GA, C_BETA, C_ALPHA = 0, 512, 1024, 1536, 2048, 2056
C_QB, C_IB, C_FB, C_GB = 2064, 2576, 3088, 4112
C_QC, C_KC, C_VC, C_MG = 4624, 5136, 5264, 5392


def padcol(seq_idx, t):
    return t + PADA * (2 * seq_idx + 1)


def host_consts():
    c = {}
    i = np.arange(128)
    c["ident"] = np.eye(128, dtype=np.float32)
    c["ones"] = np.ones((128, 128), np.float32)
    triF = (i[:, None] <= i[None, :]).astype(np.float32)
    triB = (i[:, None] >= i[None, :]).astype(np.float32)
    c["tri0"], c["tri1"] = triF, triB
    c["neg0"] = np.where(i[None, :] < i[:, None], 0.0, BIG).astype(np.float32)
    c["neg1"] = np.where(i[None, :] > i[:, None], 0.0, BIG).astype(np.float32)
    blk = (i[:, None] // 32) == (i[None, :] // 32)
    c["hm0"] = (blk & (i[None, :] >= i[:, None])).astype(np.float32)
    c["hm1"] = (blk & (i[None, :] <= i[:, None])).astype(np.float32)
    c["wprev"] = (i[:, None] >= i[None, :]).astype(np.float32)
    c["wnext"] = (i[:, None] <= i[None, :]).astype(np.float32)
    def same(b_):
        return (i[:, None] // b_) == (i[None, :] // b_)
    c["b16"] = same(16).astype(np.float32)
    c["l1"] = (same(32) & ~same(16)).astype(np.float32)
    c["l2"] = (same(64) & ~same(32)).astype(np.float32)
    c["l3"] = (~same(64)).astype(np.float32)
    seg = np.ones((128, 512), np.float32)
    seg[:, ::32] = 0.0
    c["seg"] = seg
    names = list(c)
    arr = np.concatenate([c[n] for n in names], axis=1)
    offs = {}
    o = 0
    for n in names:
        offs[n] = (o, c[n].shape[1])
        o += c[n].shape[1]
    return arr, offs


def rope_tables():
    pos = np.arange(NLAT)
    row = (pos // 64).astype(np.float32)
    col = (pos % 64).astype(np.float32)
    nf = 16
    inv = (10000.0 ** (-np.arange(nf, dtype=np.float32) / nf)).astype(np.float32)
    ar = row[:, None] * inv
    ac = col[:, None] * inv
    cos = np.concatenate([np.cos(ar), np.cos(ac)], axis=1).astype(np.float32)
    sin = np.concatenate([np.sin(ar), np.sin(ac)], axis=1).astype(np.float32)
    return cos, sin


def build_program(stage=99, debug=()):
    nc = bass.Bass("TRN2", target_bir_lowering=False)
    k = KB(nc)
    k.nosame = False
    carr, coff = host_consts()

    def din(name, shape, dt=F32):
        return nc.dram_tensor(name, list(shape), dt, kind="ExternalInput").ap()

    def dout(name, shape, dt=F32):
        return nc.dram_tensor(name, list(shape), dt, kind="ExternalOutput").ap()

    def dscr(name, shape, dt=F32):
        return nc.dram_tensor(name, list(shape), dt, kind="ExternalOutput" if debug else "Internal").ap()

    I = {}
    I["x_lat"] = din("x_lat", [NLAT, D]); I["x_ctx"] = din("x_ctx", [NCTX, D])
    I["cond2"] = din("cond2", [2, D])
    I["sd0"] = din("sd0", [DEPTH, 2, 4, 128, 128]); I["sh0"] = din("sh0", [DEPTH, 2, 4, 128, 128])
    I["ck"] = din("ck", [DEPTH, 256, 128]); I["cv"] = din("cv", [DEPTH, 256, 128])
    I["ada_w"] = din("ada_w", [DEPTH, D, 6 * D]); I["ada_b"] = din("ada_b", [DEPTH, 6 * D])
    I["norm1_w"] = din("norm1_w", [DEPTH, D]); I["w_in"] = din("w_in", [DEPTH, D, DIN])
    I["conv_a"] = din("conv_a", [DEPTH, 5, 1536]); I["a_log"] = din("a_log", [DEPTH, 8]); I["dt_bias"] = din("dt_bias", [DEPTH, 8])
    I["norm_a"] = din("norm_a", [DEPTH, 128]); I["lb_logits"] = din("lb_logits", [2, DEPTH, 512]); I["norm_b"] = din("norm_b", [DEPTH, 128])
    I["q_norm"] = din("q_norm", [DEPTH, 64]); I["k_norm"] = din("k_norm", [DEPTH, 64]); I["sink"] = din("sink", [DEPTH, 8])
    I["w_branch"] = din("w_branch", [DEPTH, 3, 512, D]); I["w_out"] = din("w_out", [DEPTH, D, D]); I["norm2_w"] = din("norm2_w", [DEPTH, D])
    I["w_up"] = din("w_up", [DEPTH, D, 2 * DFF]); I["conv_ffn"] = din("conv_ffn", [DEPTH, 3, 2 * DFF]); I["w_down"] = din("w_down", [DEPTH, DFF, D])
    I["consts"] = din("consts", list(carr.shape)); I["rcos"] = din("rcos", [NLAT, 32]); I["rsin"] = din("rsin", [NLAT, 32])
    O = {}
    O["y_lat"] = dout("y_lat", [NLAT, D]); O["y_ctx"] = dout("y_ctx", [NCTX, D])
    O["nsd"] = dout("nsd", [4, DEPTH, 2, 4, 128, 128]); O["nsh"] = dout("nsh", [4, DEPTH, 2, 4, 128, 128])
    O["nck"] = dout("nck", [4, DEPTH, 256, 128]); O["ncv"] = dout("ncv", [4, DEPTH, 256, 128])
    XT = dscr("XT", [D, NT + 2])
    QKVA = dscr("QKVA", [1536, NTP])
    QKVN = dscr("QKVN", [1536, NT], BF16)
    GA = dscr("GA", [512, NT], BF16); GB = dscr("GB", [512, NT], BF16)
    QB = dscr("QB", [512, NT], BF16); IB = dscr("IB", [512, NT], BF16)
    FFfull = dscr("FF", [1024, NT + 2])
    FF = FFfull[:, 0:NT]
    BG = dscr("BG", [16, NT])
    QCN = dscr("QCN", [64, 10, NT], BF16)
    VCN = dscr("VCN", [NT, 128], BF16)
    OA = QKVA[0:1024, 0:NT].rearrange("(d r) t -> d r t", d=2)
    OB = dscr("OB", [2, 512, NT])
    OC = dscr("OC", [64, 8, NT], BF16)
    DBG = {n: dout("dbg_" + n, s) for n, s in debug}

    XTv = XT.rearrange("(c p) t -> p c t", p=128)

    cst = k.sb("cst", [128, carr.shape[1]], F32)
    k.dma("sp", cst[:], I["consts"], writes=["cst"])

    def C(n):
        o, w = coff[n]
        return cst[:, o:o + w]
    identb = k.sb("identb", [128, 128], BF16); onesb = k.sb("onesb", [128, 128], BF16)
    k.op("dve", lambda e: e.tensor_copy(out=identb[:], in_=C("ident")), reads=["cst"], writes=["identb"])
    k.op("dve", lambda e: e.tensor_copy(out=onesb[:], in_=C("ones")), reads=["cst"], writes=["onesb"])
    PS = [k.ps(f"ps{i}", [128, 512]) for i in range(8)]
    psi = [0]

    def nps():
        i = psi[0] % 8
        psi[0] += 1
        return PS[i], f"ps{i}"

    evi = [0]

    def ev():
        evi[0] += 1
        return "dve" if evi[0] % 2 else "act"

    def copy(e, out, in_, r, w):
        if e == "act":
            k.op("act", lambda g: g.copy(out=out, in_=in_), reads=r, writes=w)
        else:
            k.op(e, lambda g: g.tensor_copy(out=out, in_=in_), reads=r, writes=w)

    def mm(ps_ap, lhsT, rhs, start, stop, r, w):
        k.op("pe", lambda e: e.matmul(ps_ap, lhsT=lhsT, rhs=rhs, start=start, stop=stop), reads=r, writes=w)

    def act(out, in_, func, r, w, bias=0.0, scale=1.0):
        k.op("act", lambda e: e.activation(out=out, in_=in_, func=func, bias=bias, scale=scale), reads=r, writes=w)

    def tt(e, out, in0, in1, op, r, w):
        k.op(e, lambda g: g.tensor_tensor(out=out, in0=in0, in1=in1, op=op), reads=r, writes=w)

    def ts(e, out, in0, s1, s2, op0, op1, r, w):
        if s2 is None:
            k.op(e, lambda g: g.tensor_scalar(out=out, in0=in0, scalar1=s1, scalar2=None, op0=op0), reads=r, writes=w)
        else:
            k.op(e, lambda g: g.tensor_scalar(out=out, in0=in0, scalar1=s1, scalar2=s2, op0=op0, op1=op1), reads=r, writes=w)

    def stt(e, out, in0, scalar, in1, op0, op1, r, w):
        k.op(e, lambda g: g.scalar_tensor_tensor(out=out, in0=in0, scalar=scalar, in1=in1, op0=op0, op1=op1), reads=r, writes=w)

    modT = k.sb("modT", [128, DEPTH, 2, 48], F32)
    gm1 = k.sb("gm1", [128, DEPTH, 2, 8], F32); gm2 = k.sb("gm2", [128, DEPTH, 2, 8], F32)
    with ExitStack() as ph:
        def sbp(name, shape, dt):
            return ph.enter_context(nc.sbuf_tensor(_un(name), list(shape), dt))
        cT = sbp("cT", [128, 8, 2], F32)
        for c in range(2):
            k.dma("sp", cT[:, :, c], I["cond2"][c].rearrange("(kc p) -> p kc", p=128), writes=["cT"], allow_slow_non_contiguous=True)
        act(cT[:], cT[:], AF.Silu, ["cT"], ["cT"])
        adab = sbp("adab", [128, DEPTH, 48], F32)
        nw = sbp("nw", [128, 2, DEPTH, 8], F32)
        for l in range(DEPTH):
            k.dma("sp", adab[:, l, :], I["ada_b"][l].rearrange("(c p) -> p c", p=128), writes=["adab"], allow_slow_non_contiguous=True)
            k.dma("sp", nw[:, 0, l, :], I["norm1_w"][l].rearrange("(c p) -> p c", p=128), writes=["nw"], allow_slow_non_contiguous=True)
            k.dma("sp", nw[:, 1, l, :], I["norm2_w"][l].rearrange("(c p) -> p c", p=128), writes=["nw"], allow_slow_non_contiguous=True)
        awb = [sbp(f"aw{i}", [128, 8, 768], F32) for i in range(2)]
        for l in range(DEPTH):
            for g in range(8):
                aw = awb[g % 2]; ak = f"aw{g % 2}"
                k.dma("sp" if g % 2 else "act", aw[:], I["ada_w"][l][:, g * 768:(g + 1) * 768].rearrange("(kc p) n -> p kc n", p=128), writes=[ak])
                pst, pk = nps()
                for j in range(6):
                    for kc in range(8):
                        mm(pst[:, 2 * j:2 * j + 2], aw[:, kc, j * 128:(j + 1) * 128], cT[:, kc, :], kc == 0, kc == 7, [ak, "cT"], [pk])
                for c in range(2):
                    tt("dve", modT[:, l, c, g * 6:(g + 1) * 6], pst[:, c:12:2], adab[:, l, g * 6:(g + 1) * 6], ALU.add, [pk, "adab"], ["modT"])
            for c in range(2):
                stt("dve", gm1[:, l, c, :], modT[:, l, c, 8:16], 1.0, nw[:, 0, l, :], ALU.add, ALU.mult, ["modT", "nw"], ["gm1"])
                stt("dve", gm2[:, l, c, :], modT[:, l, c, 32:40], 1.0, nw[:, 1, l, :], ALU.add, ALU.mult, ["modT", "nw"], ["gm2"])
        k.barrier()

    with ExitStack() as ph:
        def sbp(name, shape, dt):
            return ph.enter_context(nc.sbuf_tensor(_un(name), list(shape), dt))
        xin = [sbp(f"xin{i}", [128, D], F32) for i in range(2)]
        xo = [sbp(f"xo{i}", [128, 8, 128], F32) for i in range(2)]
        for j in range(NT // 128):
            t0 = j * 128
            src = I["x_lat"][t0:t0 + 128, :] if t0 < NLAT else I["x_ctx"][t0 - NLAT:t0 - NLAT + 128, :]
            b = j % 2
            k.dma("sp", xin[b][:], src, writes=[f"xin{b}"])
            for hf in range(2):
                pst, pk = nps()
                for q in range(4):
                    kc = hf * 4 + q
                    k.op("pe", lambda e: e.transpose(pst[:, q * 128:(q + 1) * 128], xin[b][:, kc * 128:(kc + 1) * 128], C("ident")),
                         reads=[f"xin{b}", "cst"], writes=[pk])
                copy(ev(), xo[b][:, hf * 4:hf * 4 + 4, :], pst[:].rearrange("p (c t) -> p c t", c=4), [pk], [f"xo{b}"])
            k.dma("act", XTv[:, :, 1 + t0:1 + t0 + 128], xo[b][:], reads=[f"xo{b}"], writes=["XT"])
        k.barrier()
    ctx = dict(nc=nc, k=k, I=I, O=O, C=C, nps=nps, ev=ev, copy=copy, mm=mm, act=act, tt=tt, ts=ts, stt=stt,
               identb=identb, onesb=onesb, modT=modT, gm1=gm1, gm2=gm2, XT=XT, XTv=XTv, QKVA=QKVA, QKVN=QKVN, GA=GA, GB=GB,
               QB=QB, IB=IB, FF=FF, FFfull=FFfull, BG=BG, QCN=QCN, VCN=VCN, OA=OA, OB=OB, OC=OC, DBG=DBG)
    for l in range(DEPTH):
        if stage >= 1:
            phase_proj(ctx, l)
        if stage >= 2:
            phase_aprep(ctx, l)
            with ExitStack() as ph_:
                ctx["ph"] = ph_
                gens = [("a0:", phase_delta(ctx, l, 0)), ("a1:", phase_delta(ctx, l, 1))]
                if stage >= 3:
                    gens += [("b0:", phase_hgrn(ctx, l, 0)), ("b1:", phase_hgrn(ctx, l, 1))]
                alive = list(gens)
                while alive:
                    for g in list(alive):
                        k.prefix = g[0]
                        try:
                            next(g[1])
                        except StopIteration:
                            alive.remove(g)
                k.prefix = ""
                k.barrier()
        if stage >= 4:
            phase_attn(ctx, l)
        if stage >= 5:
            phase_merge(ctx, l)
        if stage >= 6:
            phase_ffn(ctx, l)
        if stage < 99:
            break
    phase_out(ctx, stage)
    k.finish([])
    build_program.last_k = k
    return nc


def _norm_h(X, l, which, xT, hT, n, cond, tag):
    k = X["k"]; mm = X["mm"]; act = X["act"]; tt = X["tt"]; ts = X["ts"]
    gm = X["gm1"] if which == 1 else X["gm2"]
    shc = 0 if which == 1 else 24
    act(hT[:, :, :n], xT[:, :, :n], AF.Square, [tag + "xT"], [tag + "hT"])
    pst, pk = X["nps"]()
    for kc in range(8):
        mm(pst[:, :n], X["onesb"][:], hT[:, kc, :n], kc == 0, kc == 7, ["onesb", tag + "hT"], [pk])
    rstd = X[tag + "rstd"]
    act(rstd[:, :n], pst[:, :n], AF.Sqrt, [pk], [tag + "rstd"], bias=EPS, scale=1.0 / D)
    k.op("dve", lambda e: e.reciprocal(out=rstd[:, :n], in_=rstd[:, :n]), reads=[tag + "rstd"], writes=[tag + "rstd"])
    tt("dve", xT[:, :, :n], xT[:, :, :n], rstd[:, :n].unsqueeze(1).to_broadcast([128, 8, n]), ALU.mult, [tag + "xT", tag + "rstd"], [tag + "xT"])
    for kc in range(8):
        ts("pool" if kc % 2 else "dve", hT[:, kc, :n], xT[:, kc, :n], gm[:, l, cond, kc:kc + 1], X["modT"][:, l, cond, shc + kc:shc + kc + 1],
           ALU.mult, ALU.add, [tag + "xT", "gm1", "gm2", "modT"], [tag + "hT"])


def phase_proj(X, l):
    nc = X["nc"]; k = X["k"]; I = X["I"]; O = X["O"]; C = X["C"]
    mm = X["mm"]; act = X["act"]; tt = X["tt"]; ts = X["ts"]; stt = X["stt"]; copy = X["copy"]; nps = X["nps"]; ev = X["ev"]
    with ExitStack() as ph:
        def sbp(name, shape, dt):
            return ph.enter_context(nc.sbuf_tensor(_un(name), list(shape), dt))
        W = sbp("p1w", [128, 8, C_MG], BF16)
        for kc in range(8):
            k.dma("pool", W[:, kc, :], I["w_in"][l][kc * 128:(kc + 1) * 128, 0:C_MG], writes=["p1w"])
        xT = sbp("p1xT", [128, 8, 512], F32); hT = sbp("p1hT", [128, 8, 512], BF16); rstd = sbp("p1rstd", [128, 512], F32)
        X["p1rstd"] = rstd
        sqkv = sbp("sqkv", [128, 12, 512], F32); sff = sbp("sff", [128, 8, 512], F32)
        sga = sbp("sga", [128, 4, 512], BF16); sgb = sbp("sgb", [128, 4, 512], BF16)
        sqb = sbp("sqb", [128, 4, 512], BF16); sib = sbp("sib", [128, 4, 512], BF16)
        sbg = sbp("sbg", [16, 512], F32); tb1 = sbp("tb1", [16, 512], F32); tb2 = sbp("tb2", [16, 512], F32)
        bcs = sbp("bcs", [16, 4], F32)
        k.op("dve", lambda e: e.memset(bcs[:], 0.0), writes=["bcs"])
        k.op("dve", lambda e: e.memset(bcs[0:8, 1:2], 1.0), writes=["bcs"])
        k.op("dve", lambda e: e.memset(bcs[0:8, 2:3], -1.0), writes=["bcs"])
        alg = sbp("alg", [16, 1], F32)
        k.op("dve", lambda e: e.memset(alg[:], 0.0), writes=["alg"])
        k.dma("sp", bcs[8:16, 0:1], I["dt_bias"][l].rearrange("(p o) -> p o", o=1), reads=[], writes=["bcs"], allow_slow_non_contiguous=True)
        k.dma("sp", alg[8:16, 0:1], I["a_log"][l].rearrange("(p o) -> p o", o=1), writes=["alg"], allow_slow_non_contiguous=True)
        act(alg[:], alg[:], AF.Exp, ["alg"], ["alg"])
        stt("dve", bcs[:, 3:4], bcs[:, 1:2], -1.0, alg[:], ALU.add, ALU.mult, ["bcs", "alg"], ["bcs"])
        lbt = sbp("lbt", [128, 8], F32); oml = sbp("oml", [128, 8], F32)
        if l == 0:
            k.op("dve", lambda e: e.memset(lbt[:], 0.0), writes=["lbt"])
        else:
            l0 = sbp("l0", [128, 8], F32); l1 = sbp("l1", [128, 8], F32)
            for d in range(2):
                k.dma("sp", l0[:, d * 4:d * 4 + 4], I["lb_logits"][d, 0].rearrange("(h p) -> p h", p=128), writes=["l0"], allow_slow_non_contiguous=True)
                k.dma("sp", l1[:, d * 4:d * 4 + 4], I["lb_logits"][d, 1].rearrange("(h p) -> p h", p=128), writes=["l1"], allow_slow_non_contiguous=True)
            tt("dve", l1[:], l1[:], l0[:], ALU.subtract, ["l0", "l1"], ["l1"])
            act(lbt[:], l1[:], AF.Sigmoid, ["l1"], ["lbt"])
            ts("dve", lbt[:], lbt[:], 1e-6, 1.0 - 1e-6, ALU.max, ALU.min, ["lbt"], ["lbt"])
        ts("dve", oml[:], lbt[:], -1.0, 1.0, ALU.mult, ALU.add, ["lbt"], ["oml"])
        nqk = sbp("nqk", [128, 10, 64], F32)
        for hh in range(10):
            src = I["q_norm"][l:l + 1, :] if hh < 8 else I["k_norm"][l:l + 1, :]
            k.dma("sp", nqk[:, hh, :], src.to_broadcast([128, 64]), writes=["nqk"])
        qk = sbp("qk", [128, 10, 64], F32); qk2 = sbp("qk2", [128, 10, 64], F32); qss = sbp("qss", [128, 10], F32)
        qr = sbp("qr", [128, 10, 64], F32); rtmp = sbp("rtmp", [128, 10, 2, 16], F32)
        v32 = sbp("v32", [128, 128], F32); rc = sbp("rc", [128, 32], F32); rs_ = sbp("rs_", [128, 32], F32)
        qkT = sbp("qkT", [64, 10, 128], BF16)

        for (t0, n, cond, si) in TILES:
            k.dma("sp", xT[:, :, :n], X["XTv"][:, :, 1 + t0:1 + t0 + n], reads=["XT"], writes=["p1xT"])
            _norm_h(X, l, 1, xT, hT, n, cond, "p1")
            pc0 = padcol(si, t0)

            def proj(c0, m):
                pst, pk = nps()
                for kc in range(8):
                    mm(pst[:m, :n], W[:, kc, c0:c0 + m], hT[:, kc, :n], kc == 0, kc == 7, ["p1w", "p1hT"], [pk])
                return pst, pk
            for c in range(12):
                pst, pk = proj(C_QA + c * 128, 128)
                copy(ev(), sqkv[:, c, :n], pst[:, :n], [pk], ["sqkv"])
            k.dma("sp", X["QKVA"].rearrange("(c p) t -> p c t", p=128)[:, :, pc0:pc0 + n], sqkv[:, :, :n], reads=["sqkv"], writes=["QKVA"])
            for (c0, st, dst, key) in ((C_GA, sga, X["GA"], "GA"), (C_QB, sqb, X["QB"], "QB"), (C_GB, sgb, X["GB"], "GB")):
                for c in range(4):
                    pst, pk = proj(c0 + c * 128, 128)
                    act(st[:, c, :n], pst[:, :n], AF.Silu, [pk], ["s" + key])
                k.dma("act", dst.rearrange("(c p) t -> p c t", p=128)[:, :, t0:t0 + n], st[:, :, :n], reads=["s" + key], writes=[key])
            for c in range(4):
                pst, pk = proj(C_IB + c * 128, 128)
                copy(ev(), sib[:, c, :n], pst[:, :n], [pk], ["sIB"])
            k.dma("act", X["IB"].rearrange("(c p) t -> p c t", p=128)[:, :, t0:t0 + n], sib[:, :, :n], reads=["sIB"], writes=["IB"])
            for c in range(8):
                pst, pk = proj(C_FB + c * 128, 128)
                act(sff[:, c, :n], pst[:, :n], AF.Sigmoid, [pk], ["sff"])
                ts("pool", sff[:, c, :n], sff[:, c, :n], oml[:, c:c + 1], lbt[:, c:c + 1], ALU.mult, ALU.add, ["sff", "oml", "lbt"], ["sff"])
            k.dma("sp", X["FF"].rearrange("(c p) t -> p c t", p=128)[:, :, t0:t0 + n], sff[:, :, :n], reads=["sff"], writes=["FF"])
            pst, pk = proj(C_BETA, 16)
            act(tb1[:, :n], pst[:16, :n], AF.Exp, [pk, "bcs"], ["tb1"], bias=bcs[:, 0:1], scale=1.0)
            ts("dve", tb1[:, :n], tb1[:, :n], 1.0, None, ALU.add, None, ["tb1"], ["tb1"])
            act(tb2[:, :n], tb1[:, :n], AF.Ln, ["tb1"], ["tb2"])
            k.op("dve", lambda e: e.reciprocal(out=tb1[:, :n], in_=tb1[:, :n]), reads=["tb1"], writes=["tb1"])
            ts("dve", tb1[:, :n], tb1[:, :n], bcs[:, 2:3], bcs[:, 1:2], ALU.mult, ALU.add, ["tb1", "bcs"], ["tb1"])
            stt("dve", sbg[:, :n], tb2[:, :n], bcs[:, 3:4], tb1[:, :n], ALU.mult, ALU.add, ["tb1", "tb2", "bcs"], ["sbg"])
            k.dma("sp", X["BG"][:, t0:t0 + n], sbg[:, :n], reads=["sbg"], writes=["BG"])
            for j in range(n // 128):
                ta = t0 + j * 128
                p1, k1 = nps(); p2, k2 = nps()
                for kc in range(8):
                    mm(p1[:, :], hT[:, kc, j * 128:(j + 1) * 128], W[:, kc, C_QC:C_QC + 512], kc == 0, kc == 7, ["p1w", "p1hT"], [k1])
                for kc in range(8):
                    mm(p2[:, :256], hT[:, kc, j * 128:(j + 1) * 128], W[:, kc, C_KC:C_KC + 256], kc == 0, kc == 7, ["p1w", "p1hT"], [k2])
                copy("dve", qk[:, 0:8, :], p1[:].rearrange("p (h d) -> p h d", h=8), [k1], ["qk"])
                copy("act", qk[:, 8:10, :], p2[:, 0:128].rearrange("p (h d) -> p h d", h=2), [k2], ["qk"])
                copy("act", v32[:], p2[:, 128:256], [k2], ["v32"])
                k.dma("pool", X["VCN"][ta:ta + 128, :], v32[:], reads=["v32"], writes=["VCN"])
                if cond == 1:
                    k.dma("act", O["ncv"][si - 1, l, ta - t0:ta - t0 + 128, :], v32[:], reads=["v32"], writes=["ncv"])
                tt("dve", qk2[:], qk[:], qk[:], ALU.mult, ["qk"], ["qk2"])
                k.op("dve", lambda e: e.tensor_reduce(out=qss[:], in_=qk2[:], axis=AX.X, op=ALU.add), reads=["qk2"], writes=["qss"])
                act(qss[:], qss[:], AF.Sqrt, ["qss"], ["qss"], bias=EPS, scale=1.0 / 64)
                k.op("dve", lambda e: e.reciprocal(out=qss[:], in_=qss[:]), reads=["qss"], writes=["qss"])
                tt("dve", qk[:], qk[:], qss[:].unsqueeze(2).to_broadcast([128, 10, 64]), ALU.mult, ["qk", "qss"], ["qk"])
                tt("pool", qk[:], qk[:], nqk[:], ALU.mult, ["qk", "nqk"], ["qk"])
                src = qk
                if cond == 1:
                    k.dma("act", O["nck"][si - 1, l, ta - t0:ta - t0 + 128, :], qk[:, 8:10, :].rearrange("p h d -> p (h d)"), reads=["qk"], writes=["nck"])
                else:
                    k.dma("sp", rc[:], I["rcos"][ta:ta + 128, :], writes=["rc"])
                    k.dma("sp", rs_[:], I["rsin"][ta:ta + 128, :], writes=["rs_"])
                    xv = qk[:].rearrange("p h (a b f) -> p h a b f", a=2, b=2)
                    ov = qr[:].rearrange("p h (a b f) -> p h a b f", a=2, b=2)
                    cb = rc[:].rearrange("p (a f) -> p a f", a=2).unsqueeze(1).to_broadcast([128, 10, 2, 16])
                    sb_ = rs_[:].rearrange("p (a f) -> p a f", a=2).unsqueeze(1).to_broadcast([128, 10, 2, 16])
                    tt("dve", ov[:, :, :, 0, :], xv[:, :, :, 0, :], cb, ALU.mult, ["qk", "rc"], ["qr"])
                    tt("pool", rtmp[:], xv[:, :, :, 1, :], sb_, ALU.mult, ["qk", "rs_"], ["rtmp"])
                    tt("dve", ov[:, :, :, 0, :], ov[:, :, :, 0, :], rtmp[:], ALU.subtract, ["qr", "rtmp"], ["qr"])
                    tt("dve", ov[:, :, :, 1, :], xv[:, :, :, 1, :], cb, ALU.mult, ["qk", "rc"], ["qr"])
                    tt("pool", rtmp[:], xv[:, :, :, 0, :], sb_, ALU.mult, ["qk", "rs_", "qr"], ["rtmp"])
                    tt("dve", ov[:, :, :, 1, :], ov[:, :, :, 1, :], rtmp[:], ALU.add, ["qr", "rtmp"], ["qr"])
                    src = qr
                skey = "qk" if src is qk else "qr"
                for (h0, h1) in ((0, 4), (4, 8), (8, 10)):
                    pst, pk = nps()
                    for hh in range(h0, h1):
                        k.op("pe", lambda e: e.transpose(pst[:64, (hh - h0) * 128:(hh - h0 + 1) * 128], src[:, hh, :], C("ident")),
                             reads=[skey, "cst"], writes=[pk])
                    copy(ev(), qkT[:, h0:h1, :], pst[:64, :(h1 - h0) * 128].rearrange("p (h t) -> p h t", h=h1 - h0), [pk], ["qkT"])
                k.dma("sp", X["QCN"][:, :, ta:ta + 128], qkT[:], reads=["qkT"], writes=["QCN"])
        k.barrier()


def phase_out(X, stage):
    nc = X["nc"]; k = X["k"]; O = X["O"]; C = X["C"]; nps = X["nps"]; ev = X["ev"]; copy = X["copy"]
    with ExitStack() as ph:
        def sbp(name, shape, dt):
            return ph.enter_context(nc.sbuf_tensor(_un(name), list(shape), dt))
        xi = [sbp(f"oxi{i}", [128, 8, 128], F32) for i in range(2)]
        xo = [sbp(f"oxo{i}", [128, D], F32) for i in range(2)]
        for j in range(NT // 128):
            t0 = j * 128; b = j % 2
            k.dma("sp", xi[b][:], X["XTv"][:, :, 1 + t0:1 + t0 + 128], reads=["XT"], writes=[f"oxi{b}"])
            for hf in range(2):
                pst, pk = nps()
                for q in range(4):
                    kc = hf * 4 + q
                    k.op("pe", lambda e: e.transpose(pst[:, q * 128:(q + 1) * 128], xi[b][:, kc, :], C("ident")), reads=[f"oxi{b}", "cst"], writes=[pk])
                copy(ev(), xo[b][:, hf * 512:(hf + 1) * 512], pst[:], [pk], [f"oxo{b}"])
            dst = O["y_lat"][t0:t0 + 128, :] if t0 < NLAT else O["y_ctx"][t0 - NLAT:t0 - NLAT + 128, :]
            k.dma("act", dst, xo[b][:], reads=[f"oxo{b}"], writes=["y"])
        k.barrier()


def phase_aprep(X, l):
    nc = X["nc"]; k = X["k"]; I = X["I"]
    mm = X["mm"]; act = X["act"]; tt = X["tt"]; ts = X["ts"]; stt = X["stt"]; nps = X["nps"]
    with ExitStack() as ph:
        def sbp(name, shape, dt):
            return ph.enter_context(nc.sbuf_tensor(_un(name), list(shape), dt))
        cw = sbp("cw", [128, 12, 5], F32)
        for j in range(5):
            k.dma("sp", cw[:, :, j], I["conv_a"][l, j].rearrange("(c p) -> p c", p=128), writes=["cw"], allow_slow_non_contiguous=True)
        z = sbp("apz", [128, 12, 2], F32)
        k.op("dve", lambda e: e.memset(z[:], 0.0), writes=["apz"])
        Qv = X["QKVA"].rearrange("(c p) t -> p c t", p=128)
        if True:
            for si, (s0, ln, _) in enumerate(SEQS):
                a = padcol(si, s0)
                k.dma("sp", Qv[:, :, a - 2:a], z[:], reads=["apz"], writes=["QKVApad"])
                k.dma("sp", Qv[:, :, a + ln:a + ln + 2], z[:], reads=["apz"], writes=["QKVApad"])
        xp = [sbp(f"apx{i}", [128, 12, 516], F32) for i in range(2)]
        acc = sbp("apacc", [128, 12, 512], F32); sq = sbp("apsq", [128, 8, 512], BF16); rn = sbp("aprn", [128, 512], F32)
        ob = sbp("apob", [128, 12, 512], BF16)
        for ti, (t0, n, cond, si) in enumerate(TILES):
            b = ti % 2
            pc0 = padcol(si, t0)
            k.dma("sp", xp[b][:, :, :n + 4], Qv[:, :, pc0 - 2:pc0 + n + 2], reads=["QKVA", "QKVApad"], writes=[f"apx{b}"])
            for c in range(12):
                ts("dve" if c % 3 else "pool", acc[:, c, :n], xp[b][:, c, 0:n], cw[:, c, 0:1], None, ALU.mult, None, [f"apx{b}", "cw"], [f"acc{c}"])
            for j in range(1, 5):
                for c in range(12):
                    stt("dve", acc[:, c, :n], xp[b][:, c, j:j + n], cw[:, c, j:j + 1], acc[:, c, :n], ALU.mult, ALU.add, [f"apx{b}", "cw", f"acc{c}"], [f"acc{c}"])
            acck = [f"acc{c}" for c in range(12)]
            act(acc[:, :, :n], acc[:, :, :n], AF.Silu, acck, acck)
            act(sq[:, :, :n], acc[:, 0:8, :n], AF.Square, acck, ["apsq"])
            for c in range(8):
                pst, pk = nps()
                mm(pst[:, :n], X["onesb"][:], sq[:, c, :n], True, True, ["onesb", "apsq"], [pk])
                act(rn[:, :n], pst[:, :n], AF.Sqrt, [pk], ["aprn"], bias=EPS, scale=1.0)
                k.op("dve", lambda e: e.reciprocal(out=rn[:, :n], in_=rn[:, :n]), reads=["aprn"], writes=["aprn"])
                stt("dve", ob[:, c, :n], acc[:, c, :n], (128.0 ** -0.5) if c < 4 else 1.0, rn[:, :n], ALU.mult, ALU.mult, acck + ["aprn"], ["apob"])
            X["copy"]("pool", ob[:, 8:12, :n], acc[:, 8:12, :n], acck, ["apob"])
            k.dma("act", X["QKVN"].rearrange("(c p) t -> p c t", p=128)[:, :, t0:t0 + n], ob[:, :, :n], reads=["apob"], writes=["QKVN"])
        k.barrier()


def phase_delta(X, l, d):
    nc = X["nc"]; k = X["k"]; I = X["I"]; O = X["O"]; C = X["C"]
    mm = X["mm"]; act = X["act"]; tt = X["tt"]; ts = X["ts"]; stt = X["stt"]; nps = X["nps"]; ev = X["ev"]; copy = X["copy"]
    identb = X["identb"]
    ph = X["ph"]
    if True:
        def sbp(name, shape, dt):
            return ph.enter_context(nc.sbuf_tensor(_un(name), list(shape), dt))
        S = sbp("dS", [128, 4, 128], F32); Sb = sbp("dSb", [128, 4, 128], BF16)
        negm = sbp("negm", [128, 4, 128], F32)
        copy("dve", negm[:], C(f"neg{d}").unsqueeze(1).to_broadcast([128, 4, 128]), ["cst"], ["negm"])
        qkv = [sbp(f"dqkv{i}", [128, 12, 128], BF16) for i in range(2)]
        bgT = [sbp(f"dbgT{i}", [16, 128], F32) for i in range(2)]
        ktok = sbp("dktok", [128, 4, 128], BF16); vtok = sbp("dvtok", [128, 4, 128], BF16)
        bgt = sbp("dbgt", [128, 16], F32); gct = sbp("dgct", [128, 8], F32); t12 = sbp("dt12", [128, 12], F32); e12 = sbp("de12", [128, 12], F32)
        bw = sbp("dbw", [128, 4], F32); nbeta = sbp("dnbeta", [128, 4], F32); ngc = sbp("dngc", [128, 4], F32)
        vb = sbp("dvb", [128, 4, 128], BF16); kbg = sbp("dkbg", [128, 4, 128], BF16); kd = sbp("dkd", [128, 4, 128], BF16)
        dg = sbp("ddg", [128, 2, 4, 128], F32); qg = sbp("dqg", [128, 4, 128], BF16)
        decs = sbp("ddecs", [128, 4, 128], BF16); P = sbp("dP", [128, 4, 128], BF16); Q = sbp("dQ", [128, 4, 128], BF16)
        qkm = sbp("dqkm", [128, 4, 128], BF16); qkmT = sbp("dqkmT", [128, 4, 128], BF16); R = sbp("dR", [128, 4, 128], BF16)
        nwT = sbp("dnwT", [128, 4, 128], BF16); vnew = sbp("dvnew", [128, 4, 128], BF16)
        oT = [sbp(f"doT{i}", [128, 4, 128], F32) for i in range(2)]
        Pk = sbp("dPk", [128, 4, 128], BF16); Qk = sbp("dQk", [128, 4, 128], BF16); Dm = sbp("dDm", [128, 4, 128], BF16)
        Xu = sbp("dXu", [128, 4, 128], BF16); Xd = sbp("dXd", [128, 4, 128], BF16)
        mk = {}
        for nm in ("b16", "l1", "l2", "l3"):
            mk[nm] = sbp("dmk" + nm, [128, 4, 128], BF16)
            copy("dve", mk[nm][:], C(nm).unsqueeze(1).to_broadcast([128, 4, 128]), ["cst"], ["dmk"])
        tri = C(f"tri{d}")
        identB4 = identb[:].unsqueeze(1).to_broadcast([128, 4, 128])
        identF4 = C("ident").unsqueeze(1).to_broadcast([128, 4, 128])

        def v4(ps):
            return ps[:].rearrange("p (h t) -> p h t", h=4)

        def vb4(ps):
            return ps[:].bitcast(BF16)[:, 0:512].rearrange("p (h t) -> p h t", h=4)

        ci = 0
        for si, (s0, ln, cond) in enumerate(SEQS):
            if cond == 0:
                k.dma("sp", S[:], I["sd0"][l, d].rearrange("h k v -> k h v"), writes=["dS"])
            else:
                k.op("dve", lambda e: e.memset(S[:], 0.0), writes=["dS"])
            copy("dve", Sb[:], S[:], ["dS"], ["dSb"])
            chunks = list(range(ln // 128))
            if d == 1:
                chunks = chunks[::-1]
            for cj in chunks:
                ta = s0 + cj * 128
                b = ci % 2; ci += 1
                qk_ = f"dqkv{b}"; bk_ = f"dbgT{b}"
                k.dma("sp", qkv[b][:], X["QKVN"].rearrange("(c p) t -> p c t", p=128)[:, :, ta:ta + 128], reads=["QKVN"], writes=[qk_])
                k.dma("sp", bgT[b][:], X["BG"][:, ta:ta + 128], reads=["BG"], writes=[bk_])
                qT = qkv[b][:, 0:4, :]; kT = qkv[b][:, 4:8, :]; vT = qkv[b][:, 8:12, :]
                for (srcT, dst, dk_) in ((kT, ktok, "dktok"), (vT, vtok, "dvtok")):
                    pst, pk = nps()
                    pb = vb4(pst)
                    for h in range(4):
                        k.op("pe", lambda e: e.transpose(pb[:, h, :], srcT[:, h, :], identb[:]), reads=[qk_, "identb"], writes=[pk])
                    copy(ev(), dst[:], pb, [pk], [dk_])
                pst, pk = nps()
                mm(pst[:, 0:16], bgT[b][:], C("ident")[0:16, 0:16], True, True, [bk_, "cst"], [pk])
                copy("dve", bgt[:], pst[:, 0:16], [pk], ["dbgt"])
                pst, pk = nps()
                gsl = bgt[:, 8 + 4 * d:12 + 4 * d]
                mm(pst[:, 0:4], tri, gsl, True, True, ["cst", "dbgt"], [pk])
                mm(pst[:, 4:8], C("ones"), gsl, True, True, ["cst", "dbgt"], [pk])
                copy("dve", gct[:], pst[:, 0:8], [pk], ["dgct"])
                copy("dve", t12[:, 0:4], gct[:, 0:4], ["dgct"], ["dt12"])
                tt("dve", t12[:, 4:8], gct[:, 4:8], gct[:, 0:4], ALU.subtract, ["dgct"], ["dt12"])
                copy("dve", t12[:, 8:12], gct[:, 4:8], ["dgct"], ["dt12"])
                act(e12[:], t12[:], AF.Exp, ["dt12"], ["de12"])
                beta = bgt[:, 4 * d:4 * d + 4]
                tt("dve", bw[:], beta, e12[:, 0:4], ALU.mult, ["dbgt", "de12"], ["dbw"])
                ts("dve", nbeta[:], beta, -1.0, None, ALU.mult, None, ["dbgt"], ["dnbeta"])

                def bc(ap):
                    return ap.unsqueeze(2).to_broadcast([128, 4, 128])
                tt("pool", vb[:], vtok[:], bc(beta), ALU.mult, ["dvtok", "dbgt"], ["dvb"])
                tt("pool", kbg[:], ktok[:], bc(bw[:]), ALU.mult, ["dktok", "dbw"], ["dkbg"])
                tt("pool", kd[:], ktok[:], bc(e12[:, 4:8]), ALU.mult, ["dktok", "de12"], ["dkd"])
                tt("dve", dg[:, 0], identF4, bc(e12[:, 0:4]), ALU.mult, ["cst", "de12"], ["ddg0"])
                tt("dve", dg[:, 1], identF4, bc(gct[:, 0:4]), ALU.mult, ["cst", "dgct"], ["ddg1"])
                pst, pk = nps()
                mm(pst[:], C("ones"), dg[:, 0].rearrange("p h t -> p (h t)"), True, True, ["cst", "ddg0"], [pk])
                tt("dve", qg[:], qT, v4(pst), ALU.mult, [qk_, pk], ["dqg"])
                pst, pk = nps()
                mm(pst[:], C("ones"), dg[:, 1].rearrange("p h t -> p (h t)"), True, False, ["cst", "ddg1"], [pk])
                mm(pst[:], C("ident"), negm[:].rearrange("p h t -> p (h t)"), False, True, ["cst", "negm"], [pk])
                for h in range(4):
                    act(decs[:, h, :], pst[:, h * 128:(h + 1) * 128], AF.Exp, [pk, "dgct"], ["ddecs"], bias=gct[:, h:h + 1], scale=-1.0)
                pkk, kkk = nps()
                for h in range(4):
                    mm(pkk[:, h * 128:(h + 1) * 128], kT[:, h, :], kT[:, h, :], True, True, [qk_], [kkk])
                for h in range(4):
                    stt("dve", P[:, h, :], pkk[:, h * 128:(h + 1) * 128], nbeta[:, h:h + 1], decs[:, h, :], ALU.mult, ALU.mult, [kkk, "dnbeta", "ddecs"], ["dP"])
                pqk, kqk = nps()
                for h in range(4):
                    mm(pqk[:, h * 128:(h + 1) * 128], qT[:, h, :], kT[:, h, :], True, True, [qk_], [kqk])
                tt("pool", decs[:], decs[:], identB4, ALU.add, ["ddecs", "identb"], ["ddecs"])
                tt("dve", qkm[:], v4(pqk), decs[:], ALU.mult, [kqk, "ddecs"], ["dqkm"])
                for (srcm, dst, sk_, dk_) in ((P, Q, "dP", "dQ"), (qkm, qkmT, "dqkm", "dqkmT")):
                    pst, pk = nps()
                    pb = vb4(pst)
                    for h in range(4):
                        k.op("pe", lambda e: e.transpose(pb[:, h, :], srcm[:, h, :], identb[:]), reads=[sk_, "identb"], writes=[pk])
                    copy(ev(), dst[:], pb, [pk], [dk_])
                def mm4(A, B, ra, rb):
                    ps_, pk_ = nps()
                    for h in range(4):
                        mm(ps_[:, h * 128:(h + 1) * 128], A[:, h, :], B[:, h, :], True, True, [ra, rb], [pk_])
                    return ps_, pk_
                tt("pool", Pk[:], P[:], mk["b16"][:], ALU.mult, ["dP", "dmk"], ["dPk"])
                tt("pool", Qk[:], Q[:], mk["b16"][:], ALU.mult, ["dQ", "dmk"], ["dQk"])
                tt("dve", Dm[:], Pk[:], identB4, ALU.add, ["dPk", "identb"], ["dDm"])
                tt("dve", R[:], Qk[:], identB4, ALU.add, ["dQk", "identb"], ["dR"])
                for lev in range(3):
                    pp, kp = mm4(Qk, Pk, "dQk", "dPk")
                    pq, kq = mm4(Pk, Qk, "dPk", "dQk")
                    copy("act", Pk[:], v4(pp), [kp], ["dPk"])
                    copy("dve", Qk[:], v4(pq), [kq], ["dQk"])
                    pd, kd_ = mm4(Qk, Dm, "dQk", "dDm")
                    pu, ku = mm4(Pk, R, "dPk", "dR")
                    tt("dve", Dm[:], Dm[:], v4(pd), ALU.add, ["dDm", kd_], ["dDm"])
                    tt("dve", R[:], R[:], v4(pu), ALU.add, ["dR", ku], ["dR"])
                for li, ln_ in enumerate(("l1", "l2", "l3")):
                    last = li == 2
                    tt("pool", Pk[:], P[:], mk[ln_][:], ALU.mult, ["dP", "dmk"], ["dPk"])
                    px, kx = mm4(Pk, R, "dPk", "dR")
                    copy("act", Xu[:], v4(px), [kx], ["dXu"])
                    if not last:
                        tt("pool", Qk[:], Q[:], mk[ln_][:], ALU.mult, ["dQ", "dmk"], ["dQk"])
                        pxd, kxd = mm4(Qk, Dm, "dQk", "dDm")
                        copy("dve", Xd[:], v4(pxd), [kxd], ["dXd"])
                    py, ky = mm4(Dm, Xu, "dDm", "dXu")
                    if not last:
                        pyd, kyd = mm4(R, Xd, "dR", "dXd")
                    tt("dve", R[:], R[:], v4(py), ALU.add, ["dR", ky], ["dR"])
                    if not last:
                        tt("dve", Dm[:], Dm[:], v4(pyd), ALU.add, ["dDm", kyd], ["dDm"])
                pst, pk = nps()
                for h in range(4):
                    mm(pst[:, h * 128:(h + 1) * 128], kbg[:, h, :], R[:, h, :], True, True, ["dkbg", "dR"], [pk])
                ts("dve", nwT[:], v4(pst), -1.0, None, ALU.mult, None, [pk], ["dnwT"])
                pst, pk = nps()
                for h in range(4):
                    mm(pst[:, h * 128:(h + 1) * 128], R[:, h, :], vb[:, h, :], True, False, ["dR", "dvb"], [pk])
                    mm(pst[:, h * 128:(h + 1) * 128], nwT[:, h, :], Sb[:, h, :], False, True, ["dnwT", "dSb"], [pk])
                copy("act", vnew[:], v4(pst), [pk], ["dvnew"])
                pst, pk = nps()
                for h in range(4):
                    mm(pst[:, h * 128:(h + 1) * 128], Sb[:, h, :], qg[:, h, :], True, False, ["dSb", "dqg"], [pk])
                    mm(pst[:, h * 128:(h + 1) * 128], vnew[:, h, :], qkmT[:, h, :], False, True, ["dvnew", "dqkmT"], [pk])
                ob_ = ci % 2
                copy("act", oT[ob_][:], v4(pst), [pk], [f"doT{ob_}"])
                k.dma("act", X["OA"][d].rearrange("(h p) t -> p h t", p=128)[:, :, ta:ta + 128], oT[ob_][:], reads=[f"doT{ob_}"], writes=["OA"])
                pst, pk = nps()
                for h in range(4):
                    mm(pst[:, h * 128:(h + 1) * 128], kd[:, h, :], vnew[:, h, :], True, True, ["dkd", "dvnew"], [pk])
                tt("dve", S[:], S[:], bc(e12[:, 8:12]), ALU.mult, ["dS", "de12"], ["dS"])
                tt("dve", S[:], S[:], v4(pst), ALU.add, ["dS", pk], ["dS"])
                copy("dve", Sb[:], S[:], ["dS"], ["dSb"])
                yield
            if cond == 1:
                k.dma("sp", O["nsd"][si - 1, l, d].rearrange("h k v -> k h v"), S[:], reads=["dS"], writes=["nsd"])
        yield


def phase_hgrn(X, l, d):
    nc = X["nc"]; k = X["k"]; I = X["I"]; O = X["O"]; C = X["C"]
    mm = X["mm"]; act = X["act"]; tt = X["tt"]; ts = X["ts"]; stt = X["stt"]; nps = X["nps"]; ev = X["ev"]; copy = X["copy"]
    identb = X["identb"]
    ph = X["ph"]
    if True:
        def sbp(name, shape, dt):
            return ph.enter_context(nc.sbuf_tensor(_un(name), list(shape), dt))
        S = sbp("hS", [128, 4, 128], F32); Sb = sbp("hSb", [128, 4, 128], BF16)
        hm = sbp("hhm", [128, 4, 128], BF16)
        copy("dve", hm[:], C(f"hm{d}").unsqueeze(1).to_broadcast([128, 4, 128]), ["cst"], ["hhm"])
        qb = [sbp(f"hqb{i}", [128, 4, 128], BF16) for i in range(2)]
        ib = [sbp(f"hib{i}", [128, 4, 128], BF16) for i in range(2)]
        ff = [sbp(f"hff{i}", [128, 4, 128], F32) for i in range(2)]
        lf = sbp("hlf", [128, 512], F32); kf = sbp("hkf", [128, 512], F32); bb = sbp("hbb", [128, 512], F32); tmp = sbp("htmp", [128, 512], F32)
        bl = sbp("hbl", [128, 16], F32); ebl = sbp("hebl", [128, 16], F32)
        eb = sbp("heb", [128, 512], F32); enb = sbp("henb", [128, 512], F32); ekd = sbp("hekd", [128, 512], F32)
        qe = sbp("hqe", [128, 4, 128], BF16); ke = sbp("hke", [128, 4, 128], BF16); kdT = sbp("hkdT", [128, 4, 128], BF16)
        kdt = sbp("hkdt", [128, 4, 128], BF16); vtok = sbp("hvtok", [128, 4, 128], BF16); attm = sbp("hattm", [128, 4, 128], BF16); kd3 = sbp("hkd3", [128, 4, 128], BF16)
        oT = [sbp(f"hoT{i}", [128, 4, 128], F32) for i in range(2)]

        def v4(ps):
            return ps[:].rearrange("p (h t) -> p h t", h=4)

        def vb4(ps):
            return ps[:].bitcast(BF16)[:, 0:512].rearrange("p (h t) -> p h t", h=4)

        def f3(t):
            return t[:].rearrange("p (g s) -> p g s", s=32)
        ci = 0
        for si, (s0, ln, cond) in enumerate(SEQS):
            if cond == 0:
                k.dma("sp", S[:], I["sh0"][l, d].rearrange("h k v -> k h v"), writes=["hS"])
            else:
                k.op("dve", lambda e: e.memset(S[:], 0.0), writes=["hS"])
            copy("dve", Sb[:], S[:], ["hS"], ["hSb"])
            chunks = list(range(ln // 128))
            if d == 1:
                chunks = chunks[::-1]
            for cj in chunks:
                ta = s0 + cj * 128
                b = ci % 2; ci += 1
                k.dma("sp", qb[b][:], X["QB"].rearrange("(h p) t -> p h t", p=128)[:, :, ta:ta + 128], reads=["QB"], writes=[f"hqb{b}"])
                k.dma("sp", ib[b][:], X["IB"].rearrange("(h p) t -> p h t", p=128)[:, :, ta:ta + 128], reads=["IB"], writes=[f"hib{b}"])
                k.dma("act", ff[b][:], X["FF"][d * 512:(d + 1) * 512, :].rearrange("(h p) t -> p h t", p=128)[:, :, ta:ta + 128], reads=["FF"], writes=[f"hff{b}"])
                fv = ff[b][:].rearrange("p h t -> p (h t)")
                act(lf[:], fv, AF.Ln, [f"hff{b}"], ["hlf"])
                ts("pool", kf[:], fv, -1.0, 1.0, ALU.mult, ALU.add, [f"hff{b}"], ["hkf"])
                k.op("dve", lambda e: e.tensor_tensor_scan(out=bb[:], data0=C("seg"), data1=lf[:], initial=0.0, op0=ALU.mult, op1=ALU.add),
                     reads=["cst", "hlf"], writes=["hbb"])
                copy("dve", bl[:], f3(bb)[:, :, 31], ["hbb"], ["hbl"])
                if d == 1:
                    tt("dve", tmp[:], lf[:], bb[:], ALU.subtract, ["hlf", "hbb"], ["htmp"])
                    tt("dve", f3(bb), f3(tmp), bl[:].unsqueeze(2).to_broadcast([128, 16, 32]), ALU.add, ["htmp", "hbl"], ["hbb"])
                act(ebl[:], bl[:], AF.Exp, ["hbl"], ["hebl"])
                act(eb[:], bb[:], AF.Exp, ["hbb"], ["heb"])
                act(enb[:], bb[:], AF.Exp, ["hbb"], ["henb"], scale=-1.0)
                tt("dve", f3(tmp), bl[:].unsqueeze(2).to_broadcast([128, 16, 32]), f3(bb), ALU.subtract, ["hbb", "hbl", "htmp"], ["htmp"])
                act(ekd[:], tmp[:], AF.Exp, ["htmp"], ["hekd"])
                tt("dve", qe[:].rearrange("p h t -> p (h t)"), qb[b][:].rearrange("p h t -> p (h t)"), eb[:], ALU.mult, [f"hqb{b}", "heb"], ["hqe"])
                tt("pool", ke[:].rearrange("p h t -> p (h t)"), kf[:], enb[:], ALU.mult, ["hkf", "henb"], ["hke"])
                tt("pool", kdT[:].rearrange("p h t -> p (h t)"), kf[:], ekd[:], ALU.mult, ["hkf", "hekd"], ["hkdT"])
                for (srcT, dst, sk_, dk_) in ((kdT, kdt, "hkdT", "hkdt"), (ib[b], vtok, f"hib{b}", "hvtok")):
                    pst, pk = nps()
                    pb = vb4(pst)
                    for h in range(4):
                        k.op("pe", lambda e: e.transpose(pb[:, h, :], srcT[:, h, :], identb[:]), reads=[sk_, "identb"], writes=[pk])
                    copy(ev(), dst[:], pb, [pk], [dk_])
                ts("pool", kd3[64:128], kdt[64:128], C("tri1")[64:128, 96:97], None, ALU.mult, None, ["hkdt", "cst"], ["hkd3"])
                pst, pk = nps()
                for h in range(4):
                    mm(pst[:, h * 128:(h + 1) * 128], ke[:, h, :], qe[:, h, :], True, True, ["hke", "hqe"], [pk])
                tt("dve", attm[:], v4(pst), hm[:], ALU.mult, [pk, "hhm"], ["hattm"])
                po, ko = nps()
                blks = [0, 1, 2, 3] if d == 0 else [3, 2, 1, 0]
                for bi in blks:
                    cs = slice(bi * 32, (bi + 1) * 32)
                    for h in range(4):
                        osl = po[:, h * 128 + bi * 32:h * 128 + (bi + 1) * 32]
                        mm(osl, Sb[:, h, :], qe[:, h, cs], True, False, ["hSb", "hqe"], [ko])
                        mm(osl, vtok[:, h, :], attm[:, h, cs], False, True, ["hvtok", "hattm"], [ko])
                    pst, pk = nps()
                    for h in range(4):
                        if bi < 3:
                            mm(pst[:, h * 128:(h + 1) * 128], kdt[cs, h, :], vtok[cs, h, :], True, True, ["hkdt", "hvtok"], [pk])
                        else:
                            mm(pst[:, h * 128:(h + 1) * 128], kd3[64:128, h, :], vtok[64:128, h, :], True, True, ["hkd3", "hvtok"], [pk])
                    dec = ebl[:].rearrange("p (h g) -> p h g", g=4)[:, :, bi:bi + 1].to_broadcast([128, 4, 128])
                    tt("dve", S[:], S[:], dec, ALU.mult, ["hS", "hebl"], ["hS"])
                    tt("dve", S[:], S[:], v4(pst), ALU.add, ["hS", pk], ["hS"])
                    copy("act", Sb[:], S[:], ["hS"], ["hSb"])
                ob_ = ci % 2
                copy("act", oT[ob_][:], v4(po), [ko], [f"hoT{ob_}"])
                k.dma("act", X["OB"][d].rearrange("(h p) t -> p h t", p=128)[:, :, ta:ta + 128], oT[ob_][:], reads=[f"hoT{ob_}"], writes=["OB"])
                yield
            if cond == 1:
                k.dma("sp", O["nsh"][si - 1, l, d].rearrange("h k v -> k h v"), S[:], reads=["hS"], writes=["nsh"])
        yield


def phase_attn(X, l):
    nc = X["nc"]; k = X["k"]; I = X["I"]; O = X["O"]; C = X["C"]
    mm = X["mm"]; act = X["act"]; tt = X["tt"]; ts = X["ts"]; stt = X["stt"]; nps = X["nps"]; ev = X["ev"]; copy = X["copy"]
    with ExitStack() as ph:
        def sbp(name, shape, dt):
            return ph.enter_context(nc.sbuf_tensor(_un(name), list(shape), dt))
        ckt = sbp("ackt", [128, 2, 128], F32); ckT = sbp("ackT", [64, 2, 256], BF16); cvb = sbp("acvb", [128, 2, 128], BF16)
        k.dma("sp", ckt[:], I["ck"][l].rearrange("(b p) f -> p b f", p=128), writes=["ackt"])
        k.dma("pool", cvb[:], I["cv"][l].rearrange("(b p) f -> p b f", p=128), writes=["acvb"])
        for g in range(2):
            pst, pk = nps()
            for bk in range(2):
                k.op("pe", lambda e: e.transpose(pst[:64, bk * 128:(bk + 1) * 128], ckt[:, bk, g * 64:(g + 1) * 64], C("ident")), reads=["ackt", "cst"], writes=[pk])
            copy("dve", ckT[:, g, :], pst[:64, 0:256], [pk], ["ackT"])
        esk = sbp("aesk", [64, 8], F32)
        k.dma("sp", esk[:], I["sink"][l:l + 1, :].to_broadcast([64, 8]), writes=["aesk"])
        act(esk[:], esk[:], AF.Exp, ["aesk"], ["aesk"])
        wm = sbp("awm", [128, 2, 128], BF16)
        copy("dve", wm[:, 0, :], C("wprev"), ["cst"], ["awm"])
        copy("dve", wm[:, 1, :], C("wnext"), ["cst"], ["awm"])
        qT = [sbp(f"aq{i}", [64, 8, 128], BF16) for i in range(2)]
        kT3 = [sbp(f"ak{i}", [64, 2, 384], BF16) for i in range(2)]
        v3 = [sbp(f"av{i}", [128, 3, 128], BF16) for i in range(2)]
        pT = [sbp(f"ap{i}", [128, 512], BF16) for i in range(3)]
        den = sbp("aden", [64, 4, 128], F32); oc = [sbp(f"aoc{i}", [64, 8, 128], BF16) for i in range(2)]
        ci = 0; pi = 0
        for si, (s0, ln, cond) in enumerate(SEQS):
            nb = ln // 128
            for qbk in range(nb):
                ta = s0 + qbk * 128
                b = ci % 2; ci += 1
                k.dma("sp", qT[b][:], X["QCN"][:, 0:8, ta:ta + 128], reads=["QCN"], writes=[f"aq{b}"])
                if cond == 0:
                    lo = max(qbk - 1, 0); hi = min(qbk + 1, nb - 1)
                else:
                    lo, hi = 0, nb - 1
                nk = hi - lo + 1
                k.dma("sp", kT3[b][:, :, 0:nk * 128], X["QCN"][:, 8:10, s0 + lo * 128:s0 + (hi + 1) * 128], reads=["QCN"], writes=[f"ak{b}"])
                k.dma("act", v3[b][:, 0:nk, :], X["VCN"][s0 + lo * 128:s0 + (hi + 1) * 128, :].rearrange("(b p) f -> p b f", p=128), reads=["VCN"], writes=[f"av{b}"])
                for g in range(2):
                    kbl = []
                    for j in range(nk):
                        kb_ = lo + j
                        m = None
                        if cond == 0 and kb_ == qbk - 1:
                            m = 0
                        if cond == 0 and kb_ == qbk + 1:
                            m = 1
                        kbl.append((kT3[b][:, g, j * 128:(j + 1) * 128], v3[b][:, j, g * 64:(g + 1) * 64], m, [f"ak{b}"], [f"av{b}"]))
                    if cond == 0:
                        for bk in range(2):
                            kbl.append((ckT[:, g, bk * 128:(bk + 1) * 128], cvb[:, bk, g * 64:(g + 1) * 64], None, ["ackT"], ["acvb"]))
                    po, ko = nps(); pr, kr = nps()
                    for j, (kap, vap, m, kk_, vk_) in enumerate(kbl):
                        pst, pk = nps()
                        mm(pst[:], kap, qT[b][:, g * 4:(g + 1) * 4, :].rearrange("p h t -> p (h t)"), True, True, kk_ + [f"aq{b}"], [pk])
                        pb = pi % 3; pi += 1
                        act(pT[pb][:], pst[:], AF.Exp, [pk], [f"ap{pb}"], scale=0.125)
                        if m is not None:
                            tt("pool", pT[pb][:].rearrange("p (h t) -> p h t", h=4), pT[pb][:].rearrange("p (h t) -> p h t", h=4),
                               wm[:, m, :].unsqueeze(1).to_broadcast([128, 4, 128]), ALU.mult, [f"ap{pb}", "awm"], [f"ap{pb}"])
                        mm(po[:64, :], vap, pT[pb][:], j == 0, j == len(kbl) - 1, vk_ + [f"ap{pb}"], [ko])
                        mm(pr[:64, :], X["onesb"][:, 0:64], pT[pb][:], j == 0, j == len(kbl) - 1, ["onesb", f"ap{pb}"], [kr])
                    tt("dve", den[:], pr[:64, :].rearrange("p (h t) -> p h t", h=4), esk[:, g * 4:(g + 1) * 4].unsqueeze(2).to_broadcast([64, 4, 128]), ALU.add, [kr, "aesk"], ["aden"])
                    k.op("dve", lambda e: e.reciprocal(out=den[:], in_=den[:]), reads=["aden"], writes=["aden"])
                    tt("dve", oc[b][:, g * 4:(g + 1) * 4, :], po[:64, :].rearrange("p (h t) -> p h t", h=4), den[:], ALU.mult, [ko, "aden"], [f"aoc{b}"])
                k.dma("act", X["OC"][:, :, ta:ta + 128], oc[b][:], reads=[f"aoc{b}"], writes=["OC"])
        k.barrier()


MTILES = [(t0 + h * 256, 256, c, s) for (t0, n, c, s) in TILES for h in range(n // 256)]


def phase_merge(X, l):
    nc = X["nc"]; k = X["k"]; I = X["I"]; O = X["O"]; C = X["C"]
    mm = X["mm"]; act = X["act"]; tt = X["tt"]; ts = X["ts"]; stt = X["stt"]; nps = X["nps"]; ev = X["ev"]; copy = X["copy"]
    n = 256
    with ExitStack() as ph:
        def sbp(name, shape, dt):
            return ph.enter_context(nc.sbuf_tensor(_un(name), list(shape), dt))
        Wg = sbp("mWg", [128, 8, 3072], BF16); Wb = sbp("mWb", [128, 8, 1024], BF16); WbC = sbp("mWbC", [64, 8, 1024], BF16); Wo = sbp("mWo", [128, 8, 1024], BF16)
        for kc in range(8):
            k.dma("pool", Wg[:, kc, :], I["w_in"][l][kc * 128:(kc + 1) * 128, C_MG:DIN], writes=["mWg"])
            k.dma("pool", Wo[:, kc, :], I["w_out"][l][kc * 128:(kc + 1) * 128, :], writes=["mWo"])
            k.dma("pool", Wb[:, kc, :], I["w_branch"][l, kc // 4][(kc % 4) * 128:(kc % 4 + 1) * 128, :], writes=["mWb"])
            k.dma("pool", WbC[:, kc, :], I["w_branch"][l, 2][kc * 64:(kc + 1) * 64, :], writes=["mWbC"])
        nab = sbp("mnab", [128, 2], F32)
        k.dma("sp", nab[:, 0:1], I["norm_a"][l].rearrange("(p o) -> p o", o=1), writes=["mnab"], allow_slow_non_contiguous=True)
        k.dma("sp", nab[:, 1:2], I["norm_b"][l].rearrange("(p o) -> p o", o=1), writes=["mnab"], allow_slow_non_contiguous=True)
        xT = sbp("mxT", [128, 8, n], F32); xr = sbp("mxr", [128, 8, n], F32); hT = sbp("mhT", [128, 8, n], BF16); rstd = sbp("mrstd", [128, n], F32)
        X["mrstd"] = rstd
        of = sbp("mof", [128, 4, n], F32); ob = sbp("mob", [128, 4, n], F32); gt = sbp("mgt", [128, 4, n], BF16); sq = sbp("msq", [128, 4, n], BF16)
        rn = sbp("mrn", [128, 4, n], F32)
        oab = sbp("moab", [128, 8, n], BF16); oc = sbp("moc", [64, 8, n], BF16); mg = sbp("mmg", [128, 8, n], BF16)
        sg = [sbp(f"msg{i}", [128, n], F32) for i in range(3)]
        for (t0, _, cond, si) in MTILES:
            k.dma("sp", xT[:], X["XTv"][:, :, 1 + t0:1 + t0 + n], reads=["XT"], writes=["mxT"])
            k.dma("act", xr[:], X["XTv"][:, :, 1 + t0:1 + t0 + n], reads=["XT"], writes=["mxr"])
            _norm_h(X, l, 1, xT, hT, n, cond, "m")
            for r, (src, gsrc, gk) in enumerate(((X["OA"], X["GA"], "GA"), (X["OB"], X["GB"], "GB"))):
                k.dma("sp", of[:], src[0].rearrange("(h p) t -> p h t", p=128)[:, :, t0:t0 + n], reads=["OA", "OB"], writes=["mof"])
                k.dma("act", ob[:], src[1].rearrange("(h p) t -> p h t", p=128)[:, :, t0:t0 + n], reads=["OA", "OB"], writes=["mob"])
                k.dma("sp", gt[:], gsrc.rearrange("(h p) t -> p h t", p=128)[:, :, t0:t0 + n], reads=[gk], writes=["mgt"])
                tt("dve", of[:], of[:], ob[:], ALU.add, ["mof", "mob"], ["mof"])
                act(sq[:], of[:], AF.Square, ["mof"], ["msq"])
                for hp in range(2):
                    pst, pk = nps()
                    for hh in range(2):
                        mm(pst[:, hh * n:(hh + 1) * n], X["onesb"][:], sq[:, hp * 2 + hh, :], True, True, ["onesb", "msq"], [pk])
                    act(rn[:, hp * 2:hp * 2 + 2, :], pst[:].rearrange("p (h t) -> p h t", h=2), AF.Sqrt, [pk], ["mrn"], bias=EPS, scale=1.0 / 128)
                k.op("dve", lambda e: e.reciprocal(out=rn[:], in_=rn[:]), reads=["mrn"], writes=["mrn"])
                tt("dve", of[:], of[:], rn[:], ALU.mult, ["mof", "mrn"], ["mof"])
                stt("dve", oab[:, r * 4:(r + 1) * 4, :], of[:], nab[:, r:r + 1], gt[:], ALU.mult, ALU.mult, ["mof", "mnab", "mgt"], ["moab"])
            k.dma("sp", oc[:], X["OC"][:, :, t0:t0 + n], reads=["OC"], writes=["moc"])
            for m in range(8):
                ms = slice(m * 128, (m + 1) * 128)
                brs = []
                for r in range(3):
                    pb_, kb_ = nps()
                    if r < 2:
                        for kc in range(4):
                            mm(pb_[:, :n], Wb[:, r * 4 + kc, ms], oab[:, r * 4 + kc, :], kc == 0, kc == 3, ["mWb", "moab"], [kb_])
                    else:
                        for hh in range(8):
                            mm(pb_[:, :n], WbC[:, hh, ms], oc[:, hh, :], hh == 0, hh == 7, ["mWbC", "moc"], [kb_])
                    pg_, kg_ = nps()
                    for kc in range(8):
                        mm(pg_[:, :n], Wg[:, kc, r * 1024 + m * 128:r * 1024 + (m + 1) * 128], hT[:, kc, :], kc == 0, kc == 7, ["mWg", "mhT"], [kg_])
                    act(sg[r][:], pg_[:, :n], AF.Sigmoid, [kg_], [f"msg{r}"])
                    tt("dve", sg[r][:], sg[r][:], pb_[:, :n], ALU.mult, [f"msg{r}", kb_], [f"msg{r}"])
                tt("pool", sg[0][:], sg[0][:], sg[1][:], ALU.add, ["msg0", "msg1"], ["msg0"])
                tt("pool", mg[:, m, :], sg[0][:], sg[2][:], ALU.add, ["msg0", "msg2"], ["mmg"])
            for m in range(8):
                pst, pk = nps()
                for kc in range(8):
                    mm(pst[:, :n], Wo[:, kc, m * 128:(m + 1) * 128], mg[:, kc, :], kc == 0, kc == 7, ["mWo", "mmg"], [pk])
                stt("dve", xr[:, m, :], pst[:, :n], X["modT"][:, l, cond, 16 + m:17 + m], xr[:, m, :], ALU.mult, ALU.add, [pk, "modT", "mxr"], ["mxr"])
            k.dma("act", X["XTv"][:, :, 1 + t0:1 + t0 + n], xr[:], reads=["mxr"], writes=["XT"])
        k.barrier()


def phase_ffn(X, l):
    nc = X["nc"]; k = X["k"]; I = X["I"]; O = X["O"]; C = X["C"]
    mm = X["mm"]; act = X["act"]; tt = X["tt"]; ts = X["ts"]; stt = X["stt"]; nps = X["nps"]; ev = X["ev"]; copy = X["copy"]
    n = 256
    XT2 = X["FFfull"]
    XT2v = XT2.rearrange("(c p) t -> p c t", p=128)
    with ExitStack() as ph:
        def sbp(name, shape, dt):
            return ph.enter_context(nc.sbuf_tensor(_un(name), list(shape), dt))
        Wu = sbp("fWu", [128, 8, 2 * DFF], BF16); Wd = sbp("fWd", [128, 22, 1024], BF16)
        for kc in range(8):
            k.dma("pool", Wu[:, kc, :], I["w_up"][l][kc * 128:(kc + 1) * 128, :], writes=["fWu"])
        for j in range(22):
            k.dma("pool", Wd[:, j, :], I["w_down"][l][j * 128:(j + 1) * 128, :], writes=["fWd"])
        cwf = sbp("fcw", [128, 44, 3], F32)
        for j in range(3):
            k.dma("sp", cwf[:, :, j], I["conv_ffn"][l, j].rearrange("(c p) -> p c", p=128), writes=["fcw"], allow_slow_non_contiguous=True)
        xT = sbp("fxT", [128, 8, n + 2], F32); xr = sbp("fxr", [128, 8, n], F32); hT = sbp("fhT", [128, 8, n + 2], BF16); rstd = sbp("frstd", [128, n + 2], F32)
        X["frstd"] = rstd
        acc = [sbp(f"facc{i}", [128, n], F32) for i in range(2)]
        pr = sbp("fpr", [128, 22, n], BF16)
        for (t0, _, cond, si) in MTILES:
            s0, ln, _c = SEQS[si]
            has_l = t0 > s0; has_r = (t0 + n) < (s0 + ln)
            k.dma("sp", xT[:, :, 0:n], X["XTv"][:, :, 1 + t0:1 + t0 + n], reads=["XT"], writes=["fxT"])
            k.dma("act", xr[:], X["XTv"][:, :, 1 + t0:1 + t0 + n], reads=["XT"], writes=["fxr"])
            k.dma("sp", xT[:, :, n:n + 1], X["XTv"][:, :, t0:t0 + 1], reads=["XT"], writes=["fxT"], allow_slow_non_contiguous=True)
            k.dma("sp", xT[:, :, n + 1:n + 2], X["XTv"][:, :, 1 + t0 + n:2 + t0 + n], reads=["XT"], writes=["fxT"], allow_slow_non_contiguous=True)
            _norm_h(X, l, 2, xT, hT, n + 2, cond, "f")
            for j in range(22):
                for w_, c0 in enumerate((j * 128, DFF + j * 128)):
                    cc = c0 // 128
                    pst, pk = nps()
                    for kc in range(8):
                        mm(pst[:, :n + 2], Wu[:, kc, c0:c0 + 128], hT[:, kc, :], kc == 0, kc == 7, ["fWu", "fhT"], [pk])
                    a = acc[w_]; ak = f"facc{w_}"
                    act(a[:], pst[:, :n], AF.Copy, [pk, "fcw"], [ak], scale=cwf[:, cc, 1:2])
                    stt("dve", a[:, 1:n], pst[:, 0:n - 1], cwf[:, cc, 0:1], a[:, 1:n], ALU.mult, ALU.add, [pk, "fcw", ak], [ak])
                    stt("dve", a[:, 0:n - 1], pst[:, 1:n], cwf[:, cc, 2:3], a[:, 0:n - 1], ALU.mult, ALU.add, [pk, "fcw", ak], [ak])
                    if has_l:
                        stt("dve", a[:, 0:1], pst[:, n:n + 1], cwf[:, cc, 0:1], a[:, 0:1], ALU.mult, ALU.add, [pk, "fcw", ak], [ak])
                    if has_r:
                        stt("dve", a[:, n - 1:n], pst[:, n + 1:n + 2], cwf[:, cc, 2:3], a[:, n - 1:n], ALU.mult, ALU.add, [pk, "fcw", ak], [ak])
                act(acc[0][:], acc[0][:], AF.Silu, ["facc0"], ["facc0"])
                tt("pool", pr[:, j, :], acc[0][:], acc[1][:], ALU.mult, ["facc0", "facc1"], ["fpr"])
            for m in range(8):
                pst, pk = nps()
                for j in range(22):
                    mm(pst[:, :n], Wd[:, j, m * 128:(m + 1) * 128], pr[:, j, :], j == 0, j == 21, ["fWd", "fpr"], [pk])
                stt("dve", xr[:, m, :], pst[:, :n], X["modT"][:, l, cond, 40 + m:41 + m], xr[:, m, :], ALU.mult, ALU.add, [pk, "modT", "fxr"], ["fxr"])
            k.dma("act", XT2v[:, :, 1 + t0:1 + t0 + n], xr[:], reads=["fxr"], writes=["XT2"])
        k.barrier()
        k.dma("sp", X["XT"][:, 1:1 + NT], XT2[:, 1:1 + NT], reads=["XT2"], writes=["XT"])
        k.barrier()


_CACHE = {}


def kernel(x_prompt, x_sample, state_delta, state_hgrn, cache_k, cache_v, c, c_ctx,
           ada_w, ada_b, norm1_w, w_in, conv_a, a_log, dt_bias, norm_a, lb_logits, norm_b,
           q_norm, k_norm, sink, w_branch, w_out, norm2_w, w_up, conv_ffn, w_down, _stage=99, _debug=()):
    f = lambda a: np.ascontiguousarray(np.asarray(a, dtype=np.float32))
    if "nc" not in _CACHE or _CACHE.get("stage") != _stage:
        _CACHE["nc"] = build_program(_stage, _debug)
        _CACHE["stage"] = _stage
    nc = _CACHE["nc"]
    carr, _ = host_consts()
    rcos, rsin = rope_tables()
    shared = dict(ada_w=f(ada_w), ada_b=f(ada_b), norm1_w=f(norm1_w), w_in=f(w_in), conv_a=f(conv_a),
                  a_log=f(a_log).reshape(DEPTH, 8), dt_bias=f(dt_bias).reshape(DEPTH, 8), norm_a=f(norm_a), lb_logits=f(lb_logits),
                  norm_b=f(norm_b), q_norm=f(q_norm), k_norm=f(k_norm), sink=f(sink), w_branch=f(w_branch), w_out=f(w_out),
                  norm2_w=f(norm2_w), w_up=f(w_up), conv_ffn=f(conv_ffn), w_down=f(w_down), consts=carr, rcos=rcos, rsin=rsin)
    xp = f(x_prompt); xs = f(x_sample); sd = f(state_delta); sh = f(state_hgrn); ck = f(cache_k); cv = f(cache_v)
    cc = f(c); cx = f(c_ctx)
    in_maps = []
    for core in range(8):
        b = core % 4
        m = dict(shared)
        m["x_lat"] = xs[b]
        m["x_ctx"] = xp[4 * core:4 * core + 4].reshape(NCTX, D)
        m["cond2"] = np.stack([cc[b], cx], axis=0)
        m["sd0"] = sd[b]; m["sh0"] = sh[b]
        m["ck"] = ck[b].reshape(DEPTH, 256, 128); m["cv"] = cv[b].reshape(DEPTH, 256, 128)
        in_maps.append(m)
    res = run_bass_kernel_spmd(nc, in_maps, core_ids=list(range(8)))
    R = res.results
    _CACHE["last"] = R
    y_prompt = np.concatenate([R[i]["y_ctx"].reshape(4, 256, D) for i in range(8)], axis=0)
    y_sample = np.stack([R[i]["y_lat"] for i in range(4)], axis=0)
    nsd = np.concatenate([R[i]["nsd"] for i in range(8)], axis=0)
    nsh = np.concatenate([R[i]["nsh"] for i in range(8)], axis=0)
    nck = np.concatenate([R[i]["nck"].reshape(4, DEPTH, 256, 2, 64) for i in range(8)], axis=0)
    ncv = np.concatenate([R[i]["ncv"].reshape(4, DEPTH, 256, 2, 64) for i in range(8)], axis=0)
    return (y_prompt.astype(np.float32), y_sample.astype(np.float32), nsd.astype(np.float32), nsh.astype(np.float32),
            nck.astype(np.float32), ncv.astype(np.float32))
```

```python
from contextlib import ExitStack
import numpy as np
import concourse.bass as bass
import concourse.mybir as mybir
from concourse.bass_utils import run_bass_kernel_spmd

F32 = mybir.dt.float32
BF16 = mybir.dt.bfloat16
AF = mybir.ActivationFunctionType
ALU = mybir.AluOpType
AX = mybir.AxisListType


_UID = [0]


def _un(name):
    _UID[0] += 1
    return f"{name}_u{_UID[0]}"


class KB:
    RING = 12

    def __init__(self, nc):
        self.nc = nc
        self.es = ExitStack()
        self.eng = {"pe": nc.tensor, "act": nc.scalar, "dve": nc.vector, "pool": nc.gpsimd, "sp": nc.sync}
        self.sem = {}
        for e in ("pe", "act", "dve", "pool"):
            self.sem[e] = self.es.enter_context(nc.semaphore("s_" + e))
        self.cnt = {e: 0 for e in self.sem}
        self.ring = {}
        self.dcnt = {}
        for q in ("sp", "pool", "act"):
            self.ring[q] = [self.es.enter_context(nc.semaphore(f"d_{q}{i}")) for i in range(self.RING)]
            self.dcnt[q] = 0
        self.seen = {e: {} for e in self.eng}
        self.seenseq = {e: {} for e in self.eng}
        self.seq = {e: 0 for e in self.sem}
        self.lastins = {}
        self.sigmap = {}
        self.last_w = {}
        self.readers = {}
        self.n_inst = 0
        self.nosame = False
        self._in_dma = False
        self.prefix = ""

    def sb(self, name, shape, dt):
        return self.es.enter_context(self.nc.sbuf_tensor(name, list(shape), dt))

    def ps(self, name, shape, dt=F32):
        return self.es.enter_context(self.nc.psum_tensor(name, list(shape), dt))

    def dram(self, name, shape, dt, kind="Internal"):
        return self.nc.dram_tensor(name, list(shape), dt, kind=kind)

    def _signal_upto(self, e2, seq):
        sm = self.sigmap.setdefault(e2, [])
        if not sm or sm[-1][0] < seq:
            ins, lseq = self.lastins[e2]
            assert lseq >= seq
            self.cnt[e2] += 1
            ins.then_inc(self.sem[e2], 1)
            sm.append((lseq, self.cnt[e2]))
        lo, hi = 0, len(sm) - 1
        while lo < hi:
            mid = (lo + hi) // 2
            if sm[mid][0] >= seq:
                hi = mid
            else:
                lo = mid + 1
        return sm[lo][1]

    def _wait(self, e, tok):
        kind = tok[0]
        if kind == "c":
            _, e2, seq = tok
            if e2 == e and (e == "pe" or (self.nosame and not self._in_dma)):
                return
            key = ("c", e2)
            if self.seenseq[e].get(key, 0) >= seq:
                return
            val = self._signal_upto(e2, seq)
            self.seenseq[e][key] = seq
            if self.seen[e].get(key, 0) >= val:
                return
            self.eng[e].wait_ge(self.sem[e2], val)
            self.seen[e][key] = val
        else:
            _, q, slot, val = tok
            key = ("d", q, slot)
            if self.seen[e].get(key, 0) >= val:
                return
            self.eng[e].wait_ge(self.ring[q][slot], val)
            self.seen[e][key] = val

    def _deps(self, e, reads, writes):
        toks = []
        for k in reads:
            if k in self.last_w:
                toks.append(self.last_w[k])
        for k in writes:
            if k in self.last_w:
                toks.append(self.last_w[k])
            toks.extend(self.readers.get(k, ()))
        for t in toks:
            self._wait(e, t)

    def _commit(self, tok, reads, writes):
        for k in reads:
            self.readers.setdefault(k, []).append(tok)
            if len(self.readers[k]) > 64:
                best = {}
                for t in self.readers[k]:
                    kk = t[:2] if t[0] == "c" else t[:3]
                    if kk not in best or t[-1] > best[kk][-1]:
                        best[kk] = t
                self.readers[k] = list(best.values())
        for k in writes:
            self.last_w[k] = tok
            self.readers[k] = []

    def _pk(self, keys):
        if not self.prefix:
            return keys
        return [x if x.startswith("ps") else self.prefix + x for x in keys]

    def op(self, e, fn, reads=(), writes=()):
        reads = self._pk(reads); writes = self._pk(writes)
        self._deps(e, reads, writes)
        ins = fn(self.eng[e])
        self.seq[e] += 1
        self.lastins[e] = (ins, self.seq[e])
        self._commit(("c", e, self.seq[e]), reads, writes)
        self.n_inst += 1
        return ins

    def dma(self, q, out, in_, reads=(), writes=(), **kw):
        reads = self._pk(reads); writes = self._pk(writes)
        i = self.dcnt[q]
        slot = i % self.RING
        rnd = i // self.RING
        if rnd > 0:
            self._wait(q, ("d", q, slot, 16 * rnd))
        self._in_dma = True
        self._deps(q, reads, writes)
        self._in_dma = False
        ins = self.eng[q].dma_start(out=out, in_=in_, **kw)
        ins.then_inc(self.ring[q][slot], 16)
        self.dcnt[q] += 1
        self._commit(("d", q, slot, 16 * (rnd + 1)), reads, writes)
        self.n_inst += 1
        return ins

    def finish(self, out_keys):
        for k in out_keys:
            if k in self.last_w:
                self._wait("sp", self.last_w[k])
        for e in ("pe", "act", "dve", "pool"):
            if self.seq[e]:
                self._wait("sp", ("c", e, self.seq[e]))
        for q in self.ring:
            n = self.dcnt[q]
            for slot in range(min(n, self.RING)):
                last_i = ((n - 1 - slot) // self.RING) * self.RING + slot
                self._wait("sp", ("d", q, slot, 16 * (last_i // self.RING + 1)))

    def barrier(self):
        toks = [("c", e, self.seq[e]) for e in self.seq if self.seq[e]]
        for q in self.ring:
            n = self.dcnt[q]
            for slot in range(min(n, self.RING)):
                last_i = ((n - 1 - slot) // self.RING) * self.RING + slot
                toks.append(("d", q, slot, 16 * (last_i // self.RING + 1)))
        for e in self.eng:
            for t in toks:
                self._wait(e, t)
        self.last_w = {}
        self.readers = {}


D = 1024
NLAT = 4096
NCTX = 1024
NT = NLAT + NCTX
DEPTH = 2
DIN = 8464
DFF = 2816
EPS = 1e-6
SEQS = [(0, 4096, 0)] + [(4096 + 256 * i, 256, 1) for i in range(4)]
TILES = [(512 * j, 512, 0, 0) for j in range(8)] + [(4096 + 256 * i, 256, 1, 1 + i) for i in range(4)]
PADA = 2
NTP = NT + 2 * PADA * len(SEQS)
BIG = 30000.0
C_QA, C_KA, C_VA, C_GA, C_BETA, C_ALPHA = 0, 512, 1024, 1536, 2048, 2056
C_QB, C_IB, C_FB, C_GB = 2064, 2576, 3088, 4112
C_QC, C_KC, C_VC, C_MG = 4624, 5136, 5264, 5392


def padcol(seq_idx, t):
    return t + PADA * (2 * seq_idx + 1)


def host_consts():
    c = {}
    i = np.arange(128)
    c["ident"] = np.eye(128, dtype=np.float32)
    c["ones"] = np.ones((128, 128), np.float32)
    triF = (i[:, None] <= i[None, :]).astype(np.float32)
    triB = (i[:, None] >= i[None, :]).astype(np.float32)
    c["tri0"], c["tri1"] = triF, triB
    c["neg0"] = np.where(i[None, :] < i[:, None], 0.0, BIG).astype(np.float32)
    c["neg1"] = np.where(i[None, :] > i[:, None], 0.0, BIG).astype(np.float32)
    blk = (i[:, None] // 32) == (i[None, :] // 32)
    c["hm0"] = (blk & (i[None, :] >= i[:, None])).astype(np.float32)
    c["hm1"] = (blk & (i[None, :] <= i[:, None])).astype(np.float32)
    c["wprev"] = (i[:, None] >= i[None, :]).astype(np.float32)
    c["wnext"] = (i[:, None] <= i[None, :]).astype(np.float32)
    def same(b_):
        return (i[:, None] // b_) == (i[None, :] // b_)
    c["b16"] = same(16).astype(np.float32)
    c["l1"] = (same(32) & ~same(16)).astype(np.float32)
    c["l2"] = (same(64) & ~same(32)).astype(np.float32)
    c["l3"] = (~same(64)).astype(np.float32)
    seg = np.ones((128, 512), np.float32)
    seg[:, ::32] = 0.0
    c["seg"] = seg
    names = list(c)
    arr = np.concatenate([c[n] for n in names], axis=1)
    offs = {}
    o = 0
    for n in names:
        offs[n] = (o, c[n].shape[1])
        o += c[n].shape[1]
    return arr, offs


def rope_tables():
    pos = np.arange(NLAT)
    row = (pos // 64).astype(np.float32)
    col = (pos % 64).astype(np.float32)
    nf = 16
    inv = (10000.0 ** (-np.arange(nf, dtype=np.float32) / nf)).astype(np.float32)
    ar = row[:, None] * inv
    ac = col[:, None] * inv
    cos = np.concatenate([np.cos(ar), np.cos(ac)], axis=1).astype(np.float32)
    sin = np.concatenate([np.sin(ar), np.sin(ac)], axis=1).astype(np.float32)
    return cos, sin


def build_program(stage=99, debug=()):
    nc = bass.Bass("TRN2", target_bir_lowering=False)
    k = KB(nc)
    k.nosame = False
    carr, coff = host_consts()

    def din(name, shape, dt=F32):
        return nc.dram_tensor(name, list(shape), dt, kind="ExternalInput").ap()

    def dout(name, shape, dt=F32):
        return nc.dram_tensor(name, list(shape), dt, kind="ExternalOutput").ap()

    def dscr(name, shape, dt=F32):
        return nc.dram_tensor(name, list(shape), dt, kind="ExternalOutput" if debug else "Internal").ap()

    I = {}
    I["x_lat"] = din("x_lat", [NLAT, D]); I["x_ctx"] = din("x_ctx", [NCTX, D])
    I["cond2"] = din("cond2", [2, D])
    I["sd0"] = din("sd0", [DEPTH, 2, 4, 128, 128]); I["sh0"] = din("sh0", [DEPTH, 2, 4, 128, 128])
    I["ck"] = din("ck", [DEPTH, 256, 128]); I["cv"] = din("cv", [DEPTH, 256, 128])
    I["ada_w"] = din("ada_w", [DEPTH, D, 6 * D]); I["ada_b"] = din("ada_b", [DEPTH, 6 * D])
    I["norm1_w"] = din("norm1_w", [DEPTH, D]); I["w_in"] = din("w_in", [DEPTH, D, DIN])
    I["conv_a"] = din("conv_a", [DEPTH, 5, 1536]); I["a_log"] = din("a_log", [DEPTH, 8]); I["dt_bias"] = din("dt_bias", [DEPTH, 8])
    I["norm_a"] = din("norm_a", [DEPTH, 128]); I["lb_logits"] = din("lb_logits", [2, DEPTH, 512]); I["norm_b"] = din("norm_b", [DEPTH, 128])
    I["q_norm"] = din("q_norm", [DEPTH, 64]); I["k_norm"] = din("k_norm", [DEPTH, 64]); I["sink"] = din("sink", [DEPTH, 8])
    I["w_branch"] = din("w_branch", [DEPTH, 3, 512, D]); I["w_out"] = din("w_out", [DEPTH, D, D]); I["norm2_w"] = din("norm2_w", [DEPTH, D])
    I["w_up"] = din("w_up", [DEPTH, D, 2 * DFF]); I["conv_ffn"] = din("conv_ffn", [DEPTH, 3, 2 * DFF]); I["w_down"] = din("w_down", [DEPTH, DFF, D])
    I["consts"] = din("consts", list(carr.shape)); I["rcos"] = din("rcos", [NLAT, 32]); I["rsin"] = din("rsin", [NLAT, 32])
    O = {}
    O["y_lat"] = dout("y_lat", [NLAT, D]); O["y_ctx"] = dout("y_ctx", [NCTX, D])
    O["nsd"] = dout("nsd", [4, DEPTH, 2, 4, 128, 128]); O["nsh"] = dout("nsh", [4, DEPTH, 2, 4, 128, 128])
    O["nck"] = dout("nck", [4, DEPTH, 256, 128]); O["ncv"] = dout("ncv", [4, DEPTH, 256, 128])
    XT = dscr("XT", [D, NT + 2])
    QKVA = dscr("QKVA", [1536, NTP])
    QKVN = dscr("QKVN", [1536, NT], BF16)
    GA = dscr("GA", [512, NT], BF16); GB = dscr("GB", [512, NT], BF16)
    QB = dscr("QB", [512, NT], BF16); IB = dscr("IB", [512, NT], BF16)
    FFfull = dscr("FF", [1024, NT + 2])
    FF = FFfull[:, 0:NT]
    BG = dscr("BG", [16, NT])
    QCN = dscr("QCN", [64, 10, NT], BF16)
    VCN = dscr("VCN", [NT, 128], BF16)
    OA = QKVA[0:1024, 0:NT].rearrange("(d r) t -> d r t", d=2)
    OB = dscr("OB", [2, 512, NT])
    OC = dscr("OC", [64, 8, NT], BF16)
    DBG = {n: dout("dbg_" + n, s) for n, s in debug}

    XTv = XT.rearrange("(c p) t -> p c t", p=128)

    cst = k.sb("cst", [128, carr.shape[1]], F32)
    k.dma("sp", cst[:], I["consts"], writes=["cst"])

    def C(n):
        o, w = coff[n]
        return cst[:, o:o + w]
    identb = k.sb("identb", [128, 128], BF16); onesb = k.sb("onesb", [128, 128], BF16)
    k.op("dve", lambda e: e.tensor_copy(out=identb[:], in_=C("ident")), reads=["cst"], writes=["identb"])
    k.op("dve", lambda e: e.tensor_copy(out=onesb[:], in_=C("ones")), reads=["cst"], writes=["onesb"])
    PS = [k.ps(f"ps{i}", [128, 512]) for i in range(8)]
    psi = [0]

    def nps():
        i = psi[0] % 8
        psi[0] += 1
        return PS[i], f"ps{i}"

    evi = [0]

    def ev():
        evi[0] += 1
        return "dve" if evi[0] % 3 == 0 else "act"

    def copy(e, out, in_, r, w):
        if e == "act":
            k.op("act", lambda g: g.copy(out=out, in_=in_), reads=r, writes=w)
        else:
            k.op(e, lambda g: g.tensor_copy(out=out, in_=in_), reads=r, writes=w)

    def mm(ps_ap, lhsT, rhs, start, stop, r, w):
        k.op("pe", lambda e: e.matmul(ps_ap, lhsT=lhsT, rhs=rhs, start=start, stop=stop), reads=r, writes=w)

    def act(out, in_, func, r, w, bias=0.0, scale=1.0):
        k.op("act", lambda e: e.activation(out=out, in_=in_, func=func, bias=bias, scale=scale), reads=r, writes=w)

    def tt(e, out, in0, in1, op, r, w):
        k.op(e, lambda g: g.tensor_tensor(out=out, in0=in0, in1=in1, op=op), reads=r, writes=w)

    def ts(e, out, in0, s1, s2, op0, op1, r, w):
        if s2 is None:
            k.op(e, lambda g: g.tensor_scalar(out=out, in0=in0, scalar1=s1, scalar2=None, op0=op0), reads=r, writes=w)
        else:
            k.op(e, lambda g: g.tensor_scalar(out=out, in0=in0, scalar1=s1, scalar2=s2, op0=op0, op1=op1), reads=r, writes=w)

    def stt(e, out, in0, scalar, in1, op0, op1, r, w):
        k.op(e, lambda g: g.scalar_tensor_tensor(out=out, in0=in0, scalar=scalar, in1=in1, op0=op0, op1=op1), reads=r, writes=w)

    modT = k.sb("modT", [128, DEPTH, 2, 48], F32)
    gm1 = k.sb("gm1", [128, DEPTH, 2, 8], F32); gm2 = k.sb("gm2", [128, DEPTH, 2, 8], F32)
    with ExitStack() as ph:
        def sbp(name, shape, dt):
            return ph.enter_context(nc.sbuf_tensor(_un(name), list(shape), dt))
        cT = sbp("cT", [128, 8, 2], F32)
        for c in range(2):
            k.dma("sp", cT[:, :, c], I["cond2"][c].rearrange("(kc p) -> p kc", p=128), writes=["cT"], allow_slow_non_contiguous=True)
        act(cT[:], cT[:], AF.Silu, ["cT"], ["cT"])
        adab = sbp("adab", [128, DEPTH, 48], F32)
        nw = sbp("nw", [128, 2, DEPTH, 8], F32)
        for l in range(DEPTH):
            k.dma("sp", adab[:, l, :], I["ada_b"][l].rearrange("(c p) -> p c", p=128), writes=["adab"], allow_slow_non_contiguous=True)
            k.dma("sp", nw[:, 0, l, :], I["norm1_w"][l].rearrange("(c p) -> p c", p=128), writes=["nw"], allow_slow_non_contiguous=True)
            k.dma("sp", nw[:, 1, l, :], I["norm2_w"][l].rearrange("(c p) -> p c", p=128), writes=["nw"], allow_slow_non_contiguous=True)
        awb = [sbp(f"aw{i}", [128, 8, 768], F32) for i in range(2)]
        for l in range(DEPTH):
            for g in range(8):
                aw = awb[g % 2]; ak = f"aw{g % 2}"
                k.dma("sp" if g % 2 else "act", aw[:], I["ada_w"][l][:, g * 768:(g + 1) * 768].rearrange("(kc p) n -> p kc n", p=128), writes=[ak])
                pst, pk = nps()
                for j in range(6):
                    for kc in range(8):
                        mm(pst[:, 2 * j:2 * j + 2], aw[:, kc, j * 128:(j + 1) * 128], cT[:, kc, :], kc == 0, kc == 7, [ak, "cT"], [pk])
                for c in range(2):
                    tt("dve", modT[:, l, c, g * 6:(g + 1) * 6], pst[:, c:12:2], adab[:, l, g * 6:(g + 1) * 6], ALU.add, [pk, "adab"], ["modT"])
            for c in range(2):
                stt("dve", gm1[:, l, c, :], modT[:, l, c, 8:16], 1.0, nw[:, 0, l, :], ALU.add, ALU.mult, ["modT", "nw"], ["gm1"])
                stt("dve", gm2[:, l, c, :], modT[:, l, c, 32:40], 1.0, nw[:, 1, l, :], ALU.add, ALU.mult, ["modT", "nw"], ["gm2"])
        k.barrier()

    with ExitStack() as ph:
        def sbp(name, shape, dt):
            return ph.enter_context(nc.sbuf_tensor(_un(name), list(shape), dt))
        xin = [sbp(f"xin{i}", [128, D], F32) for i in range(2)]
        xo = [sbp(f"xo{i}", [128, 8, 128], F32) for i in range(2)]
        for j in range(NT // 128):
            t0 = j * 128
            src = I["x_lat"][t0:t0 + 128, :] if t0 < NLAT else I["x_ctx"][t0 - NLAT:t0 - NLAT + 128, :]
            b = j % 2
            k.dma("sp", xin[b][:], src, writes=[f"xin{b}"])
            for hf in range(2):
                pst, pk = nps()
                for q in range(4):
                    kc = hf * 4 + q
                    k.op("pe", lambda e: e.transpose(pst[:, q * 128:(q + 1) * 128], xin[b][:, kc * 128:(kc + 1) * 128], C("ident")),
                         reads=[f"xin{b}", "cst"], writes=[pk])
                copy(ev(), xo[b][:, hf * 4:hf * 4 + 4, :], pst[:].rearrange("p (c t) -> p c t", c=4), [pk], [f"xo{b}"])
            k.dma("act", XTv[:, :, 1 + t0:1 + t0 + 128], xo[b][:], reads=[f"xo{b}"], writes=["XT"])
        k.barrier()
    ctx = dict(nc=nc, k=k, I=I, O=O, C=C, nps=nps, ev=ev, copy=copy, mm=mm, act=act, tt=tt, ts=ts, stt=stt,
               identb=identb, onesb=onesb, modT=modT, gm1=gm1, gm2=gm2, XT=XT, XTv=XTv, QKVA=QKVA, QKVN=QKVN, GA=GA, GB=GB,
               QB=QB, IB=IB, FF=FF, FFfull=FFfull, BG=BG, QCN=QCN, VCN=VCN, OA=OA, OB=OB, OC=OC, DBG=DBG)
    for l in range(DEPTH):
        if stage >= 1:
            phase_proj(ctx, l)
        if stage >= 2:
            phase_aprep(ctx, l)
            with ExitStack() as ph_:
                ctx["ph"] = ph_
                gens = [("a0:", phase_delta(ctx, l, 0)), ("a1:", phase_delta(ctx, l, 1))]
                if stage >= 3:
                    gens += [("b0:", phase_hgrn(ctx, l, 0)), ("b1:", phase_hgrn(ctx, l, 1))]
                alive = list(gens)
                while alive:
                    for g in list(alive):
                        k.prefix = g[0]
                        try:
                            next(g[1])
                        except StopIteration:
                            alive.remove(g)
                k.prefix = ""
                k.barrier()
        if stage >= 4:
            phase_attn(ctx, l)
        if stage >= 5:
            phase_merge(ctx, l)
        if stage >= 6:
            phase_ffn(ctx, l)
        if stage < 99:
            break
    phase_out(ctx, stage)
    k.finish([])
    build_program.last_k = k
    return nc


def _norm_h(X, l, which, xT, hT, n, cond, tag):
    k = X["k"]; mm = X["mm"]; act = X["act"]; tt = X["tt"]; ts = X["ts"]
    gm = X["gm1"] if which == 1 else X["gm2"]
    shc = 0 if which == 1 else 24
    act(hT[:, :, :n], xT[:, :, :n], AF.Square, [tag + "xT"], [tag + "hT"])
    pst, pk = X["nps"]()
    for kc in range(8):
        mm(pst[:, :n], X["onesb"][:], hT[:, kc, :n], kc == 0, kc == 7, ["onesb", tag + "hT"], [pk])
    rstd = X[tag + "rstd"]
    act(rstd[:, :n], pst[:, :n], AF.Sqrt, [pk], [tag + "rstd"], bias=EPS, scale=1.0 / D)
    k.op("dve", lambda e: e.reciprocal(out=rstd[:, :n], in_=rstd[:, :n]), reads=[tag + "rstd"], writes=[tag + "rstd"])
    tt("dve", xT[:, :, :n], xT[:, :, :n], rstd[:, :n].unsqueeze(1).to_broadcast([128, 8, n]), ALU.mult, [tag + "xT", tag + "rstd"], [tag + "xT"])
    for kc in range(8):
        ts("pool" if kc % 2 else "dve", hT[:, kc, :n], xT[:, kc, :n], gm[:, l, cond, kc:kc + 1], X["modT"][:, l, cond, shc + kc:shc + kc + 1],
           ALU.mult, ALU.add, [tag + "xT", "gm1", "gm2", "modT"], [tag + "hT"])


def phase_proj(X, l):
    nc = X["nc"]; k = X["k"]; I = X["I"]; O = X["O"]; C = X["C"]
    mm = X["mm"]; act = X["act"]; tt = X["tt"]; ts = X["ts"]; stt = X["stt"]; copy = X["copy"]; nps = X["nps"]; ev = X["ev"]
    with ExitStack() as ph:
        def sbp(name, shape, dt):
            return ph.enter_context(nc.sbuf_tensor(_un(name), list(shape), dt))
        W = sbp("p1w", [128, 8, C_MG], BF16)
        for kc in range(8):
            k.dma("pool", W[:, kc, :], I["w_in"][l][kc * 128:(kc + 1) * 128, 0:C_MG], writes=["p1w"])
        xT = sbp("p1xT", [128, 8, 512], F32); hT = sbp("p1hT", [128, 8, 512], BF16); rstd = sbp("p1rstd", [128, 512], F32)
        X["p1rstd"] = rstd
        sqkv = sbp("sqkv", [128, 12, 512], F32); sff = sbp("sff", [128, 8, 512], F32)
        sga = sbp("sga", [128, 4, 512], BF16); sgb = sbp("sgb", [128, 4, 512], BF16)
        sqb = sbp("sqb", [128, 4, 512], BF16); sib = sbp("sib", [128, 4, 512], BF16)
        sbg = sbp("sbg", [16, 512], F32); tb1 = sbp("tb1", [16, 512], F32); tb2 = sbp("tb2", [16, 512], F32)
        bcs = sbp("bcs", [16, 4], F32)
        k.op("dve", lambda e: e.memset(bcs[:], 0.0), writes=["bcs"])
        k.op("dve", lambda e: e.memset(bcs[0:8, 1:2], 1.0), writes=["bcs"])
        k.op("dve", lambda e: e.memset(bcs[0:8, 2:3], -1.0), writes=["bcs"])
        alg = sbp("alg", [16, 1], F32)
        k.op("dve", lambda e: e.memset(alg[:], 0.0), writes=["alg"])
        k.dma("sp", bcs[8:16, 0:1], I["dt_bias"][l].rearrange("(p o) -> p o", o=1), reads=[], writes=["bcs"], allow_slow_non_contiguous=True)
        k.dma("sp", alg[8:16, 0:1], I["a_log"][l].rearrange("(p o) -> p o", o=1), writes=["alg"], allow_slow_non_contiguous=True)
        act(alg[:], alg[:], AF.Exp, ["alg"], ["alg"])
        stt("dve", bcs[:, 3:4], bcs[:, 1:2], -1.0, alg[:], ALU.add, ALU.mult, ["bcs", "alg"], ["bcs"])
        lbt = sbp("lbt", [128, 8], F32); oml = sbp("oml", [128, 8], F32)
        if l == 0:
            k.op("dve", lambda e: e.memset(lbt[:], 0.0), writes=["lbt"])
        else:
            l0 = sbp("l0", [128, 8], F32); l1 = sbp("l1", [128, 8], F32)
            for d in range(2):
                k.dma("sp", l0[:, d * 4:d * 4 + 4], I["lb_logits"][d, 0].rearrange("(h p) -> p h", p=128), writes=["l0"], allow_slow_non_contiguous=True)
                k.dma("sp", l1[:, d * 4:d * 4 + 4], I["lb_logits"][d, 1].rearrange("(h p) -> p h", p=128), writes=["l1"], allow_slow_non_contiguous=True)
            tt("dve", l1[:], l1[:], l0[:], ALU.subtract, ["l0", "l1"], ["l1"])
            act(lbt[:], l1[:], AF.Sigmoid, ["l1"], ["lbt"])
            ts("dve", lbt[:], lbt[:], 1e-6, 1.0 - 1e-6, ALU.max, ALU.min, ["lbt"], ["lbt"])
        ts("dve", oml[:], lbt[:], -1.0, 1.0, ALU.mult, ALU.add, ["lbt"], ["oml"])
        nqk = sbp("nqk", [128, 10, 64], F32)
        for hh in range(10):
            src = I["q_norm"][l:l + 1, :] if hh < 8 else I["k_norm"][l:l + 1, :]
            k.dma("sp", nqk[:, hh, :], src.to_broadcast([128, 64]), writes=["nqk"])
        qk = sbp("qk", [128, 10, 64], F32); qk2 = sbp("qk2", [128, 10, 64], F32); qss = sbp("qss", [128, 10], F32)
        qr = sbp("qr", [128, 10, 64], F32); rtmp = sbp("rtmp", [128, 10, 2, 16], F32)
        v32 = sbp("v32", [128, 128], F32); rc = sbp("rc", [128, 32], F32); rs_ = sbp("rs_", [128, 32], F32)
        qkT = sbp("qkT", [64, 10, 128], BF16)

        for (t0, n, cond, si) in TILES:
            k.dma("sp", xT[:, :, :n], X["XTv"][:, :, 1 + t0:1 + t0 + n], reads=["XT"], writes=["p1xT"])
            _norm_h(X, l, 1, xT, hT, n, cond, "p1")
            pc0 = padcol(si, t0)

            def proj(c0, m):
                pst, pk = nps()
                for kc in range(8):
                    mm(pst[:m, :n], W[:, kc, c0:c0 + m], hT[:, kc, :n], kc == 0, kc == 7, ["p1w", "p1hT"], [pk])
                return pst, pk
            for c in range(12):
                pst, pk = proj(C_QA + c * 128, 128)
                copy(ev(), sqkv[:, c, :n], pst[:, :n], [pk], [f"sqkv{c}"])
            k.dma("sp", X["QKVA"].rearrange("(c p) t -> p c t", p=128)[:, :, pc0:pc0 + n], sqkv[:, :, :n], reads=[f"sqkv{c}" for c in range(12)], writes=["QKVA"])
            for (c0, st, dst, key) in ((C_GA, sga, X["GA"], "GA"), (C_QB, sqb, X["QB"], "QB"), (C_GB, sgb, X["GB"], "GB")):
                for c in range(4):
                    pst, pk = proj(c0 + c * 128, 128)
                    act(st[:, c, :n], pst[:, :n], AF.Silu, [pk], [f"s{key}{c}"])
                k.dma("act", dst.rearrange("(c p) t -> p c t", p=128)[:, :, t0:t0 + n], st[:, :, :n], reads=[f"s{key}{c}" for c in range(4)], writes=[key])
            for c in range(4):
                pst, pk = proj(C_IB + c * 128, 128)
                copy(ev(), sib[:, c, :n], pst[:, :n], [pk], [f"sIB{c}"])
            k.dma("act", X["IB"].rearrange("(c p) t -> p c t", p=128)[:, :, t0:t0 + n], sib[:, :, :n], reads=[f"sIB{c}" for c in range(4)], writes=["IB"])
            for c in range(8):
                pst, pk = proj(C_FB + c * 128, 128)
                act(sff[:, c, :n], pst[:, :n], AF.Sigmoid, [pk], [f"sff{c}"])
                ts("pool", sff[:, c, :n], sff[:, c, :n], oml[:, c:c + 1], lbt[:, c:c + 1], ALU.mult, ALU.add, [f"sff{c}", "oml", "lbt"], [f"sff{c}"])
            k.dma("sp", X["FF"].rearrange("(c p) t -> p c t", p=128)[:, :, t0:t0 + n], sff[:, :, :n], reads=[f"sff{c}" for c in range(8)], writes=["FF"])
            pst, pk = proj(C_BETA, 16)
            act(tb1[:, :n], pst[:16, :n], AF.Exp, [pk, "bcs"], ["tb1"], bias=bcs[:, 0:1], scale=1.0)
            ts("dve", tb1[:, :n], tb1[:, :n], 1.0, None, ALU.add, None, ["tb1"], ["tb1"])
            act(tb2[:, :n], tb1[:, :n], AF.Ln, ["tb1"], ["tb2"])
            k.op("dve", lambda e: e.reciprocal(out=tb1[:, :n], in_=tb1[:, :n]), reads=["tb1"], writes=["tb1"])
            ts("dve", tb1[:, :n], tb1[:, :n], bcs[:, 2:3], bcs[:, 1:2], ALU.mult, ALU.add, ["tb1", "bcs"], ["tb1"])
            stt("dve", sbg[:, :n], tb2[:, :n], bcs[:, 3:4], tb1[:, :n], ALU.mult, ALU.add, ["tb1", "tb2", "bcs"], ["sbg"])
            k.dma("sp", X["BG"][:, t0:t0 + n], sbg[:, :n], reads=["sbg"], writes=["BG"])
            for j in range(n // 128):
                ta = t0 + j * 128
                p1, k1 = nps(); p2, k2 = nps()
                for kc in range(8):
                    mm(p1[:, :], hT[:, kc, j * 128:(j + 1) * 128], W[:, kc, C_QC:C_QC + 512], kc == 0, kc == 7, ["p1w", "p1hT"], [k1])
                for kc in range(8):
                    mm(p2[:, :256], hT[:, kc, j * 128:(j + 1) * 128], W[:, kc, C_KC:C_KC + 256], kc == 0, kc == 7, ["p1w", "p1hT"], [k2])
                copy("dve", qk[:, 0:8, :], p1[:].rearrange("p (h d) -> p h d", h=8), [k1], ["qk"])
                copy("act", qk[:, 8:10, :], p2[:, 0:128].rearrange("p (h d) -> p h d", h=2), [k2], ["qk"])
                copy("act", v32[:], p2[:, 128:256], [k2], ["v32"])
                k.dma("pool", X["VCN"][ta:ta + 128, :], v32[:], reads=["v32"], writes=["VCN"])
                if cond == 1:
                    k.dma("act", O["ncv"][si - 1, l, ta - t0:ta - t0 + 128, :], v32[:], reads=["v32"], writes=["ncv"])
                tt("dve", qk2[:], qk[:], qk[:], ALU.mult, ["qk"], ["qk2"])
                k.op("dve", lambda e: e.tensor_reduce(out=qss[:], in_=qk2[:], axis=AX.X, op=ALU.add), reads=["qk2"], writes=["qss"])
                act(qss[:], qss[:], AF.Sqrt, ["qss"], ["qss"], bias=EPS, scale=1.0 / 64)
                k.op("dve", lambda e: e.reciprocal(out=qss[:], in_=qss[:]), reads=["qss"], writes=["qss"])
                tt("dve", qk[:], qk[:], qss[:].unsqueeze(2).to_broadcast([128, 10, 64]), ALU.mult, ["qk", "qss"], ["qk"])
                tt("pool", qk[:], qk[:], nqk[:], ALU.mult, ["qk", "nqk"], ["qk"])
                src = qk
                if cond == 1:
                    k.dma("act", O["nck"][si - 1, l, ta - t0:ta - t0 + 128, :], qk[:, 8:10, :].rearrange("p h d -> p (h d)"), reads=["qk"], writes=["nck"])
                else:
                    k.dma("sp", rc[:], I["rcos"][ta:ta + 128, :], writes=["rc"])
                    k.dma("sp", rs_[:], I["rsin"][ta:ta + 128, :], writes=["rs_"])
                    xv = qk[:].rearrange("p h (a b f) -> p h a b f", a=2, b=2)
                    ov = qr[:].rearrange("p h (a b f) -> p h a b f", a=2, b=2)
                    cb = rc[:].rearrange("p (a f) -> p a f", a=2).unsqueeze(1).to_broadcast([128, 10, 2, 16])
                    sb_ = rs_[:].rearrange("p (a f) -> p a f", a=2).unsqueeze(1).to_broadcast([128, 10, 2, 16])
                    tt("dve", ov[:, :, :, 0, :], xv[:, :, :, 0, :], cb, ALU.mult, ["qk", "rc"], ["qr"])
                    tt("pool", rtmp[:], xv[:, :, :, 1, :], sb_, ALU.mult, ["qk", "rs_"], ["rtmp"])
                    tt("dve", ov[:, :, :, 0, :], ov[:, :, :, 0, :], rtmp[:], ALU.subtract, ["qr", "rtmp"], ["qr"])
                    tt("dve", ov[:, :, :, 1, :], xv[:, :, :, 1, :], cb, ALU.mult, ["qk", "rc"], ["qr"])
                    tt("pool", rtmp[:], xv[:, :, :, 0, :], sb_, ALU.mult, ["qk", "rs_", "qr"], ["rtmp"])
                    tt("dve", ov[:, :, :, 1, :], ov[:, :, :, 1, :], rtmp[:], ALU.add, ["qr", "rtmp"], ["qr"])
                    src = qr
                skey = "qk" if src is qk else "qr"
                for (h0, h1) in ((0, 4), (4, 8), (8, 10)):
                    pst, pk = nps()
                    for hh in range(h0, h1):
                        k.op("pe", lambda e: e.transpose(pst[:64, (hh - h0) * 128:(hh - h0 + 1) * 128], src[:, hh, :], C("ident")),
                             reads=[skey, "cst"], writes=[pk])
                    copy(ev(), qkT[:, h0:h1, :], pst[:64, :(h1 - h0) * 128].rearrange("p (h t) -> p h t", h=h1 - h0), [pk], ["qkT"])
                k.dma("sp", X["QCN"][:, :, ta:ta + 128], qkT[:], reads=["qkT"], writes=["QCN"])
        k.barrier()


def phase_out(X, stage):
    nc = X["nc"]; k = X["k"]; O = X["O"]; C = X["C"]; nps = X["nps"]; ev = X["ev"]; copy = X["copy"]
    with ExitStack() as ph:
        def sbp(name, shape, dt):
            return ph.enter_context(nc.sbuf_tensor(_un(name), list(shape), dt))
        xi = [sbp(f"oxi{i}", [128, 8, 128], F32) for i in range(2)]
        xo = [sbp(f"oxo{i}", [128, D], F32) for i in range(2)]
        for j in range(NT // 128):
            t0 = j * 128; b = j % 2
            k.dma("sp", xi[b][:], X["XTv"][:, :, 1 + t0:1 + t0 + 128], reads=["XT"], writes=[f"oxi{b}"])
            for hf in range(2):
                pst, pk = nps()
                for q in range(4):
                    kc = hf * 4 + q
                    k.op("pe", lambda e: e.transpose(pst[:, q * 128:(q + 1) * 128], xi[b][:, kc, :], C("ident")), reads=[f"oxi{b}", "cst"], writes=[pk])
                copy(ev(), xo[b][:, hf * 512:(hf + 1) * 512], pst[:], [pk], [f"oxo{b}"])
            dst = O["y_lat"][t0:t0 + 128, :] if t0 < NLAT else O["y_ctx"][t0 - NLAT:t0 - NLAT + 128, :]
            k.dma("act", dst, xo[b][:], reads=[f"oxo{b}"], writes=["y"])
        k.barrier()


def phase_aprep(X, l):
    nc = X["nc"]; k = X["k"]; I = X["I"]
    mm = X["mm"]; act = X["act"]; tt = X["tt"]; ts = X["ts"]; stt = X["stt"]; nps = X["nps"]
    with ExitStack() as ph:
        def sbp(name, shape, dt):
            return ph.enter_context(nc.sbuf_tensor(_un(name), list(shape), dt))
        cw = sbp("cw", [128, 12, 5], F32)
        for j in range(5):
            k.dma("sp", cw[:, :, j], I["conv_a"][l, j].rearrange("(c p) -> p c", p=128), writes=["cw"], allow_slow_non_contiguous=True)
        z = sbp("apz", [128, 12, 2], F32)
        k.op("dve", lambda e: e.memset(z[:], 0.0), writes=["apz"])
        Qv = X["QKVA"].rearrange("(c p) t -> p c t", p=128)
        if True:
            for si, (s0, ln, _) in enumerate(SEQS):
                a = padcol(si, s0)
                k.dma("sp", Qv[:, :, a - 2:a], z[:], reads=["apz"], writes=["QKVApad"])
                k.dma("sp", Qv[:, :, a + ln:a + ln + 2], z[:], reads=["apz"], writes=["QKVApad"])
        xp = [sbp(f"apx{i}", [128, 12, 516], F32) for i in range(2)]
        acc = sbp("apacc", [128, 12, 512], F32); sq = sbp("apsq", [128, 8, 512], BF16); rn = sbp("aprn", [128, 512], F32)
        ob = sbp("apob", [128, 12, 512], BF16)
        for ti, (t0, n, cond, si) in enumerate(TILES):
            b = ti % 2
            pc0 = padcol(si, t0)
            k.dma("sp", xp[b][:, :, :n + 4], Qv[:, :, pc0 - 2:pc0 + n + 2], reads=["QKVA", "QKVApad"], writes=[f"apx{b}"])
            for c in range(12):
                ts("dve" if c % 3 else "pool", acc[:, c, :n], xp[b][:, c, 0:n], cw[:, c, 0:1], None, ALU.mult, None, [f"apx{b}", "cw"], [f"acc{c}"])
            for j in range(1, 5):
                for c in range(12):
                    stt("dve", acc[:, c, :n], xp[b][:, c, j:j + n], cw[:, c, j:j + 1], acc[:, c, :n], ALU.mult, ALU.add, [f"apx{b}", "cw", f"acc{c}"], [f"acc{c}"])
            acck = [f"acc{c}" for c in range(12)]
            act(acc[:, :, :n], acc[:, :, :n], AF.Silu, acck, acck)
            act(sq[:, :, :n], acc[:, 0:8, :n], AF.Square, acck, ["apsq"])
            for c in range(8):
                pst, pk = nps()
                mm(pst[:, :n], X["onesb"][:], sq[:, c, :n], True, True, ["onesb", "apsq"], [pk])
                act(rn[:, :n], pst[:, :n], AF.Sqrt, [pk], ["aprn"], bias=EPS, scale=1.0)
                k.op("dve", lambda e: e.reciprocal(out=rn[:, :n], in_=rn[:, :n]), reads=["aprn"], writes=["aprn"])
                stt("dve", ob[:, c, :n], acc[:, c, :n], (128.0 ** -0.5) if c < 4 else 1.0, rn[:, :n], ALU.mult, ALU.mult, acck + ["aprn"], [f"apob{c}"])
            X["copy"]("pool", ob[:, 8:12, :n], acc[:, 8:12, :n], acck, ["apob8"])
            k.dma("act", X["QKVN"].rearrange("(c p) t -> p c t", p=128)[:, :, t0:t0 + n], ob[:, :, :n], reads=[f"apob{c}" for c in range(9)], writes=["QKVN"])
        k.barrier()


def phase_delta(X, l, d):
    nc = X["nc"]; k = X["k"]; I = X["I"]; O = X["O"]; C = X["C"]
    mm = X["mm"]; act = X["act"]; tt = X["tt"]; ts = X["ts"]; stt = X["stt"]; nps = X["nps"]; ev = X["ev"]; copy = X["copy"]
    identb = X["identb"]
    ph = X["ph"]
    if True:
        def sbp(name, shape, dt):
            return ph.enter_context(nc.sbuf_tensor(_un(name), list(shape), dt))
        S = sbp("dS", [128, 4, 128], F32); Sb = sbp("dSb", [128, 4, 128], BF16)
        negm = sbp("negm", [128, 4, 128], F32)
        copy("dve", negm[:], C(f"neg{d}").unsqueeze(1).to_broadcast([128, 4, 128]), ["cst"], ["negm"])
        qkv = [sbp(f"dqkv{i}", [128, 12, 128], BF16) for i in range(2)]
        bgT = [sbp(f"dbgT{i}", [16, 128], F32) for i in range(2)]
        ktok = sbp("dktok", [128, 4, 128], BF16); vtok = sbp("dvtok", [128, 4, 128], BF16)
        bgt = sbp("dbgt", [128, 16], F32); gct = sbp("dgct", [128, 8], F32); t12 = sbp("dt12", [128, 12], F32); e12 = sbp("de12", [128, 12], F32)
        bw = sbp("dbw", [128, 4], F32); nbeta = sbp("dnbeta", [128, 4], F32); ngc = sbp("dngc", [128, 4], F32)
        vb = sbp("dvb", [128, 4, 128], BF16); kbg = sbp("dkbg", [128, 4, 128], BF16); kd = sbp("dkd", [128, 4, 128], BF16)
        dg = sbp("ddg", [128, 2, 4, 128], F32); qg = sbp("dqg", [128, 4, 128], BF16)
        decs = sbp("ddecs", [128, 4, 128], BF16); P = sbp("dP", [128, 4, 128], BF16); Q = sbp("dQ", [128, 4, 128], BF16)
        qkm = sbp("dqkm", [128, 4, 128], BF16); qkmT = sbp("dqkmT", [128, 4, 128], BF16); R = sbp("dR", [128, 4, 128], BF16)
        nwT = sbp("dnwT", [128, 4, 128], BF16); vnew = sbp("dvnew", [128, 4, 128], BF16)
        oT = [sbp(f"doT{i}", [128, 4, 128], F32) for i in range(2)]
        Pk = sbp("dPk", [128, 4, 128], BF16); Qk = sbp("dQk", [128, 4, 128], BF16); Dm = sbp("dDm", [128, 4, 128], BF16)
        Xu = sbp("dXu", [128, 4, 128], BF16); Xd = sbp("dXd", [128, 4, 128], BF16)
        mk = {}
        for nm in ("b16", "l1", "l2", "l3"):
            mk[nm] = sbp("dmk" + nm, [128, 4, 128], BF16)
            copy("dve", mk[nm][:], C(nm).unsqueeze(1).to_broadcast([128, 4, 128]), ["cst"], ["dmk"])
        tri = C(f"tri{d}")
        identB4 = identb[:].unsqueeze(1).to_broadcast([128, 4, 128])
        identF4 = C("ident").unsqueeze(1).to_broadcast([128, 4, 128])

        def v4(ps):
            return ps[:].rearrange("p (h t) -> p h t", h=4)

        def vb4(ps):
            return ps[:].bitcast(BF16)[:, 0:512].rearrange("p (h t) -> p h t", h=4)

        ci = 0
        for si, (s0, ln, cond) in enumerate(SEQS):
            if cond == 0:
                k.dma("sp", S[:], I["sd0"][l, d].rearrange("h k v -> k h v"), writes=["dS"])
            else:
                k.op("dve", lambda e: e.memset(S[:], 0.0), writes=["dS"])
            copy("dve", Sb[:], S[:], ["dS"], ["dSb"])
            chunks = list(range(ln // 128))
            if d == 1:
                chunks = chunks[::-1]
            for cj in chunks:
                ta = s0 + cj * 128
                b = ci % 2; ci += 1
                qk_ = f"dqkv{b}"; bk_ = f"dbgT{b}"
                k.dma("sp", qkv[b][:], X["QKVN"].rearrange("(c p) t -> p c t", p=128)[:, :, ta:ta + 128], reads=["QKVN"], writes=[qk_])
                k.dma("sp", bgT[b][:], X["BG"][:, ta:ta + 128], reads=["BG"], writes=[bk_])
                qT = qkv[b][:, 0:4, :]; kT = qkv[b][:, 4:8, :]; vT = qkv[b][:, 8:12, :]
                for (srcT, dst, dk_) in ((kT, ktok, "dktok"), (vT, vtok, "dvtok")):
                    pst, pk = nps()
                    pb = vb4(pst)
                    for h in range(4):
                        k.op("pe", lambda e: e.transpose(pb[:, h, :], srcT[:, h, :], identb[:]), reads=[qk_, "identb"], writes=[pk])
                    copy(ev(), dst[:], pb, [pk], [dk_])
                pst, pk = nps()
                mm(pst[:, 0:16], bgT[b][:], C("ident")[0:16, 0:16], True, True, [bk_, "cst"], [pk])
                copy("dve", bgt[:], pst[:, 0:16], [pk], ["dbgt"])
                pst, pk = nps()
                gsl = bgt[:, 8 + 4 * d:12 + 4 * d]
                mm(pst[:, 0:4], tri, gsl, True, True, ["cst", "dbgt"], [pk])
                mm(pst[:, 4:8], C("ones"), gsl, True, True, ["cst", "dbgt"], [pk])
                copy("dve", gct[:], pst[:, 0:8], [pk], ["dgct"])
                copy("dve", t12[:, 0:4], gct[:, 0:4], ["dgct"], ["dt12"])
                tt("dve", t12[:, 4:8], gct[:, 4:8], gct[:, 0:4], ALU.subtract, ["dgct"], ["dt12"])
                copy("dve", t12[:, 8:12], gct[:, 4:8], ["dgct"], ["dt12"])
                act(e12[:], t12[:], AF.Exp, ["dt12"], ["de12"])
                beta = bgt[:, 4 * d:4 * d + 4]
                tt("dve", bw[:], beta, e12[:, 0:4], ALU.mult, ["dbgt", "de12"], ["dbw"])
                ts("dve", nbeta[:], beta, -1.0, None, ALU.mult, None, ["dbgt"], ["dnbeta"])

                def bc(ap):
                    return ap.unsqueeze(2).to_broadcast([128, 4, 128])
                tt("pool", vb[:], vtok[:], bc(beta), ALU.mult, ["dvtok", "dbgt"], ["dvb"])
                tt("pool", kbg[:], ktok[:], bc(bw[:]), ALU.mult, ["dktok", "dbw"], ["dkbg"])
                tt("pool", kd[:], ktok[:], bc(e12[:, 4:8]), ALU.mult, ["dktok", "de12"], ["dkd"])
                tt("pool", dg[:, 0], identF4, bc(e12[:, 0:4]), ALU.mult, ["cst", "de12"], ["ddg0"])
                tt("pool", dg[:, 1], identF4, bc(gct[:, 0:4]), ALU.mult, ["cst", "dgct"], ["ddg1"])
                pst, pk = nps()
                mm(pst[:], C("ones"), dg[:, 0].rearrange("p h t -> p (h t)"), True, True, ["cst", "ddg0"], [pk])
                tt("dve", qg[:], qT, v4(pst), ALU.mult, [qk_, pk], ["dqg"])
                pst, pk = nps()
                mm(pst[:], C("ones"), dg[:, 1].rearrange("p h t -> p (h t)"), True, False, ["cst", "ddg1"], [pk])
                mm(pst[:], C("ident"), negm[:].rearrange("p h t -> p (h t)"), False, True, ["cst", "negm"], [pk])
                for h in range(4):
                    act(decs[:, h, :], pst[:, h * 128:(h + 1) * 128], AF.Exp, [pk, "dgct"], ["ddecs"], bias=gct[:, h:h + 1], scale=-1.0)
                pkk, kkk = nps()
                for h in range(4):
                    mm(pkk[:, h * 128:(h + 1) * 128], kT[:, h, :], kT[:, h, :], True, True, [qk_], [kkk])
                for h in range(4):
                    stt("dve", P[:, h, :], pkk[:, h * 128:(h + 1) * 128], nbeta[:, h:h + 1], decs[:, h, :], ALU.mult, ALU.mult, [kkk, "dnbeta", "ddecs"], ["dP"])
                pqk, kqk = nps()
                for h in range(4):
                    mm(pqk[:, h * 128:(h + 1) * 128], qT[:, h, :], kT[:, h, :], True, True, [qk_], [kqk])
                tt("pool", decs[:], decs[:], identB4, ALU.add, ["ddecs", "identb"], ["ddecs"])
                tt("dve", qkm[:], v4(pqk), decs[:], ALU.mult, [kqk, "ddecs"], ["dqkm"])
                for (srcm, dst, sk_, dk_) in ((P, Q, "dP", "dQ"), (qkm, qkmT, "dqkm", "dqkmT")):
                    pst, pk = nps()
                    pb = vb4(pst)
                    for h in range(4):
                        k.op("pe", lambda e: e.transpose(pb[:, h, :], srcm[:, h, :], identb[:]), reads=[sk_, "identb"], writes=[pk])
                    copy(ev(), dst[:], pb, [pk], [dk_])
                def mm4(A, B, ra, rb):
                    ps_, pk_ = nps()
                    for h in range(4):
                        mm(ps_[:, h * 128:(h + 1) * 128], A[:, h, :], B[:, h, :], True, True, [ra, rb], [pk_])
                    return ps_, pk_
                tt("pool", Pk[:], P[:], mk["b16"][:], ALU.mult, ["dP", "dmk"], ["dPk"])
                tt("pool", Qk[:], Q[:], mk["b16"][:], ALU.mult, ["dQ", "dmk"], ["dQk"])
                tt("dve", Dm[:], Pk[:], identB4, ALU.add, ["dPk", "identb"], ["dDm"])
                tt("dve", R[:], Qk[:], identB4, ALU.add, ["dQk", "identb"], ["dR"])
                for lev in range(3):
                    pp, kp = mm4(Qk, Pk, "dQk", "dPk")
                    pq, kq = mm4(Pk, Qk, "dPk", "dQk")
                    copy("act", Pk[:], v4(pp), [kp], ["dPk"])
                    copy("act", Qk[:], v4(pq), [kq], ["dQk"])
                    pd, kd_ = mm4(Qk, Dm, "dQk", "dDm")
                    pu, ku = mm4(Pk, R, "dPk", "dR")
                    tt("dve", Dm[:], Dm[:], v4(pd), ALU.add, ["dDm", kd_], ["dDm"])
                    tt("dve", R[:], R[:], v4(pu), ALU.add, ["dR", ku], ["dR"])
                for li, ln_ in enumerate(("l1", "l2", "l3")):
                    last = li == 2
                    tt("pool", Pk[:], P[:], mk[ln_][:], ALU.mult, ["dP", "dmk"], ["dPk"])
                    px, kx = mm4(Pk, R, "dPk", "dR")
                    copy("act", Xu[:], v4(px), [kx], ["dXu"])
                    if not last:
                        tt("pool", Qk[:], Q[:], mk[ln_][:], ALU.mult, ["dQ", "dmk"], ["dQk"])
                        pxd, kxd = mm4(Qk, Dm, "dQk", "dDm")
                        copy("act", Xd[:], v4(pxd), [kxd], ["dXd"])
                    py, ky = mm4(Dm, Xu, "dDm", "dXu")
                    if not last:
                        pyd, kyd = mm4(R, Xd, "dR", "dXd")
                    tt("dve", R[:], R[:], v4(py), ALU.add, ["dR", ky], ["dR"])
                    if not last:
                        tt("dve", Dm[:], Dm[:], v4(pyd), ALU.add, ["dDm", kyd], ["dDm"])
                pst, pk = nps()
                for h in range(4):
                    mm(pst[:, h * 128:(h + 1) * 128], kbg[:, h, :], R[:, h, :], True, True, ["dkbg", "dR"], [pk])
                act(nwT[:], v4(pst), AF.Copy, [pk], ["dnwT"], scale=-1.0)
                pst, pk = nps()
                for h in range(4):
                    mm(pst[:, h * 128:(h + 1) * 128], R[:, h, :], vb[:, h, :], True, False, ["dR", "dvb"], [pk])
                    mm(pst[:, h * 128:(h + 1) * 128], nwT[:, h, :], Sb[:, h, :], False, True, ["dnwT", "dSb"], [pk])
                copy("act", vnew[:], v4(pst), [pk], ["dvnew"])
                pst, pk = nps()
                for h in range(4):
                    mm(pst[:, h * 128:(h + 1) * 128], Sb[:, h, :], qg[:, h, :], True, False, ["dSb", "dqg"], [pk])
                    mm(pst[:, h * 128:(h + 1) * 128], vnew[:, h, :], qkmT[:, h, :], False, True, ["dvnew", "dqkmT"], [pk])
                ob_ = ci % 2
                copy("act", oT[ob_][:], v4(pst), [pk], [f"doT{ob_}"])
                k.dma("act", X["OA"][d].rearrange("(h p) t -> p h t", p=128)[:, :, ta:ta + 128], oT[ob_][:], reads=[f"doT{ob_}"], writes=["OA"])
                pst, pk = nps()
                for h in range(4):
                    mm(pst[:, h * 128:(h + 1) * 128], kd[:, h, :], vnew[:, h, :], True, True, ["dkd", "dvnew"], [pk])
                tt("dve", S[:], S[:], bc(e12[:, 8:12]), ALU.mult, ["dS", "de12"], ["dS"])
                tt("dve", S[:], S[:], v4(pst), ALU.add, ["dS", pk], ["dS"])
                copy("act", Sb[:], S[:], ["dS"], ["dSb"])
                yield
            if cond == 1:
                k.dma("sp", O["nsd"][si - 1, l, d].rearrange("h k v -> k h v"), S[:], reads=["dS"], writes=["nsd"])
        yield


def phase_hgrn(X, l, d):
    nc = X["nc"]; k = X["k"]; I = X["I"]; O = X["O"]; C = X["C"]
    mm = X["mm"]; act = X["act"]; tt = X["tt"]; ts = X["ts"]; stt = X["stt"]; nps = X["nps"]; ev = X["ev"]; copy = X["copy"]
    identb = X["identb"]
    ph = X["ph"]
    if True:
        def sbp(name, shape, dt):
            return ph.enter_context(nc.sbuf_tensor(_un(name), list(shape), dt))
        S = sbp("hS", [128, 4, 128], F32); Sb = sbp("hSb", [128, 4, 128], BF16)
        hm = sbp("hhm", [128, 4, 128], BF16)
        copy("dve", hm[:], C(f"hm{d}").unsqueeze(1).to_broadcast([128, 4, 128]), ["cst"], ["hhm"])
        qb = [sbp(f"hqb{i}", [128, 4, 128], BF16) for i in range(2)]
        ib = [sbp(f"hib{i}", [128, 4, 128], BF16) for i in range(2)]
        ff = [sbp(f"hff{i}", [128, 4, 128], F32) for i in range(2)]
        lf = sbp("hlf", [128, 512], F32); kf = sbp("hkf", [128, 512], F32); bb = sbp("hbb", [128, 512], F32); tmp = sbp("htmp", [128, 512], F32)
        bl = sbp("hbl", [128, 16], F32); ebl = sbp("hebl", [128, 16], F32)
        eb = sbp("heb", [128, 512], F32); enb = sbp("henb", [128, 512], F32); ekd = sbp("hekd", [128, 512], F32)
        qe = sbp("hqe", [128, 4, 128], BF16); ke = sbp("hke", [128, 4, 128], BF16); kdT = sbp("hkdT", [128, 4, 128], BF16)
        kdt = sbp("hkdt", [128, 4, 128], BF16); vtok = sbp("hvtok", [128, 4, 128], BF16); attm = sbp("hattm", [128, 4, 128], BF16); kd3 = sbp("hkd3", [128, 4, 128], BF16)
        oT = [sbp(f"hoT{i}", [128, 4, 128], F32) for i in range(2)]

        def v4(ps):
            return ps[:].rearrange("p (h t) -> p h t", h=4)

        def vb4(ps):
            return ps[:].bitcast(BF16)[:, 0:512].rearrange("p (h t) -> p h t", h=4)

        def f3(t):
            return t[:].rearrange("p (g s) -> p g s", s=32)
        ci = 0
        for si, (s0, ln, cond) in enumerate(SEQS):
            if cond == 0:
                k.dma("sp", S[:], I["sh0"][l, d].rearrange("h k v -> k h v"), writes=["hS"])
            else:
                k.op("dve", lambda e: e.memset(S[:], 0.0), writes=["hS"])
            copy("dve", Sb[:], S[:], ["hS"], ["hSb"])
            chunks = list(range(ln // 128))
            if d == 1:
                chunks = chunks[::-1]
            for cj in chunks:
                ta = s0 + cj * 128
                b = ci % 2; ci += 1
                k.dma("sp", qb[b][:], X["QB"].rearrange("(h p) t -> p h t", p=128)[:, :, ta:ta + 128], reads=["QB"], writes=[f"hqb{b}"])
                k.dma("sp", ib[b][:], X["IB"].rearrange("(h p) t -> p h t", p=128)[:, :, ta:ta + 128], reads=["IB"], writes=[f"hib{b}"])
                k.dma("act", ff[b][:], X["FF"][d * 512:(d + 1) * 512, :].rearrange("(h p) t -> p h t", p=128)[:, :, ta:ta + 128], reads=["FF"], writes=[f"hff{b}"])
                fv = ff[b][:].rearrange("p h t -> p (h t)")
                act(lf[:], fv, AF.Ln, [f"hff{b}"], ["hlf"])
                ts("pool", kf[:], fv, -1.0, 1.0, ALU.mult, ALU.add, [f"hff{b}"], ["hkf"])
                k.op("dve", lambda e: e.tensor_tensor_scan(out=bb[:], data0=C("seg"), data1=lf[:], initial=0.0, op0=ALU.mult, op1=ALU.add),
                     reads=["cst", "hlf"], writes=["hbb"])
                copy("dve", bl[:], f3(bb)[:, :, 31], ["hbb"], ["hbl"])
                if d == 1:
                    tt("dve", tmp[:], lf[:], bb[:], ALU.subtract, ["hlf", "hbb"], ["htmp"])
                    tt("dve", f3(bb), f3(tmp), bl[:].unsqueeze(2).to_broadcast([128, 16, 32]), ALU.add, ["htmp", "hbl"], ["hbb"])
                act(ebl[:], bl[:], AF.Exp, ["hbl"], ["hebl"])
                act(eb[:], bb[:], AF.Exp, ["hbb"], ["heb"])
                act(enb[:], bb[:], AF.Exp, ["hbb"], ["henb"], scale=-1.0)
                tt("dve", f3(tmp), bl[:].unsqueeze(2).to_broadcast([128, 16, 32]), f3(bb), ALU.subtract, ["hbb", "hbl", "htmp"], ["htmp"])
                act(ekd[:], tmp[:], AF.Exp, ["htmp"], ["hekd"])
                tt("dve", qe[:].rearrange("p h t -> p (h t)"), qb[b][:].rearrange("p h t -> p (h t)"), eb[:], ALU.mult, [f"hqb{b}", "heb"], ["hqe"])
                tt("pool", ke[:].rearrange("p h t -> p (h t)"), kf[:], enb[:], ALU.mult, ["hkf", "henb"], ["hke"])
                tt("pool", kdT[:].rearrange("p h t -> p (h t)"), kf[:], ekd[:], ALU.mult, ["hkf", "hekd"], ["hkdT"])
                for (srcT, dst, sk_, dk_) in ((kdT, kdt, "hkdT", "hkdt"), (ib[b], vtok, f"hib{b}", "hvtok")):
                    pst, pk = nps()
                    pb = vb4(pst)
                    for h in range(4):
                        k.op("pe", lambda e: e.transpose(pb[:, h, :], srcT[:, h, :], identb[:]), reads=[sk_, "identb"], writes=[pk])
                    copy(ev(), dst[:], pb, [pk], [dk_])
                ts("pool", kd3[64:128], kdt[64:128], C("tri1")[64:128, 96:97], None, ALU.mult, None, ["hkdt", "cst"], ["hkd3"])
                pst, pk = nps()
                for h in range(4):
                    mm(pst[:, h * 128:(h + 1) * 128], ke[:, h, :], qe[:, h, :], True, True, ["hke", "hqe"], [pk])
                tt("dve", attm[:], v4(pst), hm[:], ALU.mult, [pk, "hhm"], ["hattm"])
                po, ko = nps()
                blks = [0, 1, 2, 3] if d == 0 else [3, 2, 1, 0]
                for bi in blks:
                    cs = slice(bi * 32, (bi + 1) * 32)
                    for h in range(4):
                        osl = po[:, h * 128 + bi * 32:h * 128 + (bi + 1) * 32]
                        mm(osl, Sb[:, h, :], qe[:, h, cs], True, False, ["hSb", "hqe"], [ko])
                        mm(osl, vtok[:, h, :], attm[:, h, cs], False, True, ["hvtok", "hattm"], [ko])
                    pst, pk = nps()
                    for h in range(4):
                        if bi < 3:
                            mm(pst[:, h * 128:(h + 1) * 128], kdt[cs, h, :], vtok[cs, h, :], True, True, ["hkdt", "hvtok"], [pk])
                        else:
                            mm(pst[:, h * 128:(h + 1) * 128], kd3[64:128, h, :], vtok[64:128, h, :], True, True, ["hkd3", "hvtok"], [pk])
                    dec = ebl[:].rearrange("p (h g) -> p h g", g=4)[:, :, bi:bi + 1].to_broadcast([128, 4, 128])
                    tt("dve", S[:], S[:], dec, ALU.mult, ["hS", "hebl"], ["hS"])
                    tt("dve", S[:], S[:], v4(pst), ALU.add, ["hS", pk], ["hS"])
                    copy("act", Sb[:], S[:], ["hS"], ["hSb"])
                ob_ = ci % 2
                copy("act", oT[ob_][:], v4(po), [ko], [f"hoT{ob_}"])
                k.dma("act", X["OB"][d].rearrange("(h p) t -> p h t", p=128)[:, :, ta:ta + 128], oT[ob_][:], reads=[f"hoT{ob_}"], writes=["OB"])
                yield
            if cond == 1:
                k.dma("sp", O["nsh"][si - 1, l, d].rearrange("h k v -> k h v"), S[:], reads=["hS"], writes=["nsh"])
        yield


def phase_attn(X, l):
    nc = X["nc"]; k = X["k"]; I = X["I"]; O = X["O"]; C = X["C"]
    mm = X["mm"]; act = X["act"]; tt = X["tt"]; ts = X["ts"]; stt = X["stt"]; nps = X["nps"]; ev = X["ev"]; copy = X["copy"]
    with ExitStack() as ph:
        def sbp(name, shape, dt):
            return ph.enter_context(nc.sbuf_tensor(_un(name), list(shape), dt))
        ckt = sbp("ackt", [128, 2, 128], F32); ckT = sbp("ackT", [64, 2, 256], BF16); cvb = sbp("acvb", [128, 2, 128], BF16)
        k.dma("sp", ckt[:], I["ck"][l].rearrange("(b p) f -> p b f", p=128), writes=["ackt"])
        k.dma("pool", cvb[:], I["cv"][l].rearrange("(b p) f -> p b f", p=128), writes=["acvb"])
        for g in range(2):
            pst, pk = nps()
            for bk in range(2):
                k.op("pe", lambda e: e.transpose(pst[:64, bk * 128:(bk + 1) * 128], ckt[:, bk, g * 64:(g + 1) * 64], C("ident")), reads=["ackt", "cst"], writes=[pk])
            copy("dve", ckT[:, g, :], pst[:64, 0:256], [pk], ["ackT"])
        esk = sbp("aesk", [64, 8], F32)
        k.dma("sp", esk[:], I["sink"][l:l + 1, :].to_broadcast([64, 8]), writes=["aesk"])
        act(esk[:], esk[:], AF.Exp, ["aesk"], ["aesk"])
        wm = sbp("awm", [128, 2, 128], BF16)
        copy("dve", wm[:, 0, :], C("wprev"), ["cst"], ["awm"])
        copy("dve", wm[:, 1, :], C("wnext"), ["cst"], ["awm"])
        qT = [sbp(f"aq{i}", [64, 8, 128], BF16) for i in range(2)]
        kT3 = [sbp(f"ak{i}", [64, 2, 384], BF16) for i in range(2)]
        v3 = [sbp(f"av{i}", [128, 3, 128], BF16) for i in range(2)]
        pT = [sbp(f"ap{i}", [128, 512], BF16) for i in range(3)]
        den = sbp("aden", [64, 4, 128], F32); oc = [sbp(f"aoc{i}", [64, 8, 128], BF16) for i in range(2)]
        ci = 0; pi = 0
        for si, (s0, ln, cond) in enumerate(SEQS):
            nb = ln // 128
            for qbk in range(nb):
                ta = s0 + qbk * 128
                b = ci % 2; ci += 1
                k.dma("sp", qT[b][:], X["QCN"][:, 0:8, ta:ta + 128], reads=["QCN"], writes=[f"aq{b}"])
                if cond == 0:
                    lo = max(qbk - 1, 0); hi = min(qbk + 1, nb - 1)
                else:
                    lo, hi = 0, nb - 1
                nk = hi - lo + 1
                k.dma("sp", kT3[b][:, :, 0:nk * 128], X["QCN"][:, 8:10, s0 + lo * 128:s0 + (hi + 1) * 128], reads=["QCN"], writes=[f"ak{b}"])
                k.dma("act", v3[b][:, 0:nk, :], X["VCN"][s0 + lo * 128:s0 + (hi + 1) * 128, :].rearrange("(b p) f -> p b f", p=128), reads=["VCN"], writes=[f"av{b}"])
                for g in range(2):
                    kbl = []
                    for j in range(nk):
                        kb_ = lo + j
                        m = None
                        if cond == 0 and kb_ == qbk - 1:
                            m = 0
                        if cond == 0 and kb_ == qbk + 1:
                            m = 1
                        kbl.append((kT3[b][:, g, j * 128:(j + 1) * 128], v3[b][:, j, g * 64:(g + 1) * 64], m, [f"ak{b}"], [f"av{b}"]))
                    if cond == 0:
                        for bk in range(2):
                            kbl.append((ckT[:, g, bk * 128:(bk + 1) * 128], cvb[:, bk, g * 64:(g + 1) * 64], None, ["ackT"], ["acvb"]))
                    po, ko = nps(); pr, kr = nps()
                    for j, (kap, vap, m, kk_, vk_) in enumerate(kbl):
                        pst, pk = nps()
                        mm(pst[:], kap, qT[b][:, g * 4:(g + 1) * 4, :].rearrange("p h t -> p (h t)"), True, True, kk_ + [f"aq{b}"], [pk])
                        pb = pi % 3; pi += 1
                        act(pT[pb][:], pst[:], AF.Exp, [pk], [f"ap{pb}"], scale=0.125)
                        if m is not None:
                            tt("pool", pT[pb][:].rearrange("p (h t) -> p h t", h=4), pT[pb][:].rearrange("p (h t) -> p h t", h=4),
                               wm[:, m, :].unsqueeze(1).to_broadcast([128, 4, 128]), ALU.mult, [f"ap{pb}", "awm"], [f"ap{pb}"])
                        mm(po[:64, :], vap, pT[pb][:], j == 0, j == len(kbl) - 1, vk_ + [f"ap{pb}"], [ko])
                        mm(pr[:64, :], X["onesb"][:, 0:64], pT[pb][:], j == 0, j == len(kbl) - 1, ["onesb", f"ap{pb}"], [kr])
                    tt("dve", den[:], pr[:64, :].rearrange("p (h t) -> p h t", h=4), esk[:, g * 4:(g + 1) * 4].unsqueeze(2).to_broadcast([64, 4, 128]), ALU.add, [kr, "aesk"], ["aden"])
                    k.op("dve", lambda e: e.reciprocal(out=den[:], in_=den[:]), reads=["aden"], writes=["aden"])
                    tt("dve", oc[b][:, g * 4:(g + 1) * 4, :], po[:64, :].rearrange("p (h t) -> p h t", h=4), den[:], ALU.mult, [ko, "aden"], [f"aoc{b}"])
                k.dma("act", X["OC"][:, :, ta:ta + 128], oc[b][:], reads=[f"aoc{b}"], writes=["OC"])
        k.barrier()


MTILES = [(t0 + h * 256, 256, c, s) for (t0, n, c, s) in TILES for h in range(n // 256)]


def phase_merge(X, l):
    nc = X["nc"]; k = X["k"]; I = X["I"]; O = X["O"]; C = X["C"]
    mm = X["mm"]; act = X["act"]; tt = X["tt"]; ts = X["ts"]; stt = X["stt"]; nps = X["nps"]; ev = X["ev"]; copy = X["copy"]
    n = 256
    with ExitStack() as ph:
        def sbp(name, shape, dt):
            return ph.enter_context(nc.sbuf_tensor(_un(name), list(shape), dt))
        Wg = sbp("mWg", [128, 8, 3072], BF16); Wb = sbp("mWb", [128, 8, 1024], BF16); WbC = sbp("mWbC", [64, 8, 1024], BF16); Wo = sbp("mWo", [128, 8, 1024], BF16)
        for kc in range(8):
            k.dma("pool", Wg[:, kc, :], I["w_in"][l][kc * 128:(kc + 1) * 128, C_MG:DIN], writes=["mWg"])
            k.dma("pool", Wo[:, kc, :], I["w_out"][l][kc * 128:(kc + 1) * 128, :], writes=["mWo"])
            k.dma("pool", Wb[:, kc, :], I["w_branch"][l, kc // 4][(kc % 4) * 128:(kc % 4 + 1) * 128, :], writes=["mWb"])
            k.dma("pool", WbC[:, kc, :], I["w_branch"][l, 2][kc * 64:(kc + 1) * 64, :], writes=["mWbC"])
        nab = sbp("mnab", [128, 2], F32)
        k.dma("sp", nab[:, 0:1], I["norm_a"][l].rearrange("(p o) -> p o", o=1), writes=["mnab"], allow_slow_non_contiguous=True)
        k.dma("sp", nab[:, 1:2], I["norm_b"][l].rearrange("(p o) -> p o", o=1), writes=["mnab"], allow_slow_non_contiguous=True)
        xT = sbp("mxT", [128, 8, n], F32); xr = sbp("mxr", [128, 8, n], F32); hT = sbp("mhT", [128, 8, n], BF16); rstd = sbp("mrstd", [128, n], F32)
        X["mrstd"] = rstd
        of = sbp("mof", [128, 4, n], F32); ob = sbp("mob", [128, 4, n], F32); gt = sbp("mgt", [128, 4, n], BF16); sq = sbp("msq", [128, 4, n], BF16)
        rn = sbp("mrn", [128, 4, n], F32)
        oab = sbp("moab", [128, 8, n], BF16); oc = sbp("moc", [64, 8, n], BF16); mg = sbp("mmg", [128, 8, n], BF16)
        sgs = [[sbp(f"msg{q}_{i}", [128, n], F32) for i in range(3)] for q in range(3)]
        for (t0, _, cond, si) in MTILES:
            k.dma("sp", xT[:], X["XTv"][:, :, 1 + t0:1 + t0 + n], reads=[f"XT{t0}"], writes=["mxT"])
            k.dma("act", xr[:], X["XTv"][:, :, 1 + t0:1 + t0 + n], reads=[f"XT{t0}"], writes=["mxr"] + [f"mxr{m}" for m in range(8)])
            _norm_h(X, l, 1, xT, hT, n, cond, "m")
            for r, (src, gsrc, gk) in enumerate(((X["OA"], X["GA"], "GA"), (X["OB"], X["GB"], "GB"))):
                k.dma("sp", of[:], src[0].rearrange("(h p) t -> p h t", p=128)[:, :, t0:t0 + n], reads=["OA", "OB"], writes=["mof"])
                k.dma("act", ob[:], src[1].rearrange("(h p) t -> p h t", p=128)[:, :, t0:t0 + n], reads=["OA", "OB"], writes=["mob"])
                k.dma("sp", gt[:], gsrc.rearrange("(h p) t -> p h t", p=128)[:, :, t0:t0 + n], reads=[gk], writes=["mgt"])
                tt("dve", of[:], of[:], ob[:], ALU.add, ["mof", "mob"], ["mof"])
                act(sq[:], of[:], AF.Square, ["mof"], ["msq"])
                for hp in range(2):
                    pst, pk = nps()
                    for hh in range(2):
                        mm(pst[:, hh * n:(hh + 1) * n], X["onesb"][:], sq[:, hp * 2 + hh, :], True, True, ["onesb", "msq"], [pk])
                    act(rn[:, hp * 2:hp * 2 + 2, :], pst[:].rearrange("p (h t) -> p h t", h=2), AF.Sqrt, [pk], ["mrn"], bias=EPS, scale=1.0 / 128)
                k.op("dve", lambda e: e.reciprocal(out=rn[:], in_=rn[:]), reads=["mrn"], writes=["mrn"])
                tt("dve", of[:], of[:], rn[:], ALU.mult, ["mof", "mrn"], ["mof"])
                stt("dve", oab[:, r * 4:(r + 1) * 4, :], of[:], nab[:, r:r + 1], gt[:], ALU.mult, ALU.mult, ["mof", "mnab", "mgt"], ["moab"])
            k.dma("sp", oc[:], X["OC"][:, :, t0:t0 + n], reads=["OC"], writes=["moc"])
            for m in range(8):
                ms = slice(m * 128, (m + 1) * 128)
                sg = sgs[m % 3]; sk = [f"msg{m % 3}_{i}" for i in range(3)]
                brs = []
                for r in range(3):
                    pb_, kb_ = nps()
                    if r < 2:
                        for kc in range(4):
                            mm(pb_[:, :n], Wb[:, r * 4 + kc, ms], oab[:, r * 4 + kc, :], kc == 0, kc == 3, ["mWb", "moab"], [kb_])
                    else:
                        for hh in range(8):
                            mm(pb_[:, :n], WbC[:, hh, ms], oc[:, hh, :], hh == 0, hh == 7, ["mWbC", "moc"], [kb_])
                    pg_, kg_ = nps()
                    for kc in range(8):
                        mm(pg_[:, :n], Wg[:, kc, r * 1024 + m * 128:r * 1024 + (m + 1) * 128], hT[:, kc, :], kc == 0, kc == 7, ["mWg", "mhT"], [kg_])
                    act(sg[r][:], pg_[:, :n], AF.Sigmoid, [kg_], [sk[r]])
                    tt("dve", sg[r][:], sg[r][:], pb_[:, :n], ALU.mult, [sk[r], kb_], [sk[r]])
                tt("pool", sg[0][:], sg[0][:], sg[1][:], ALU.add, [sk[0], sk[1]], [sk[0]])
                tt("pool", mg[:, m, :], sg[0][:], sg[2][:], ALU.add, [sk[0], sk[2]], [f"mmg{m}"])
            for m in range(8):
                pst, pk = nps()
                for kc in range(8):
                    mm(pst[:, :n], Wo[:, kc, m * 128:(m + 1) * 128], mg[:, kc, :], kc == 0, kc == 7, ["mWo", f"mmg{kc}"], [pk])
                stt("dve", xr[:, m, :], pst[:, :n], X["modT"][:, l, cond, 16 + m:17 + m], xr[:, m, :], ALU.mult, ALU.add, [pk, "modT", "mxr"], [f"mxr{m}"])
            k.dma("act", X["XTv"][:, :, 1 + t0:1 + t0 + n], xr[:], reads=["mxr"] + [f"mxr{m}" for m in range(8)], writes=[f"XT{t0}"])
        k.barrier()


def phase_ffn(X, l):
    nc = X["nc"]; k = X["k"]; I = X["I"]; O = X["O"]; C = X["C"]
    mm = X["mm"]; act = X["act"]; tt = X["tt"]; ts = X["ts"]; stt = X["stt"]; nps = X["nps"]; ev = X["ev"]; copy = X["copy"]
    n = 256
    XT2 = X["FFfull"]
    XT2v = XT2.rearrange("(c p) t -> p c t", p=128)
    with ExitStack() as ph:
        def sbp(name, shape, dt):
            return ph.enter_context(nc.sbuf_tensor(_un(name), list(shape), dt))
        Wu = sbp("fWu", [128, 8, 2 * DFF], BF16); Wd = sbp("fWd", [128, 22, 1024], BF16)
        for kc in range(8):
            k.dma("pool", Wu[:, kc, :], I["w_up"][l][kc * 128:(kc + 1) * 128, :], writes=["fWu"])
        for j in range(22):
            k.dma("pool", Wd[:, j, :], I["w_down"][l][j * 128:(j + 1) * 128, :], writes=["fWd"])
        cwf = sbp("fcw", [128, 44, 3], F32)
        for j in range(3):
            k.dma("sp", cwf[:, :, j], I["conv_ffn"][l, j].rearrange("(c p) -> p c", p=128), writes=["fcw"], allow_slow_non_contiguous=True)
        xT = sbp("fxT", [128, 8, n + 2], F32); xr = sbp("fxr", [128, 8, n], F32); hT = sbp("fhT", [128, 8, n + 2], BF16); rstd = sbp("frstd", [128, n + 2], F32)
        X["frstd"] = rstd
        accs = [[sbp(f"facc{i}_{w}", [128, n], F32) for w in range(2)] for i in range(4)]
        pr = sbp("fpr", [128, 22, n], BF16)
        for (t0, _, cond, si) in MTILES:
            s0, ln, _c = SEQS[si]
            has_l = t0 > s0; has_r = (t0 + n) < (s0 + ln)
            k.dma("sp", xT[:, :, 0:n], X["XTv"][:, :, 1 + t0:1 + t0 + n], reads=["XT"], writes=["fxT"])
            k.dma("act", xr[:], X["XTv"][:, :, 1 + t0:1 + t0 + n], reads=["XT"], writes=["fxr"] + [f"fxr{m}" for m in range(8)])
            k.dma("sp", xT[:, :, n:n + 1], X["XTv"][:, :, t0:t0 + 1], reads=["XT"], writes=["fxT"], allow_slow_non_contiguous=True)
            k.dma("sp", xT[:, :, n + 1:n + 2], X["XTv"][:, :, 1 + t0 + n:2 + t0 + n], reads=["XT"], writes=["fxT"], allow_slow_non_contiguous=True)
            _norm_h(X, l, 2, xT, hT, n + 2, cond, "f")
            for j in range(22):
                for w_, c0 in enumerate((j * 128, DFF + j * 128)):
                    cc = c0 // 128
                    pst, pk = nps()
                    for kc in range(8):
                        mm(pst[:, :n + 2], Wu[:, kc, c0:c0 + 128], hT[:, kc, :], kc == 0, kc == 7, ["fWu", "fhT"], [pk])
                    a = accs[j % 4][w_]; ak = f"facc{j % 4}_{w_}"
                    act(a[:], pst[:, :n], AF.Copy, [pk, "fcw"], [ak], scale=cwf[:, cc, 1:2])
                    stt("dve", a[:, 1:n], pst[:, 0:n - 1], cwf[:, cc, 0:1], a[:, 1:n], ALU.mult, ALU.add, [pk, "fcw", ak], [ak])
                    stt("dve", a[:, 0:n - 1], pst[:, 1:n], cwf[:, cc, 2:3], a[:, 0:n - 1], ALU.mult, ALU.add, [pk, "fcw", ak], [ak])
                    if has_l:
                        stt("dve", a[:, 0:1], pst[:, n:n + 1], cwf[:, cc, 0:1], a[:, 0:1], ALU.mult, ALU.add, [pk, "fcw", ak], [ak])
                    if has_r:
                        stt("dve", a[:, n - 1:n], pst[:, n + 1:n + 2], cwf[:, cc, 2:3], a[:, n - 1:n], ALU.mult, ALU.add, [pk, "fcw", ak], [ak])
                a0 = accs[j % 4][0]; a1 = accs[j % 4][1]
                act(a0[:], a0[:], AF.Silu, [f"facc{j % 4}_0"], [f"facc{j % 4}_0"])
                tt("pool", pr[:, j, :], a0[:], a1[:], ALU.mult, [f"facc{j % 4}_0", f"facc{j % 4}_1"], [f"fpr{j}"])
            for m in range(8):
                pst, pk = nps()
                for j in range(22):
                    mm(pst[:, :n], Wd[:, j, m * 128:(m + 1) * 128], pr[:, j, :], j == 0, j == 21, ["fWd", f"fpr{j}"], [pk])
                stt("dve", xr[:, m, :], pst[:, :n], X["modT"][:, l, cond, 40 + m:41 + m], xr[:, m, :], ALU.mult, ALU.add, [pk, "modT", "fxr"], [f"fxr{m}"])
            k.dma("act", XT2v[:, :, 1 + t0:1 + t0 + n], xr[:], reads=["fxr"] + [f"fxr{m}" for m in range(8)], writes=["XT2"])
        k.barrier()
        k.dma("sp", X["XT"][:, 1:1 + NT], XT2[:, 1:1 + NT], reads=["XT2"], writes=["XT"])
        k.barrier()


_CACHE = {}


def kernel(x_prompt, x_sample, state_delta, state_hgrn, cache_k, cache_v, c, c_ctx,
           ada_w, ada_b, norm1_w, w_in, conv_a, a_log, dt_bias, norm_a, lb_logits, norm_b,
           q_norm, k_norm, sink, w_branch, w_out, norm2_w, w_up, conv_ffn, w_down, _stage=99, _debug=()):
    f = lambda a: np.ascontiguousarray(np.asarray(a, dtype=np.float32))
    if "nc" not in _CACHE or _CACHE.get("stage") != _stage:
        _CACHE["nc"] = build_program(_stage, _debug)
        _CACHE["stage"] = _stage
    nc = _CACHE["nc"]
    carr, _ = host_consts()
    rcos, rsin = rope_tables()
    shared = dict(ada_w=f(ada_w), ada_b=f(ada_b), norm1_w=f(norm1_w), w_in=f(w_in), conv_a=f(conv_a),
                  a_log=f(a_log).reshape(DEPTH, 8), dt_bias=f(dt_bias).reshape(DEPTH, 8), norm_a=f(norm_a), lb_logits=f(lb_logits),
                  norm_b=f(norm_b), q_norm=f(q_norm), k_norm=f(k_norm), sink=f(sink), w_branch=f(w_branch), w_out=f(w_out),
                  norm2_w=f(norm2_w), w_up=f(w_up), conv_ffn=f(conv_ffn), w_down=f(w_down), consts=carr, rcos=rcos, rsin=rsin)
    xp = f(x_prompt); xs = f(x_sample); sd = f(state_delta); sh = f(state_hgrn); ck = f(cache_k); cv = f(cache_v)
    cc = f(c); cx = f(c_ctx)
    in_maps = []
    for core in range(8):
        b = core % 4
        m = dict(shared)
        m["x_lat"] = xs[b]
        m["x_ctx"] = xp[4 * core:4 * core + 4].reshape(NCTX, D)
        m["cond2"] = np.stack([cc[b], cx], axis=0)
        m["sd0"] = sd[b]; m["sh0"] = sh[b]
        m["ck"] = ck[b].reshape(DEPTH, 256, 128); m["cv"] = cv[b].reshape(DEPTH, 256, 128)
        in_maps.append(m)
    res = run_bass_kernel_spmd(nc, in_maps, core_ids=list(range(8)))
    R = res.results
    _CACHE["last"] = R
    y_prompt = np.concatenate([R[i]["y_ctx"].reshape(4, 256, D) for i in range(8)], axis=0)
    y_sample = np.stack([R[i]["y_lat"] for i in range(4)], axis=0)
    nsd = np.concatenate([R[i]["nsd"] for i in range(8)], axis=0)
    nsh = np.concatenate([R[i]["nsh"] for i in range(8)], axis=0)
    nck = np.concatenate([R[i]["nck"].reshape(4, DEPTH, 256, 2, 64) for i in range(8)], axis=0)
    ncv = np.concatenate([R[i]["ncv"].reshape(4, DEPTH, 256, 2, 64) for i in range(8)], axis=0)
    return (y_prompt.astype(np.float32), y_sample.astype(np.float32), nsd.astype(np.float32), nsh.astype(np.float32),
            nck.astype(np.float32), ncv.astype(np.float32))
```
